# Optimizing a Trainium2 kernel written in Bass

```python
import math
import jax, jax.numpy as jnp
from jax import lax
import numpy as np

D_MODEL = 4096
BATCH = 4
SEQ = 4096
DEPTH = 2

CTX_LEN = 256
GRID_W = 64
EPS = 1e-6
ROPE_BASE = 10000.0
S5_WIDTH = D_MODEL // 4
S5_GROUP = 16
S5_GROUPS = S5_WIDTH // S5_GROUP
S5_STATE = 64
S5_DT_MIN = 0.001
S5_DT_MAX = 0.1
MLSTM_WIDTH = D_MODEL - S5_WIDTH
MLSTM_DV = 512
MLSTM_DQK = MLSTM_DV // 2
MLSTM_HEADS = MLSTM_WIDTH // MLSTM_DV
MLSTM_CHUNK = 64
MLSTM_QK_W = MLSTM_HEADS * MLSTM_DQK
MLSTM_GATE_W = 2 * 2 * MLSTM_HEADS
AB_SPLITS = (MLSTM_QK_W, 2 * MLSTM_QK_W, 2 * MLSTM_QK_W + MLSTM_WIDTH,
             2 * MLSTM_QK_W + 2 * MLSTM_WIDTH, 2 * MLSTM_QK_W + 2 * MLSTM_WIDTH + MLSTM_GATE_W)
AB_IN_WIDTH = 2 * MLSTM_QK_W + 2 * MLSTM_WIDTH + MLSTM_GATE_W + S5_WIDTH
NA_HEADS = 32
NA_HEAD_DIM = D_MODEL // NA_HEADS
WIN_H = 8
WIN_W = 16
D_FF = 128 * ((8 * D_MODEL // 3 + 127) // 128)
CONV_W = 3

kernel_name = "hybrid_mlstm_s5_natten_dit"


def rms_norm(x, g):
    xf = x.astype(jnp.float32)
    y = xf * lax.rsqrt(jnp.mean(xf * xf, axis=-1, keepdims=True) + EPS)
    return (y * g.astype(jnp.float32)).astype(x.dtype)


def modulate(x, g, shift, scale):
    return rms_norm(x, g) * (1 + scale) + shift


def ada_chunks(cond, w, b):
    m = jax.nn.silu(cond) @ w + b
    return [t[:, None, :] for t in jnp.split(m, 6, axis=-1)]


def split_heads(t, n_heads):
    b, n, w = t.shape
    return t.reshape(b, n, n_heads, w // n_heads).transpose(0, 2, 1, 3)


def merge_heads(t):
    b, h, n, d = t.shape
    return t.transpose(0, 2, 1, 3).reshape(b, n, h * d)


def axial_rope(x):
    n, d = x.shape[2], x.shape[3]
    t = jnp.arange(n)
    half = d // 2
    inv = ROPE_BASE ** (-jnp.arange(0, half, 2, dtype=jnp.float32) / half)

    def rot(xa, pos):
        ang = pos.astype(jnp.float32)[:, None] * inv
        cos, sin = jnp.cos(ang), jnp.sin(ang)
        x1, x2 = jnp.split(xa.astype(jnp.float32), 2, axis=-1)
        return jnp.concatenate([x1 * cos - x2 * sin, x1 * sin + x2 * cos], axis=-1)

    out = jnp.concatenate([rot(x[..., :half], t // GRID_W), rot(x[..., half:], t % GRID_W)], axis=-1)
    return out.astype(x.dtype)


def mlstm_chunkwise(q, k, v, log_i, log_f, state):
    b_, h_, n_, _ = q.shape
    nc = n_ // MLSTM_CHUNK

    def chunks(t):
        t = t.reshape((b_, h_, nc, MLSTM_CHUNK) + t.shape[3:])
        return jnp.moveaxis(t, 2, 0)

    lower = jnp.tril(jnp.ones((MLSTM_CHUNK, MLSTM_CHUNK), dtype=bool))

    def step(carry, xs):
        c_st, n_st, m_st = carry
        qc, kc, vc, li, lf = xs
        cum_f = jnp.cumsum(lf, axis=-1)
        d = jnp.where(lower, cum_f[..., :, None] - cum_f[..., None, :] + li[..., None, :], -jnp.inf)
        carried = cum_f + m_st[..., None]
        m_loc = jnp.maximum(carried, jnp.max(d, axis=-1))
        w = jnp.exp(d - m_loc[..., None])
        w_state = jnp.exp(carried - m_loc)
        sc = jnp.einsum('bhld,bhsd->bhls', qc, kc) * w
        num = jnp.einsum('bhls,bhsv->bhlv', sc, vc) + w_state[..., None] * jnp.einsum('bhvd,bhld->bhlv', c_st, qc)
        den = jnp.sum(sc, axis=-1) + w_state * jnp.einsum('bhd,bhld->bhl', n_st, qc)
        h = num / jnp.maximum(jnp.abs(den), jnp.exp(-m_loc))[..., None]
        total_f = cum_f[..., -1]
        src = total_f[..., None] - cum_f + li
        m_new = jnp.maximum(total_f + m_st, jnp.max(src, axis=-1))
        decay = jnp.exp(total_f + m_st - m_new)
        w_src = jnp.exp(src - m_new[..., None])
        c_st = decay[..., None, None] * c_st + jnp.einsum('bhl,bhlv,bhld->bhvd', w_src, vc, kc)
        n_st = decay[..., None] * n_st + jnp.einsum('bhl,bhld->bhd', w_src, kc)
        return (c_st, n_st, m_new), h

    state, hs = lax.scan(step, state, tuple(chunks(t) for t in (q, k, v, log_i, log_f)))
    return jnp.moveaxis(hs, 0, 2).reshape(b_, h_, n_, v.shape[-1]), state


def mlstm_bidirectional(q_c, k_c, v_c, g_c, q_l, k_l, v_l, g_l, gate_b):
    f32 = jnp.float32
    q_c, k_c, v_c, q_l, k_l, v_l = (t.astype(f32) for t in (q_c, k_c, v_c, q_l, k_l, v_l))
    b_ = q_l.shape[0]

    def gates(g, direction):
        g = g.astype(f32).reshape(g.shape[0], g.shape[1], 2, 2, MLSTM_HEADS) + gate_b.astype(f32)
        li = jnp.transpose(g[:, :, direction, 0, :], (0, 2, 1))
        lf = jax.nn.log_sigmoid(jnp.transpose(g[:, :, direction, 1, :], (0, 2, 1)))
        return li, lf

    flip = lambda t: jnp.flip(t, axis=2)
    state0 = (jnp.zeros((b_, MLSTM_HEADS, MLSTM_DV, MLSTM_DQK), f32),
              jnp.zeros((b_, MLSTM_HEADS, MLSTM_DQK), f32),
              jnp.zeros((b_, MLSTM_HEADS), f32))
    li_c, lf_c = gates(g_c, 0)
    li_l, lf_l = gates(g_l, 0)
    hc_f, st = mlstm_chunkwise(q_c, k_c, v_c, li_c, lf_c, state0)
    hl_f, _ = mlstm_chunkwise(q_l, k_l, v_l, li_l, lf_l, st)
    li_c, lf_c = gates(g_c, 1)
    li_l, lf_l = gates(g_l, 1)
    hc_b, st = mlstm_chunkwise(flip(q_c), flip(k_c), flip(v_c), flip(li_c), flip(lf_c), state0)
    hl_b, _ = mlstm_chunkwise(flip(q_l), flip(k_l), flip(v_l), flip(li_l), flip(lf_l), st)
    return hc_f + flip(hc_b), hl_f + flip(hl_b)


def mlstm_readout(h, o, head_g):
    b_, _, n_, _ = h.shape
    hn = rms_norm(h.transpose(0, 2, 1, 3), head_g.reshape(MLSTM_HEADS, MLSTM_DV))
    return (jax.nn.sigmoid(o.astype(jnp.float32)) * hn.reshape(b_, n_, MLSTM_WIDTH)).astype(o.dtype)


def s5_discretise(a_re, a_im, log_dt):
    lam = lax.complex(a_re.astype(jnp.float32), a_im.astype(jnp.float32))
    a_bar = jnp.exp(lam * jnp.exp(log_dt.astype(jnp.float32))[:, None])
    return a_bar, (a_bar - 1) / lam


def s5_scan(bu, a_bar, init):
    bu = bu.at[:, 0].add(a_bar * init)
    a = jnp.broadcast_to(a_bar, (1, bu.shape[1]) + a_bar.shape)

    def op(e1, e2):
        return e1[0] * e2[0], e2[0] * e1[1] + e2[1]

    _, s = lax.associative_scan(op, (a, bu), axis=1)
    return s, s[:, -1]


def s5_bidirectional(u_c, u_l, a_re, a_im, log_dt, b_re, b_im, c_re, c_im, d_skip, glu_w, glu_b, need_ctx):
    f32 = jnp.float32
    bmat = lax.complex(b_re.astype(f32), b_im.astype(f32))
    cmat = lax.complex(c_re.astype(f32), c_im.astype(f32))

    def drive(u):
        ug = u.astype(f32).reshape(u.shape[0], u.shape[1], S5_GROUPS, S5_GROUP).astype(jnp.complex64)
        return jnp.einsum('bngh,gph->bngp', ug, bmat)

    bu_c, bu_l = drive(u_c), drive(u_l)
    init0 = jnp.zeros((u_l.shape[0], S5_GROUPS, S5_STATE), jnp.complex64)
    a_f, fac_f = s5_discretise(a_re[0], a_im[0], log_dt[0])
    s_cf, fin = s5_scan(bu_c * fac_f, a_f, init0)
    s_lf, _ = s5_scan(bu_l * fac_f, a_f, fin)
    a_b, fac_b = s5_discretise(a_re[1], a_im[1], log_dt[1])
    s_cb, fin = s5_scan(jnp.flip(bu_c, 1) * fac_b, a_b, init0)
    s_lb, _ = s5_scan(jnp.flip(bu_l, 1) * fac_b, a_b, fin)

    def readout(u, s):
        y = jnp.real(jnp.einsum('bngp,ghp->bngh', s, cmat)).reshape(u.shape) + d_skip.astype(f32) * u.astype(f32)
        g = jax.nn.gelu(y)
        return (g * jax.nn.sigmoid(g @ glu_w.astype(f32) + glu_b.astype(f32))).astype(u.dtype)

    y_l = readout(u_l, s_lf + jnp.flip(s_lb, 1))
    y_c = readout(u_c, s_cf + jnp.flip(s_cb, 1)) if need_ctx else None
    return y_c, y_l


def ab_mixer(h_c, h_l, w_in, gate_b, head_g, a_re, a_im, log_dt, b_re, b_im, c_re, c_im,
             d_skip, glu_w, glu_b, w_out, need_ctx):
    qc, kc, vc, oc, gc, uc = jnp.split(h_c @ w_in, AB_SPLITS, axis=-1)
    ql, kl, vl, ol, gl, ul = jnp.split(h_l @ w_in, AB_SPLITS, axis=-1)
    scale = MLSTM_DQK ** -0.5
    q_c = split_heads(qc, MLSTM_HEADS) * scale
    k_c = split_heads(kc, MLSTM_HEADS)
    q_l = axial_rope(split_heads(ql, MLSTM_HEADS)) * scale
    k_l = axial_rope(split_heads(kl, MLSTM_HEADS))
    hm_c, hm_l = mlstm_bidirectional(q_c, k_c, split_heads(vc, MLSTM_HEADS), gc,
                                     q_l, k_l, split_heads(vl, MLSTM_HEADS), gl, gate_b)
    ys_c, ys_l = s5_bidirectional(uc, ul, a_re, a_im, log_dt, b_re, b_im, c_re, c_im,
                                  d_skip, glu_w, glu_b, need_ctx)
    out_l = jnp.concatenate([mlstm_readout(hm_l, ol, head_g), ys_l], axis=-1) @ w_out
    out_c = (jnp.concatenate([mlstm_readout(hm_c, oc, head_g), ys_c], axis=-1) @ w_out) if need_ctx else None
    return out_c, out_l


def neighborhood_attention(q, k, v, k_ctx, v_ctx, rpb):
    b_, h_, n_, d = q.shape
    rows = n_ // GRID_W
    kh = min(WIN_H, rows)
    kw = WIN_W
    scale = d ** -0.5
    qg, kg, vg = (t.reshape(b_, h_, rows, GRID_W, d) for t in (q, k, v))
    col = jnp.arange(GRID_W)
    col_start = jnp.clip(col - kw // 2, 0, GRID_W - kw)
    col_mask = (col[None, :] >= col_start[:, None]) & (col[None, :] < col_start[:, None] + kw)
    ci = jnp.clip(col[None, :] - col[:, None], -(WIN_W - 1), WIN_W - 1) + (WIN_W - 1)

    def row_block(r):
        r_start = jnp.clip(r - kh // 2, 0, rows - kh)
        q_r = lax.dynamic_index_in_dim(qg, r, axis=2, keepdims=False)
        k_band = lax.dynamic_slice_in_dim(kg, r_start, kh, axis=2)
        v_band = lax.dynamic_slice_in_dim(vg, r_start, kh, axis=2)
        ri = r_start + jnp.arange(kh) - r + (WIN_H - 1)
        bias = rpb[:, ri[None, :, None], ci[:, None, :]].astype(jnp.float32)
        s_loc = jnp.einsum('bhqd,bhrkd->bhqrk', q_r, k_band).astype(jnp.float32) * scale + bias
        s_loc = jnp.where(col_mask[:, None, :], s_loc, -jnp.inf)
        s_ctx = jnp.einsum('bhqd,bhcd->bhqc', q_r, k_ctx).astype(jnp.float32) * scale
        p = jax.nn.softmax(jnp.concatenate([s_loc.reshape(b_, h_, GRID_W, kh * GRID_W), s_ctx], axis=-1), axis=-1)
        p = p.astype(v.dtype)
        p_loc = p[..., :kh * GRID_W].reshape(b_, h_, GRID_W, kh, GRID_W)
        return (jnp.einsum('bhqrk,bhrkd->bhqd', p_loc, v_band)
                + jnp.einsum('bhqc,bhcd->bhqd', p[..., kh * GRID_W:], v_ctx))

    out = lax.map(row_block, jnp.arange(rows))
    return jnp.moveaxis(out, 0, 2).reshape(b_, h_, n_, d)


def na_mixer(h_c, h_l, w_qkv, rpb, w_out, need_ctx):
    q_l, k_l, v_l = (split_heads(t, NA_HEADS) for t in jnp.split(h_l @ w_qkv, 3, axis=-1))
    k_c, v_c = (split_heads(t, NA_HEADS) for t in jnp.split(h_c @ w_qkv[:, D_MODEL:], 2, axis=-1))
    out_l = merge_heads(neighborhood_attention(q_l, k_l, v_l, k_c, v_c, rpb)) @ w_out
    out_c = None
    if need_ctx:
        q_c = split_heads(h_c @ w_qkv[:, :D_MODEL], NA_HEADS)
        s = jnp.einsum('bhqd,bhkd->bhqk', q_c, k_c).astype(jnp.float32) * NA_HEAD_DIM ** -0.5
        o_c = jnp.einsum('bhqk,bhkd->bhqd', jax.nn.softmax(s, axis=-1).astype(v_c.dtype), v_c)
        out_c = merge_heads(o_c) @ w_out
    return out_c, out_l


def conv_ffn(h, w_up, conv_w, conv_b, w_down):
    a, g = jnp.split(h @ w_up, 2, axis=-1)
    g = lax.conv_general_dilated(g, conv_w[:, None, :], window_strides=(1,),
                                 padding=((CONV_W // 2, CONV_W // 2),),
                                 dimension_numbers=('NWC', 'WIO', 'NWC'),
                                 feature_group_count=D_FF) + conv_b
    return (jax.nn.gelu(g) * a) @ w_down


def setup_inputs(seed: int = 0) -> dict:
    key = jax.random.key(seed)
    ks = iter(jax.random.split(key, 32))
    f32 = jnp.float32
    n_even = (DEPTH + 1) // 2
    n_odd = DEPTH // 2

    def nrm(shape, scale):
        return scale * jax.random.normal(next(ks), shape, f32)

    x = nrm((BATCH, SEQ, D_MODEL), 1.0)
    c = nrm((BATCH, D_MODEL), 1.0)
    ctx = nrm((BATCH, CTX_LEN, D_MODEL), 1.0)
    c_ctx = nrm((D_MODEL,), 1.0)
    mod_w = nrm((DEPTH, D_MODEL, 6 * D_MODEL), 0.5 * D_MODEL ** -0.5)
    mod_b = nrm((DEPTH, 6 * D_MODEL), 0.01)
    norm_mix_g = 1.0 + nrm((DEPTH, D_MODEL), 0.02)
    norm_ffn_g = 1.0 + nrm((DEPTH, D_MODEL), 0.02)
    ab_w_in = nrm((n_even, D_MODEL, AB_IN_WIDTH), D_MODEL ** -0.5)
    gate_base = jnp.stack([jnp.zeros((MLSTM_HEADS,), f32), jnp.linspace(3.0, 6.0, MLSTM_HEADS, dtype=f32)])
    mlstm_gate_b = nrm((n_even, 2, 2, MLSTM_HEADS), 0.1) + gate_base[None, None]
    mlstm_head_g = 1.0 + nrm((n_even, MLSTM_WIDTH), 0.02)
    s5_a_re = -0.5 + nrm((n_even, 2, S5_GROUPS, S5_STATE), 0.01)
    s5_a_im = math.pi * jnp.arange(S5_STATE, dtype=f32) + nrm((n_even, 2, S5_GROUPS, S5_STATE), 0.01)
    s5_log_dt = jax.random.uniform(next(ks), (n_even, 2, S5_GROUPS), f32,
                                   math.log(S5_DT_MIN), math.log(S5_DT_MAX))
    s5_b_re = nrm((n_even, S5_GROUPS, S5_STATE, S5_GROUP), (2 * S5_GROUP) ** -0.5)
    s5_b_im = nrm((n_even, S5_GROUPS, S5_STATE, S5_GROUP), (2 * S5_GROUP) ** -0.5)
    s5_c_re = nrm((n_even, S5_GROUPS, S5_GROUP, S5_STATE), S5_STATE ** -0.5)
    s5_c_im = nrm((n_even, S5_GROUPS, S5_GROUP, S5_STATE), S5_STATE ** -0.5)
    s5_d = nrm((n_even, S5_WIDTH), 0.5)
    s5_glu_w = nrm((n_even, S5_WIDTH, S5_WIDTH), S5_WIDTH ** -0.5)
    s5_glu_b = nrm((n_even, S5_WIDTH), 0.01)
    ab_w_out = nrm((n_even, D_MODEL, D_MODEL), D_MODEL ** -0.5)
    na_w_qkv = nrm((n_odd, D_MODEL, 3 * D_MODEL), D_MODEL ** -0.5)
    na_rpb = nrm((n_odd, NA_HEADS, 2 * WIN_H - 1, 2 * WIN_W - 1), 0.02)
    na_w_out = nrm((n_odd, D_MODEL, D_MODEL), D_MODEL ** -0.5)
    ffn_w_up = nrm((DEPTH, D_MODEL, 2 * D_FF), D_MODEL ** -0.5)
    ffn_conv_w = nrm((DEPTH, CONV_W, D_FF), CONV_W ** -0.5)
    ffn_conv_b = nrm((DEPTH, D_FF), 0.01)
    ffn_w_down = nrm((DEPTH, D_FF, D_MODEL), D_FF ** -0.5)
    final_norm_g = 1.0 + nrm((D_MODEL,), 0.02)
    return {"x": x, "c": c, "ctx": ctx, "c_ctx": c_ctx, "mod_w": mod_w, "mod_b": mod_b,
            "norm_mix_g": norm_mix_g, "norm_ffn_g": norm_ffn_g, "ab_w_in": ab_w_in,
            "mlstm_gate_b": mlstm_gate_b, "mlstm_head_g": mlstm_head_g,
            "s5_a_re": s5_a_re, "s5_a_im": s5_a_im, "s5_log_dt": s5_log_dt,
            "s5_b_re": s5_b_re, "s5_b_im": s5_b_im, "s5_c_re": s5_c_re, "s5_c_im": s5_c_im,
            "s5_d": s5_d, "s5_glu_w": s5_glu_w, "s5_glu_b": s5_glu_b, "ab_w_out": ab_w_out,
            "na_w_qkv": na_w_qkv, "na_rpb": na_rpb, "na_w_out": na_w_out,
            "ffn_w_up": ffn_w_up, "ffn_conv_w": ffn_conv_w, "ffn_conv_b": ffn_conv_b,
            "ffn_w_down": ffn_w_down, "final_norm_g": final_norm_g}


def reference(x, c, ctx, c_ctx, mod_w, mod_b, norm_mix_g, norm_ffn_g, ab_w_in, mlstm_gate_b,
              mlstm_head_g, s5_a_re, s5_a_im, s5_log_dt, s5_b_re, s5_b_im, s5_c_re, s5_c_im,
              s5_d, s5_glu_w, s5_glu_b, ab_w_out, na_w_qkv, na_rpb, na_w_out,
              ffn_w_up, ffn_conv_w, ffn_conv_b, ffn_w_down, final_norm_g):
    for i in range(DEPTH):
        need_ctx = i < DEPTH - 1
        sh_m, sc_m, gt_m, sh_f, sc_f, gt_f = ada_chunks(c, mod_w[i], mod_b[i])
        csh_m, csc_m, cgt_m, csh_f, csc_f, cgt_f = ada_chunks(c_ctx[None], mod_w[i], mod_b[i])
        h_l = modulate(x, norm_mix_g[i], sh_m, sc_m)
        h_c = modulate(ctx, norm_mix_g[i], csh_m, csc_m)
        if i % 2 == 0:
            e = i // 2
            mix_c, mix_l = ab_mixer(h_c, h_l, ab_w_in[e], mlstm_gate_b[e], mlstm_head_g[e],
                                    s5_a_re[e], s5_a_im[e], s5_log_dt[e], s5_b_re[e], s5_b_im[e],
                                    s5_c_re[e], s5_c_im[e], s5_d[e], s5_glu_w[e], s5_glu_b[e],
                                    ab_w_out[e], need_ctx)
        else:
            o = i // 2
            mix_c, mix_l = na_mixer(h_c, h_l, na_w_qkv[o], na_rpb[o], na_w_out[o], need_ctx)
        x = x + gt_m * mix_l
        x = x + gt_f * conv_ffn(modulate(x, norm_ffn_g[i], sh_f, sc_f),
                                ffn_w_up[i], ffn_conv_w[i], ffn_conv_b[i], ffn_w_down[i])
        if need_ctx:
            ctx = ctx + cgt_m * mix_c
            ctx = ctx + cgt_f * conv_ffn(modulate(ctx, norm_ffn_g[i], csh_f, csc_f),
                                         ffn_w_up[i], ffn_conv_w[i], ffn_conv_b[i], ffn_w_down[i])
    return rms_norm(x, final_norm_g)
```

```python
import contextlib
import numpy as np
import ml_dtypes
import concourse.bass as bass
import concourse.mybir as mybir
from concourse.bass_utils import run_bass_kernel_spmd

F32 = mybir.dt.float32
BF16 = mybir.dt.bfloat16
AF = mybir.ActivationFunctionType
ALU = mybir.AluOpType
AX = mybir.AxisListType
NPBF = ml_dtypes.bfloat16


class Tk:
    __slots__ = ("t", "w", "r", "name")

    def __init__(self, t, name=""):
        self.t = t
        self.w = []
        self.r = {}
        self.name = name

    def __getitem__(self, idx):
        return self.t[idx]


class K:
    def __init__(self, nc):
        self.nc = nc
        self.es = contextlib.ExitStack()
        self.eng = {"pe": nc.tensor, "act": nc.scalar, "dve": nc.vector, "pool": nc.gpsimd, "sp": nc.sync}
        self.sems = {}
        self.cnt = {}
        self.waited = {e: {} for e in self.eng}
        for e in ("pe", "act", "dve", "pool", "sp"):
            self.sems[e] = self.es.enter_context(nc.semaphore("s_" + e))
            self.cnt[e] = 0
        self.free_dsems = []
        self.phase_dsems = []
        self.nd = 0
        self.pst = None
        self.uid = 0

    def _stack(self):
        return self.pst if self.pst is not None else self.es

    def sb(self, name, shape, dt):
        self.uid += 1
        return Tk(self._stack().enter_context(self.nc.sbuf_tensor("%s_%d" % (name, self.uid), list(shape), dt)), name)

    def ps(self, name, shape, dt=F32):
        self.uid += 1
        return Tk(self._stack().enter_context(self.nc.psum_tensor("%s_%d" % (name, self.uid), list(shape), dt)), name)

    def dram(self, name, shape, dt, kind="Internal"):
        return Tk(self.nc.dram_tensor(name, list(shape), dt, kind=kind).ap(), name)

    def dsem(self):
        if self.free_dsems:
            key = self.free_dsems.pop()
        else:
            self.nd += 1
            key = "d%d" % self.nd
            self.sems[key] = self.es.enter_context(self.nc.semaphore(key))
            self.cnt[key] = 0
        if self.pst is not None:
            self.phase_dsems.append(key)
        return key

    @contextlib.contextmanager
    def phase(self):
        assert self.pst is None
        self.pst = contextlib.ExitStack()
        self.phase_dsems = []
        yield
        self.barrier()
        self.pst.close()
        self.pst = None
        self.free_dsems += self.phase_dsems
        self.phase_dsems = []

    def _wait(self, e, deps):
        best = {}
        for (s, v) in deps:
            if s == "pe" and e == "pe":
                continue
            if best.get(s, 0) < v:
                best[s] = v
        for s, v in best.items():
            if self.waited[e].get(s, 0) < v:
                self.eng[e].wait_ge(self.sems[s], v)
                self.waited[e][s] = v

    def op(self, e, fn, reads=(), writes=(), sig=True):
        deps = []
        for t in reads:
            deps += t.w
        for t in writes:
            deps += t.w
            deps += list(t.r.items())
        self._wait(e, deps)
        inst = fn(self.eng[e])
        if sig:
            self.cnt[e] += 1
            inst.then_inc(self.sems[e], 1)
            v = self.cnt[e]
        else:
            v = self.cnt[e] + 1
        for t in reads:
            if t.r.get(e, 0) < v:
                t.r[e] = v
        for t in writes:
            t.w = [(e, v)]
            t.r = {}
        return inst

    def dma(self, q, sem, out_ap, in_ap, dst=None, src=None, **kw):
        deps = []
        if src is not None:
            deps += src.w
        if dst is not None:
            deps += dst.w
            deps += list(dst.r.items())
        self._wait(q, deps)
        inst = self.eng[q].dma_start(out=out_ap, in_=in_ap, **kw)
        inst.then_inc(self.sems[sem], 16)
        self.cnt[sem] += 16
        v = self.cnt[sem]
        if src is not None and src.r.get(sem, 0) < v:
            src.r[sem] = v
        if dst is not None:
            dst.w = [(sem, v)]
            dst.r = {}
        return inst

    def allgather(self, sem, out_tk, in_tk, groups):
        deps = list(in_tk.w) + list(out_tk.w) + list(out_tk.r.items())
        self._wait("pool", deps)
        inst = self.nc.gpsimd.collective_compute("AllGather", op=ALU.bypass, replica_groups=groups,
                                                 ins=[in_tk[:]], outs=[out_tk[:]])
        inst.then_inc(self.sems[sem], 1)
        self.cnt[sem] += 1
        v = self.cnt[sem]
        in_tk.r[sem] = v
        out_tk.w = [(sem, v)]
        out_tk.r = {}

    def barrier(self):
        sp = self.eng["sp"]
        for s, v in self.cnt.items():
            if s == "sp" or v == 0:
                continue
            if self.waited["sp"].get(s, 0) < v:
                sp.wait_ge(self.sems[s], v)
                self.waited["sp"][s] = v
        self.cnt["sp"] += 1
        sp.nop().then_inc(self.sems["sp"], 1)
        for e in ("pe", "act", "dve", "pool"):
            self.eng[e].wait_ge(self.sems["sp"], self.cnt["sp"])
            for s, v in self.cnt.items():
                self.waited[e][s] = v
        for s, v in self.cnt.items():
            self.waited["sp"][s] = v

    def close(self):
        self.es.close()


def ceil_div(a, b):
    return (a + b - 1) // b


class PsPool:
    def __init__(self, k, n, name="ps", shape=(128, 512), dt=F32):
        self.t = [k.ps("%s%d" % (name, i), shape, dt) for i in range(n)]
        self.i = 0

    def get(self):
        t = self.t[self.i % len(self.t)]
        self.i += 1
        return t


def load_rows(k, q, sem, dst_tk, dst_ap, src_tk, src_ap):
    k.dma(q, sem, dst_ap, src_ap.rearrange("(kc p) t -> p kc t", p=128), dst=dst_tk, src=src_tk)


def gemm(k, mode, KC, blocks, load_a, W, n0, N, NB, epi, pp, abufs=1, wbufs=2, tag="g", wq="sp", TBmax=None):
    TBmax = TBmax or max(b[1] for b in blocks)
    A = [k.sb(tag + "A%d" % i, [128, KC, TBmax], BF16) for i in range(abufs)]
    Asem = [k.dsem() for _ in range(abufs)]
    Wt = [k.sb(tag + "W%d" % i, [128, KC, NB], BF16) for i in range(wbufs)]
    Wsem = [k.dsem() for _ in range(wbufs)]
    wblocks = [(wb, min(NB, N - wb)) for wb in range(0, N, NB)]
    steps = [(bi, wi) for bi in range(len(blocks)) for wi in range(len(wblocks))]

    def issue_a(bi):
        tb0, ntb, _ = blocks[bi]
        load_a(A[bi % abufs], Asem[bi % abufs], tb0, ntb)

    def issue_w(si):
        bi, wi = steps[si]
        wb, ncols = wblocks[wi]
        wt = Wt[si % wbufs]
        if callable(W):
            wtk, wap = W(n0 + wb, ncols)
        else:
            wtk, wap = W, W[:, n0 + wb:n0 + wb + ncols]
        k.dma(wq, Wsem[si % wbufs], wt[:, :, :ncols], wap.rearrange("(kc p) n -> p kc n", p=128), dst=wt, src=wtk)

    issue_a(0)
    issue_w(0)
    for si, (bi, wi) in enumerate(steps):
        nxt_a = si + 1 < len(steps) and steps[si + 1][1] == 0
        if si + 1 < len(steps):
            if nxt_a and abufs >= 2:
                issue_a(steps[si + 1][0])
            issue_w(si + 1)
        tb0, ntb, subs = blocks[bi]
        wb, ncols = wblocks[wi]
        at = A[bi % abufs]
        wt = Wt[si % wbufs]
        if mode == "fm":
            for nch in range(0, ncols, 128):
                m = min(128, ncols - nch)
                for (ts, n) in subs:
                    ps = pp.get()
                    for kc in range(KC):
                        k.op("pe", lambda e: e.matmul(out=ps[:m, :n], lhsT=wt[:, kc, nch:nch + m],
                                                     rhs=at[:, kc, ts:ts + n], start=(kc == 0), stop=(kc == KC - 1)),
                             reads=[wt, at], writes=[ps], sig=(kc == KC - 1))
                    epi(ps, m, n, wb + nch, tb0 + ts, at=at, ts=ts)
        else:
            for (ts, mtok) in subs:
                ps = pp.get()
                for kc in range(KC):
                    k.op("pe", lambda e: e.matmul(out=ps[:mtok, :ncols], lhsT=at[:, kc, ts:ts + mtok],
                                                 rhs=wt[:, kc, :ncols], start=(kc == 0), stop=(kc == KC - 1)),
                         reads=[wt, at], writes=[ps], sig=(kc == KC - 1))
                epi(ps, mtok, ncols, tb0 + ts, wb, at=at, ts=ts)
        if nxt_a and abufs < 2:
            issue_a(steps[si + 1][0])


def mk_blocks(T, TB, sub, bounds=()):
    blocks = []
    cuts = sorted(set([0, T] + [b for b in bounds if 0 < b < T]))
    segs = [(cuts[i], cuts[i + 1]) for i in range(len(cuts) - 1)]
    t = 0
    while t < T:
        n = min(TB, T - t)
        subs = []
        for (s0, s1) in segs:
            a, b = max(s0, t), min(s1, t + n)
            x = a
            while x < b:
                c = min(sub, b - x)
                subs.append((x - t, c))
                x += c
        blocks.append((t, n, subs))
        t += n
    return blocks


class Stage:
    def __init__(self, k, n, name, shape, dt):
        self.t = [k.sb("%s%d" % (name, i), shape, dt) for i in range(n)]
        self.s = [k.dsem() for _ in range(n)]
        self.i = 0

    def get(self):
        j = self.i % len(self.t)
        self.i += 1
        return self.t[j], self.s[j]


def cast_rows(k, src, dst, R, N, cw=2048, q_in="sp", q_out="act", sc0=0, bufs=None):
    nb = 3
    if bufs is None:
        bufs = cast_bufs(k, cw)
    tin, tout, sin, sout, ctr = bufs
    i = ctr[0]
    for r0 in range(0, R, 128):
        rr = min(128, R - r0)
        for c0 in range(0, N, cw):
            cc = min(cw, N - c0)
            b = i % nb
            k.dma(q_in, sin[b], tin[b][:rr, :cc], src[r0:r0 + rr, sc0 + c0:sc0 + c0 + cc], dst=tin[b], src=src)
            if i % 2 == 0:
                k.op("act", lambda e: e.copy(out=tout[b][:rr, :cc], in_=tin[b][:rr, :cc]), reads=[tin[b]], writes=[tout[b]])
            else:
                k.op("dve", lambda e: e.tensor_copy(out=tout[b][:rr, :cc], in_=tin[b][:rr, :cc]), reads=[tin[b]], writes=[tout[b]])
            k.dma(q_out, sout[b], dst[r0:r0 + rr, c0:c0 + cc], tout[b][:rr, :cc], src=tout[b], dst=dst)
            i += 1
    ctr[0] = i


def cast_rows_pieces(k, src, pieces, R, N, w, cw=2048):
    tin, tout, sin, sout, ctr = cast_bufs(k, cw)
    nb = 3
    i = 0
    for r0 in range(0, R, 128):
        rr = min(128, R - r0)
        for c0 in range(0, N, cw):
            cc = min(cw, N - c0)
            b = i % nb
            k.dma("sp", sin[b], tin[b][:rr, :cc], src[r0:r0 + rr, c0:c0 + cc], dst=tin[b], src=src)
            if i % 2 == 0:
                k.op("act", lambda e: e.copy(out=tout[b][:rr, :cc], in_=tin[b][:rr, :cc]), reads=[tin[b]], writes=[tout[b]])
            else:
                k.op("dve", lambda e: e.tensor_copy(out=tout[b][:rr, :cc], in_=tin[b][:rr, :cc]), reads=[tin[b]], writes=[tout[b]])
            for p0 in range(c0, c0 + cc, w):
                pc = pieces[p0 // w]
                k.dma("act", sout[b], pc[r0:r0 + rr, :], tout[b][:rr, p0 - c0:p0 - c0 + w], src=tout[b], dst=pc)
            i += 1


def cast_bufs(k, cw=2048):
    nb = 3
    return ([k.sb("cin%d" % i, [128, cw], F32) for i in range(nb)], [k.sb("cout%d" % i, [128, cw], BF16) for i in range(nb)],
            [k.dsem() for _ in range(nb)], [k.dsem() for _ in range(nb)], [0])


def norm_mod(k, xT, hT, segs, gsc, sh, ones_bf, D, eps=1e-6, hT_col0=0, TBN=256, out_f32=None, writer=None):
    KC = D // 128
    nb = 2
    xt = [k.sb("nx%d" % i, [128, KC, TBN], F32) for i in range(nb)]
    xs = [k.dsem() for _ in range(nb)]
    sq = k.sb("nsq", [128, KC, TBN], BF16)
    ht = [k.sb("nh%d" % i, [128, KC, TBN], BF16 if out_f32 is None else F32) for i in range(nb)]
    hs = [k.dsem() for _ in range(nb)]
    rs = k.sb("nrs", [128, TBN], F32)
    pp = PsPool(k, 2, "nps")
    i = 0
    for (t0, n, st) in segs:
        for a in range(t0, t0 + n, TBN):
            m = min(TBN, t0 + n - a)
            b = i % nb
            x = xt[b]
            k.dma("sp", xs[b], x[:, :, :m], xT[:, a:a + m].rearrange("(kc p) t -> p kc t", p=128), dst=x, src=xT)
            k.op("act", lambda e: e.activation(out=sq[:, :, :m], in_=x[:, :, :m], func=AF.Square), reads=[x], writes=[sq])
            ps = pp.get()
            for kc in range(KC):
                k.op("pe", lambda e: e.matmul(out=ps[:, :m], lhsT=ones_bf[:, :], rhs=sq[:, kc, :m], start=(kc == 0), stop=(kc == KC - 1)),
                     reads=[sq, ones_bf], writes=[ps], sig=(kc == KC - 1))
            k.op("act", lambda e: e.activation(out=rs[:, :m], in_=ps[:, :m], func=AF.Sqrt, scale=1.0 / D, bias=k.eps_t[:, 0:1]), reads=[ps, k.eps_t], writes=[rs])
            k.op("dve", lambda e: e.reciprocal(out=rs[:, :m], in_=rs[:, :m]), reads=[rs], writes=[rs])
            k.op("dve", lambda e: e.tensor_tensor(out=x[:, :, :m], in0=x[:, :, :m], in1=rs[:, :m].unsqueeze(1).broadcast_to([128, KC, m]), op=ALU.mult),
                 reads=[x, rs], writes=[x])
            h = ht[b]
            k.op("dve", lambda e: e.tensor_tensor(out=x[:, :, :m], in0=x[:, :, :m], in1=gsc[st][:, :].unsqueeze(2).broadcast_to([128, KC, m]), op=ALU.mult),
                 reads=[x, gsc[st]], writes=[x])
            if sh is not None:
                k.op("pool", lambda e: e.tensor_tensor(out=h[:, :, :m], in0=x[:, :, :m], in1=sh[st][:, :].unsqueeze(2).broadcast_to([128, KC, m]), op=ALU.add),
                     reads=[x, sh[st]], writes=[h])
            else:
                k.op("act", lambda e: e.copy(out=h[:, :, :m], in_=x[:, :, :m]), reads=[x], writes=[h])
            dstT = hT if out_f32 is None else out_f32
            if writer is not None:
                writer(h, a, m, hs[b])
            else:
                k.dma("act", hs[b], dstT[:, hT_col0 + a:hT_col0 + a + m].rearrange("(kc p) t -> p kc t", p=128), h[:, :, :m], src=h, dst=dstT)
            i += 1


def mlstm_scan(k, C, qT, kT, ktm, vtm, gates, gb_bc, hdir, orders, NH):
    L = 128
    pg = k.ps("mg", [128, 512]); psc = k.ps("msc", [128, 512]); pden = k.ps("mden", [128, 512])
    pnum = [k.ps("mnum%d" % i, [128, 512]) for i in range(2)]
    pS = [k.ps("mS%d" % i, [128, 512]) for i in range(2)]
    S32 = {}; Sbf = {}; n32 = {}; nbf = {}
    for d in range(2):
        for h in range(NH):
            for dc in range(2):
                S32[d, h, dc] = k.sb("S32", [128, 512], F32); Sbf[d, h, dc] = S32[d, h, dc]
                n32[d, h, dc] = k.sb("n32", [128, 2], F32); nbf[d, h, dc] = n32[d, h, dc]
                for t in (S32[d, h, dc], n32[d, h, dc]):
                    k.op("dve", lambda e: e.memset(t[:, :], 0.0), writes=[t])
    nb = 2
    bufs = []
    for i in range(nb):
        bufs.append(dict(q=k.sb("mq", [128, NH * 2, L], F32), kt=k.sb("mkt", [128, NH * 2, L], F32),
                         km=k.sb("mkm", [128, NH * 256], F32), v=k.sb("mv", [128, NH * 512], F32),
                         g=k.sb("mgt", [128, 4 * NH], F32), sem=k.dsem()))
    G = k.sb("mG", [128, 2 * NH], F32); nlf = k.sb("mnlf", [128, NH], F32); lf = k.sb("mlf", [128, NH], F32)
    Fc = k.sb("mFc", [128, NH], F32); aa = k.sb("maa", [128, NH], F32)
    ly0 = k.sb("mly0", [128, NH], F32); lt = k.sb("mlt", [128, NH], F32)
    Lt = [k.sb("mLt%d" % i, [128, 128], F32) for i in range(2)]
    Dm = [k.sb("mDm%d" % i, [128, 128], F32) for i in range(2)]
    Wm = [k.sb("mWm%d" % i, [128, 128], F32) for i in range(2)]
    EF = [k.sb("mEF%d" % i, [128, 128], F32) for i in range(2)]
    qs = [k.sb("mqs%d" % i, [128, 2, 128], F32) for i in range(2)]
    scw = [k.sb("mscw%d" % i, [128, 128], F32) for i in range(2)]
    kw = [k.sb("mkw%d" % i, [128, 256], F32) for i in range(2)]
    wsrc = [k.sb("mws%d" % i, [128, 1], F32) for i in range(2)]
    rd = [k.sb("mrd%d" % i, [128, 1], F32) for i in range(2)]
    ho = Stage(k, 3, "mho", [128, 512], F32)
    it = 0
    nsteps = len(orders[0])
    for s in range(nsteps):
        for d in range(2):
            ci = orders[d][s]
            t0 = ci * L
            B = bufs[it % nb]; it += 1
            sem = B["sem"]
            k.dma("sp", sem, B["q"][:, :, :], qT[:, t0:t0 + L].rearrange("(c p) t -> p c t", p=128), dst=B["q"], src=qT)
            k.dma("sp", sem, B["kt"][:, :, :], kT[:, t0:t0 + L].rearrange("(c p) t -> p c t", p=128), dst=B["kt"], src=kT)
            k.dma("act", sem, B["km"][:, :], ktm[t0:t0 + L, :], dst=B["km"], src=ktm)
            k.dma("act", sem, B["v"][:, :], vtm[t0:t0 + L, :], dst=B["v"], src=vtm)
            k.dma("sp", sem, B["g"][:, :], gates[t0:t0 + L, :], dst=B["g"], src=gates)
            lastw = B["g"].w
            for nm in ("q", "kt", "km", "v"):
                B[nm].w = list(lastw)
            tri, mm = C["tri"][d], C["mm"][d]
            lc = L - 1 if d == 0 else 0
            k.op("dve", lambda e: e.tensor_tensor(out=G[:, :], in0=B["g"][:, d * 2 * NH:(d + 1) * 2 * NH], in1=gb_bc[:, d * 2 * NH:(d + 1) * 2 * NH], op=ALU.add),
                 reads=[B["g"], gb_bc], writes=[G])
            k.op("act", lambda e: e.activation(out=nlf[:, :], in_=G[:, NH:2 * NH], func=AF.Exp, scale=-1.0), reads=[G], writes=[nlf])
            k.op("act", lambda e: e.activation(out=ly0[:, :], in_=nlf[:, :], func=AF.Ln, bias=C["one_c"][:, 0:1], scale=1.0), reads=[nlf, C["one_c"]], writes=[ly0])
            k.op("act", lambda e: e.activation(out=lt[:, :], in_=ly0[:, :], func=AF.Exp, scale=-1.0), reads=[ly0], writes=[lt])
            k.op("dve", lambda e: e.scalar_tensor_tensor(out=lt[:, :], in0=nlf[:, :], scalar=1.0, in1=lt[:, :], op0=ALU.add, op1=ALU.mult), reads=[nlf, lt], writes=[lt])
            k.op("dve", lambda e: e.scalar_tensor_tensor(out=nlf[:, :], in0=lt[:, :], scalar=-1.0, in1=ly0[:, :], op0=ALU.add, op1=ALU.add), reads=[lt, ly0], writes=[nlf])
            k.op("dve", lambda e: e.tensor_scalar(out=lf[:, :], in0=nlf[:, :], scalar1=-1.0, scalar2=None, op0=ALU.mult), reads=[nlf], writes=[lf])
            k.op("pe", lambda e: e.matmul(out=pg[:, 384:384 + NH], lhsT=tri[:, :], rhs=lf[:, :], start=True, stop=True), reads=[tri, lf], writes=[pg])
            k.op("dve", lambda e: e.tensor_copy(out=Fc[:, :], in_=pg[:, 384:384 + NH]), reads=[pg], writes=[Fc])
            k.op("dve", lambda e: e.tensor_tensor(out=aa[:, :], in0=G[:, 0:NH], in1=Fc[:, :], op=ALU.subtract), reads=[G, Fc], writes=[aa])
            for h in range(NH):
                j = h % 2
                k.op("dve", lambda e: e.tensor_scalar(out=Lt[j][:, :], in0=C["ones_f"][:, :], scalar1=lf[:, h:h + 1], scalar2=None, op0=ALU.mult),
                     reads=[C["ones_f"], lf], writes=[Lt[j]])
                fr = pg[:, h * 128:(h + 1) * 128]
                k.op("pe", lambda e: e.matmul(out=fr, lhsT=Lt[j][:, :], rhs=tri[:, :], start=True, stop=True), reads=[Lt[j], tri], writes=[pg])
                k.op("dve", lambda e: e.scalar_tensor_tensor(out=Dm[j][:, :], in0=fr, scalar=Fc[:, h:h + 1], in1=mm[:, :], op0=ALU.subtract, op1=ALU.min),
                     reads=[pg, Fc, mm], writes=[Dm[j]])
                k.op("act", lambda e: e.activation(out=Wm[j][:, :], in_=Dm[j][:, :], func=AF.Exp, bias=G[:, h:h + 1], scale=1.0), reads=[Dm[j], G], writes=[Wm[j]])
                k.op("act", lambda e: e.activation(out=EF[j][:, :], in_=fr, func=AF.Exp), reads=[pg], writes=[EF[j]])
                k.op("act", lambda e: e.activation(out=wsrc[j][:, :], in_=pg[:, h * 128 + lc:h * 128 + lc + 1], func=AF.Exp, bias=aa[:, h:h + 1], scale=1.0),
                     reads=[pg, aa], writes=[wsrc[j]])
                k.op("dve", lambda e: e.tensor_tensor(out=qs[j][:, :, :], in0=B["q"][:, 2 * h:2 * h + 2, :], in1=EF[j][:, :].unsqueeze(1).broadcast_to([128, 2, 128]), op=ALU.mult),
                     reads=[B["q"], EF[j]], writes=[qs[j]])
                sc = psc[:, h * 128:(h + 1) * 128]
                for dc in range(2):
                    k.op("pe", lambda e: e.matmul(out=sc, lhsT=B["kt"][:, 2 * h + dc, :], rhs=B["q"][:, 2 * h + dc, :], start=(dc == 0), stop=(dc == 1)),
                         reads=[B["kt"], B["q"]], writes=[psc], sig=(dc == 1))
                k.op("dve", lambda e: e.tensor_tensor(out=scw[j][:, :], in0=sc, in1=Wm[j][:, :], op=ALU.mult), reads=[psc, Wm[j]], writes=[scw[j]])
                pn = pnum[h % 2]
                vv = B["v"][:, h * 512:(h + 1) * 512]
                k.op("pe", lambda e: e.matmul(out=pn[:, :], lhsT=scw[j][:, :], rhs=vv, start=True, stop=False), reads=[scw[j], B["v"]], writes=[pn], sig=False)
                for dc in range(2):
                    k.op("pe", lambda e: e.matmul(out=pn[:, :], lhsT=qs[j][:, dc, :], rhs=Sbf[d, h, dc][:, :], start=False, stop=(dc == 1)),
                         reads=[qs[j], Sbf[d, h, dc]], writes=[pn], sig=(dc == 1))
                dn = pden[:, 2 * h:2 * h + 2]
                k.op("pe", lambda e: e.matmul(out=dn, lhsT=scw[j][:, :], rhs=C["ones_f"][:, 0:2], start=True, stop=False), reads=[scw[j], C["ones_f"]], writes=[pden], sig=False)
                for dc in range(2):
                    k.op("pe", lambda e: e.matmul(out=dn, lhsT=qs[j][:, dc, :], rhs=nbf[d, h, dc][:, :], start=False, stop=(dc == 1)),
                         reads=[qs[j], nbf[d, h, dc]], writes=[pden], sig=(dc == 1))
                k.op("act", lambda e: e.activation(out=rd[j][:, :], in_=pden[:, 2 * h:2 * h + 1], func=AF.Abs), reads=[pden], writes=[rd[j]])
                k.op("dve", lambda e: e.tensor_scalar(out=rd[j][:, :], in0=rd[j][:, :], scalar1=1.0, scalar2=None, op0=ALU.max), reads=[rd[j]], writes=[rd[j]])
                k.op("dve", lambda e: e.reciprocal(out=rd[j][:, :], in_=rd[j][:, :]), reads=[rd[j]], writes=[rd[j]])
                st, ssem = ho.get()
                k.op("act", lambda e: e.activation(out=st[:, :], in_=pn[:, :], func=AF.Copy, scale=rd[j][:, 0:1]), reads=[pn, rd[j]], writes=[st])
                k.dma("sp", ssem, hdir[d][t0:t0 + L, h * 512:(h + 1) * 512], st[:, :], src=st, dst=hdir[d])
                k.op("pool", lambda e: e.tensor_scalar(out=kw[j][:, :], in0=B["km"][:, h * 256:(h + 1) * 256], scalar1=wsrc[j][:, 0:1], scalar2=None, op0=ALU.mult),
                     reads=[B["km"], wsrc[j]], writes=[kw[j]])
                for dc in range(2):
                    k.op("pe", lambda e: e.matmul(out=pS[dc][:, :], lhsT=kw[j][:, dc * 128:(dc + 1) * 128], rhs=vv, start=True, stop=True), reads=[kw[j], B["v"]], writes=[pS[dc]])
                    nn = pden[:, 16 + 4 * h + 2 * dc:16 + 4 * h + 2 * dc + 2]
                    k.op("pe", lambda e: e.matmul(out=nn, lhsT=kw[j][:, dc * 128:(dc + 1) * 128], rhs=C["ones_f"][:, 0:2], start=True, stop=True), reads=[kw[j], C["ones_f"]], writes=[pden])
                    dec = EF[j][:, lc:lc + 1]
                    k.op("dve", lambda e: e.scalar_tensor_tensor(out=S32[d, h, dc][:, :], in0=S32[d, h, dc][:, :], scalar=dec, in1=pS[dc][:, :], op0=ALU.mult, op1=ALU.add),
                         reads=[S32[d, h, dc], EF[j], pS[dc]], writes=[S32[d, h, dc]])
                    k.op("dve", lambda e: e.scalar_tensor_tensor(out=n32[d, h, dc][:, :], in0=n32[d, h, dc][:, :], scalar=dec, in1=nn, op0=ALU.mult, op1=ALU.add),
                         reads=[n32[d, h, dc], EF[j], pden], writes=[n32[d, h, dc]])


def host_consts():
    idx = np.arange(128)
    tri_f = (idx[:, None] <= idx[None, :]).astype(np.float32)
    tri_b = tri_f.T.copy()
    mm_f = (tri_f - 1.0) * 30000.0
    mm_b = (tri_b - 1.0) * 30000.0
    ident = np.eye(128, dtype=np.float32)
    return np.concatenate([tri_f, tri_b, mm_f, mm_b, ident], axis=1).astype(np.float32)


def load_consts(k, cdram):
    C = {}
    ct = k.sb("consts", [128, 640], F32)
    s = k.dsem()
    k.dma("sp", s, ct[:, :], cdram[:, :], dst=ct)
    C["raw"] = ct

    class View:
        def __init__(self, tk, a, b):
            self.tk = tk; self.a = a; self.b = b;
        def __getitem__(self, idx):
            return self.tk.t[:, self.a:self.b][idx]
    def view(a, b):
        v = Tk.__new__(Tk)
        v.t = _Sub(ct, a, b); v.w = ct.w; v.r = ct.r; v.name = "cv"
        return v
    C["tri"] = [view(0, 128), view(128, 256)]
    C["mm"] = [view(256, 384), view(384, 512)]
    C["ident"] = view(512, 640)
    of = k.sb("ones_f", [128, 128], F32); ob = k.sb("ones_bf", [128, 128], BF16); oc = k.sb("one_c", [128, 1], F32)
    ib = k.sb("ident_bf", [128, 128], BF16); ep = k.sb("eps_t", [128, 1], F32)
    k.op("dve", lambda e: e.memset(of[:, :], 1.0), writes=[of])
    k.op("dve", lambda e: e.memset(ob[:, :], 1.0), writes=[ob])
    k.op("dve", lambda e: e.memset(oc[:, :], 1.0), writes=[oc])
    k.op("dve", lambda e: e.memset(ep[:, :], 1e-6), writes=[ep])
    k.op("dve", lambda e: e.tensor_copy(out=ib[:, :], in_=ct[:, 512:640]), reads=[ct], writes=[ib])
    C["ones_f"] = of; C["ones_bf"] = ob; C["one_c"] = oc; C["ident_bf"] = ib
    k.eps_t = ep
    return C


class _Sub:
    def __init__(self, tk, a, b):
        self.tk = tk; self.a = a; self.b = b

    def __getitem__(self, idx):
        return self.tk.t[:, self.a:self.b][idx]


TWO_PI = 6.283185307179586


def _cmul_sc(k, eng, o_re, o_im, a_re, a_im, p_re, p_im, tmp, tks_r, tks_w):
    k.op(eng, lambda e: e.tensor_scalar(out=tmp, in0=a_im, scalar1=p_im, scalar2=None, op0=ALU.mult), reads=tks_r, writes=tks_w)
    k.op("dve", lambda e: e.scalar_tensor_tensor(out=o_re, in0=a_re, scalar=p_re, in1=tmp, op0=ALU.mult, op1=ALU.subtract), reads=tks_r + tks_w, writes=tks_w)
    k.op(eng, lambda e: e.tensor_scalar(out=tmp, in0=a_im, scalar1=p_re, scalar2=None, op0=ALU.mult), reads=tks_r + tks_w, writes=tks_w)
    k.op("dve", lambda e: e.scalar_tensor_tensor(out=o_im, in0=a_re, scalar=p_im, in1=tmp, op0=ALU.mult, op1=ALU.add), reads=tks_r + tks_w, writes=tks_w)


def s5_phase(k, C, uT, par, Btre_d, Btim_d, Ctre_d, Ctim_d, dsk_d, outT, out_row0, T, NG, chunks):
    NST = NG // 2
    NCC = NG * 16 // 128
    s0 = k.dsem()
    ub = k.sb("s5ub", [128, NCC, T], BF16)
    uft = [k.sb("s5uft%d" % i, [128, 1088], F32) for i in range(2)]
    ufs = [k.dsem() for _ in range(2)]
    iu = 0
    for c_ in range(NCC):
        for a_ in range(0, T, 1088):
            n_ = min(1088, T - a_)
            tt = uft[iu % 2]
            k.dma("sp", ufs[iu % 2], tt[:, :n_], uT[c_ * 128:(c_ + 1) * 128, a_:a_ + n_], dst=tt, src=uT)
            if iu % 2 == 0:
                k.op("act", lambda e: e.copy(out=ub[:, c_, a_:a_ + n_], in_=tt[:, :n_]), reads=[tt], writes=[ub])
            else:
                k.op("dve", lambda e: e.tensor_copy(out=ub[:, c_, a_:a_ + n_], in_=tt[:, :n_]), reads=[tt], writes=[ub])
            iu += 1
    P = k.sb("s5P", [128, 3, 2 * NST], F32)
    k.dma("act", s0, P[:, :, :], par[:, :, :], dst=P)
    Bre32 = k.sb("s5Bre32", [128, NST, 128], F32); Bim32 = k.sb("s5Bim32", [128, NST, 128], F32)
    Cre32 = k.sb("s5Cre32", [128, NST, 64], F32); Cim32 = k.sb("s5Cim32", [128, NST, 64], F32)
    dsk = k.sb("s5dsk", [128, NCC], F32)
    k.dma("sp", s0, Bre32[:, :, :], Btre_d[:, :, :], dst=Bre32); k.dma("sp", s0, Bim32[:, :, :], Btim_d[:, :, :], dst=Bim32)
    k.dma("act", s0, Cre32[:, :, :], Ctre_d[:, :, :], dst=Cre32); k.dma("act", s0, Cim32[:, :, :], Ctim_d[:, :, :], dst=Cim32)
    k.dma("sp", s0, dsk[:, :], dsk_d[:, :], dst=dsk)
    for t in (P, Bre32, Bim32, Cre32, Cim32, dsk):
        t.w = list(dsk.w)
    Bre = k.sb("s5Bre", [128, NST, 128], BF16); Bim = k.sb("s5Bim", [128, NST, 128], BF16)
    Cre = k.sb("s5Cre", [128, NST, 64], BF16); Cimn = k.sb("s5Cimn", [128, NST, 64], BF16)
    k.op("dve", lambda e: e.tensor_copy(out=Bre[:, :, :], in_=Bre32[:, :, :]), reads=[Bre32], writes=[Bre])
    k.op("dve", lambda e: e.tensor_copy(out=Bim[:, :, :], in_=Bim32[:, :, :]), reads=[Bim32], writes=[Bim])
    k.op("dve", lambda e: e.tensor_copy(out=Cre[:, :, :], in_=Cre32[:, :, :]), reads=[Cre32], writes=[Cre])
    k.op("dve", lambda e: e.tensor_scalar(out=Cimn[:, :, :], in0=Cim32[:, :, :], scalar1=-1.0, scalar2=None, op0=ALU.mult), reads=[Cim32], writes=[Cimn])
    W = 2 * NST
    def tl(nm):
        return k.sb("s5" + nm, [128, W], F32)
    dt, zr, zi, rmag, sn, cs, tmp, tmp2, kf, abr, abi, fre, fim, den = (tl(n) for n in
        ("dt", "zr", "zi", "rmag", "sn", "cs", "tmp", "tmp2", "kf", "abr", "abi", "fre", "fim", "den"))
    ki = k.sb("s5ki", [128, W], mybir.dt.int32)
    are, aim, ldt = P[:, 0, :], P[:, 1, :], P[:, 2, :]
    k.op("act", lambda e: e.activation(out=dt[:, :], in_=ldt, func=AF.Exp), reads=[P], writes=[dt])
    k.op("dve", lambda e: e.tensor_tensor(out=zr[:, :], in0=are, in1=dt[:, :], op=ALU.mult), reads=[P, dt], writes=[zr])
    k.op("dve", lambda e: e.tensor_tensor(out=zi[:, :], in0=aim, in1=dt[:, :], op=ALU.mult), reads=[P, dt], writes=[zi])
    k.op("act", lambda e: e.activation(out=rmag[:, :], in_=zr[:, :], func=AF.Exp), reads=[zr], writes=[rmag])

    def sin_of(dst, shift):
        k.op("dve", lambda e: e.tensor_scalar(out=tmp[:, :], in0=zi[:, :], scalar1=shift, scalar2=None, op0=ALU.add), reads=[zi], writes=[tmp])
        k.op("dve", lambda e: e.tensor_scalar(out=kf[:, :], in0=tmp[:, :], scalar1=1.0 / TWO_PI, scalar2=None, op0=ALU.mult), reads=[tmp], writes=[kf])
        k.op("dve", lambda e: e.tensor_copy(out=ki[:, :], in_=kf[:, :]), reads=[kf], writes=[ki])
        k.op("dve", lambda e: e.tensor_copy(out=kf[:, :], in_=ki[:, :]), reads=[ki], writes=[kf])
        k.op("dve", lambda e: e.scalar_tensor_tensor(out=tmp[:, :], in0=kf[:, :], scalar=-TWO_PI, in1=tmp[:, :], op0=ALU.mult, op1=ALU.add), reads=[kf, tmp], writes=[tmp])
        k.op("dve", lambda e: e.tensor_scalar(out=tmp2[:, :], in0=tmp[:, :], scalar1=3.141592653589793, scalar2=None, op0=ALU.is_gt), reads=[tmp], writes=[tmp2])
        k.op("dve", lambda e: e.scalar_tensor_tensor(out=tmp[:, :], in0=tmp2[:, :], scalar=-TWO_PI, in1=tmp[:, :], op0=ALU.mult, op1=ALU.add), reads=[tmp2, tmp], writes=[tmp])
        k.op("dve", lambda e: e.tensor_scalar(out=tmp2[:, :], in0=tmp[:, :], scalar1=-3.141592653589793, scalar2=None, op0=ALU.is_lt), reads=[tmp], writes=[tmp2])
        k.op("dve", lambda e: e.scalar_tensor_tensor(out=tmp[:, :], in0=tmp2[:, :], scalar=TWO_PI, in1=tmp[:, :], op0=ALU.mult, op1=ALU.add), reads=[tmp2, tmp], writes=[tmp])
        k.op("act", lambda e: e.activation(out=dst[:, :], in_=tmp[:, :], func=AF.Sin), reads=[tmp], writes=[dst])
    sin_of(sn, 0.0)
    sin_of(cs, 1.5707963267948966)
    k.op("dve", lambda e: e.tensor_tensor(out=abr[:, :], in0=rmag[:, :], in1=cs[:, :], op=ALU.mult), reads=[rmag, cs], writes=[abr])
    k.op("dve", lambda e: e.tensor_tensor(out=abi[:, :], in0=rmag[:, :], in1=sn[:, :], op=ALU.mult), reads=[rmag, sn], writes=[abi])
    k.op("dve", lambda e: e.tensor_scalar(out=abr[:, :], in0=abr[:, :], scalar1=-1.0, scalar2=None, op0=ALU.add), reads=[abr], writes=[abr])
    k.op("dve", lambda e: e.tensor_tensor(out=den[:, :], in0=are, in1=are, op=ALU.mult), reads=[P], writes=[den])
    k.op("dve", lambda e: e.tensor_tensor(out=tmp[:, :], in0=aim, in1=aim, op=ALU.mult), reads=[P], writes=[tmp])
    k.op("dve", lambda e: e.tensor_tensor(out=den[:, :], in0=den[:, :], in1=tmp[:, :], op=ALU.add), reads=[den, tmp], writes=[den])
    k.op("dve", lambda e: e.reciprocal(out=den[:, :], in_=den[:, :]), reads=[den], writes=[den])
    k.op("dve", lambda e: e.tensor_tensor(out=fre[:, :], in0=abr[:, :], in1=are, op=ALU.mult), reads=[abr, P], writes=[fre])
    k.op("dve", lambda e: e.tensor_tensor(out=tmp[:, :], in0=abi[:, :], in1=aim, op=ALU.mult), reads=[abi, P], writes=[tmp])
    k.op("dve", lambda e: e.tensor_tensor(out=fre[:, :], in0=fre[:, :], in1=tmp[:, :], op=ALU.add), reads=[fre, tmp], writes=[fre])
    k.op("dve", lambda e: e.tensor_tensor(out=fre[:, :], in0=fre[:, :], in1=den[:, :], op=ALU.mult), reads=[fre, den], writes=[fre])
    k.op("dve", lambda e: e.tensor_tensor(out=fim[:, :], in0=abi[:, :], in1=are, op=ALU.mult), reads=[abi, P], writes=[fim])
    k.op("dve", lambda e: e.tensor_tensor(out=tmp[:, :], in0=abr[:, :], in1=aim, op=ALU.mult), reads=[abr, P], writes=[tmp])
    k.op("dve", lambda e: e.tensor_tensor(out=fim[:, :], in0=fim[:, :], in1=tmp[:, :], op=ALU.subtract), reads=[fim, tmp], writes=[fim])
    k.op("dve", lambda e: e.tensor_tensor(out=fim[:, :], in0=fim[:, :], in1=den[:, :], op=ALU.mult), reads=[fim, den], writes=[fim])
    nsn = tl("nsn")
    k.op("dve", lambda e: e.tensor_scalar(out=nsn[:, :], in0=sn[:, :], scalar1=-1.0, scalar2=None, op0=ALU.mult), reads=[sn], writes=[nsn])
    TC = 512
    Rre = k.sb("s5Rre", [128, TC], F32); Rim = k.sb("s5Rim", [128, TC], F32)
    Ere = k.sb("s5Ere", [128, TC], F32); Eim = k.sb("s5Eim", [128, TC], F32)
    sc1 = k.sb("s5sc1", [128, TC], F32)
    Sre = k.sb("s5Sre", [128, T], F32); Sim = k.sb("s5Sim", [128, T], F32)
    Sreb = [k.sb("s5Sreb%d" % i, [128, T], BF16) for i in range(2)]; Simb = [k.sb("s5Simb%d" % i, [128, T], BF16) for i in range(2)]
    wre = k.sb("s5wre", [128, TC], F32); wim = k.sb("s5wim", [128, TC], F32)
    zre = k.sb("s5zre", [128, TC], F32); zim = k.sb("s5zim", [128, TC], F32)
    t1 = k.sb("s5t1", [128, TC], F32); t2 = k.sb("s5t2", [128, TC], F32)
    t3 = k.sb("s5t3", [128, TC], F32); t4 = k.sb("s5t4", [128, TC], F32)
    ire = k.sb("s5ire", [128, 1], F32); iim = k.sb("s5iim", [128, 1], F32)
    pbr = [k.ps("s5pbr%d" % i, [128, 512]) for i in range(2)]
    pbi = [k.ps("s5pbi%d" % i, [128, 512]) for i in range(2)]
    py = [k.ps("s5py%d" % i, [128, 512]) for i in range(2)]
    yst = Stage(k, 2, "s5yo", [128, 512], BF16)
    ust5 = Stage(k, 2, "s5uu", [128, 512], F32)
    ya = k.sb("s5ya", [128, 512], F32); yb = k.sb("s5yb", [128, 512], F32); yc = k.sb("s5yc", [128, 512], F32)
    ic = 0
    for st in range(NST):
        cc, jh, jl = st // 4, (st // 2) % 2, st % 2
        pr = slice(64 * jh, 64 * jh + 64)
        for d in range(2):
            col = d * NST + st
            k.op("dve", lambda e: e.tensor_copy(out=Rre[:, 0:1], in_=cs[:, col:col + 1]), reads=[cs], writes=[Rre])
            k.op("dve", lambda e: e.tensor_copy(out=Rim[:, 0:1], in_=nsn[:, col:col + 1]), reads=[nsn], writes=[Rim])
            m = 1
            while m < TC:
                _cmul_sc(k, "pool", Rre[:, m:2 * m], Rim[:, m:2 * m], Rre[:, 0:m], Rim[:, 0:m], Rre[:, m - 1:m], Rim[:, m - 1:m], sc1[:, 0:m], [Rre, Rim], [Rre, Rim, sc1])
                m *= 2
            _cmul_sc(k, "pool", Ere[:, :], Eim[:, :], Rre[:, :], Rim[:, :], fre[:, col:col + 1], fim[:, col:col + 1], sc1[:, :], [Rre, Rim, fre, fim], [Ere, Eim, sc1])
            k.op("dve", lambda e: e.memset(ire[:, :], 0.0), writes=[ire])
            k.op("dve", lambda e: e.memset(iim[:, :], 0.0), writes=[iim])
            for (t0, n) in chunks[d]:
                pb_r, pb_i = pbr[ic % 2], pbi[ic % 2]; ic += 1
                k.op("pe", lambda e: e.matmul(out=pb_r[:, :n], lhsT=Bre[pr, st, :], rhs=ub[pr, cc, t0:t0 + n], start=True, stop=True), reads=[Bre, ub], writes=[pb_r])
                k.op("pe", lambda e: e.matmul(out=pb_i[:, :n], lhsT=Bim[pr, st, :], rhs=ub[pr, cc, t0:t0 + n], start=True, stop=True), reads=[Bim, ub], writes=[pb_i])
                if d == 0:
                    br, bi = pb_r[:, 0:n], pb_i[:, 0:n]
                else:
                    br, bi = pb_r[:, n - 1::-1] if n > 1 else pb_r[:, 0:1], pb_i[:, n - 1::-1]
                k.op("dve", lambda e: e.tensor_tensor(out=t1[:, :n], in0=Ere[:, :n], in1=br, op=ALU.mult), reads=[Ere, pb_r], writes=[t1])
                k.op("dve", lambda e: e.tensor_tensor(out=t2[:, :n], in0=Eim[:, :n], in1=bi, op=ALU.mult), reads=[Eim, pb_i], writes=[t2])
                k.op("dve", lambda e: e.tensor_tensor(out=t3[:, :n], in0=Ere[:, :n], in1=bi, op=ALU.mult), reads=[Ere, pb_i], writes=[t3])
                k.op("dve", lambda e: e.tensor_tensor(out=t4[:, :n], in0=Eim[:, :n], in1=br, op=ALU.mult), reads=[Eim, pb_r], writes=[t4])
                k.op("pool", lambda e: e.tensor_tensor(out=wre[:, :n], in0=t1[:, :n], in1=t2[:, :n], op=ALU.subtract), reads=[t1, t2], writes=[wre])
                k.op("pool", lambda e: e.tensor_tensor(out=wim[:, :n], in0=t3[:, :n], in1=t4[:, :n], op=ALU.add), reads=[t3, t4], writes=[wim])
                rb = rmag[:, col:col + 1].broadcast_to([128, n])
                k.op("dve", lambda e: e.tensor_tensor_scan(out=zre[:, :n], data0=rb, data1=wre[:, :n], initial=ire[:, 0:1], op0=ALU.mult, op1=ALU.add), reads=[rmag, wre, ire], writes=[zre])
                k.op("dve", lambda e: e.tensor_tensor_scan(out=zim[:, :n], data0=rb, data1=wim[:, :n], initial=iim[:, 0:1], op0=ALU.mult, op1=ALU.add), reads=[rmag, wim, iim], writes=[zim])
                k.op("pool", lambda e: e.tensor_tensor(out=t1[:, :n], in0=Rre[:, :n], in1=zre[:, :n], op=ALU.mult), reads=[Rre, zre], writes=[t1])
                k.op("pool", lambda e: e.tensor_tensor(out=t2[:, :n], in0=Rim[:, :n], in1=zim[:, :n], op=ALU.mult), reads=[Rim, zim], writes=[t2])
                k.op("pool", lambda e: e.tensor_tensor(out=t3[:, :n], in0=Rre[:, :n], in1=zim[:, :n], op=ALU.mult), reads=[Rre, zim], writes=[t3])
                k.op("pool", lambda e: e.tensor_tensor(out=t4[:, :n], in0=Rim[:, :n], in1=zre[:, :n], op=ALU.mult), reads=[Rim, zre], writes=[t4])
                k.op("pool", lambda e: e.tensor_tensor(out=t1[:, :n], in0=t1[:, :n], in1=t2[:, :n], op=ALU.add), reads=[t1, t2], writes=[t1])
                k.op("pool", lambda e: e.tensor_tensor(out=t3[:, :n], in0=t3[:, :n], in1=t4[:, :n], op=ALU.subtract), reads=[t3, t4], writes=[t3])
                k.op("dve", lambda e: e.tensor_copy(out=ire[:, :], in_=t1[:, n - 1:n]), reads=[t1], writes=[ire])
                k.op("dve", lambda e: e.tensor_copy(out=iim[:, :], in_=t3[:, n - 1:n]), reads=[t3], writes=[iim])
                if d == 0:
                    k.op("act", lambda e: e.copy(out=Sre[:, t0:t0 + n], in_=t1[:, :n]), reads=[t1], writes=[Sre])
                    k.op("act", lambda e: e.copy(out=Sim[:, t0:t0 + n], in_=t3[:, :n]), reads=[t3], writes=[Sim])
                else:
                    k.op("pool", lambda e: e.tensor_tensor(out=Sre[:, t0:t0 + n], in0=Sre[:, t0:t0 + n], in1=t1[:, n - 1::-1], op=ALU.add), reads=[Sre, t1], writes=[Sre])
                    k.op("pool", lambda e: e.tensor_tensor(out=Sim[:, t0:t0 + n], in0=Sim[:, t0:t0 + n], in1=t3[:, n - 1::-1], op=ALU.add), reads=[Sim, t3], writes=[Sim])
        k.op("act", lambda e: e.copy(out=Sreb[jl][:, :], in_=Sre[:, :]), reads=[Sre], writes=[Sreb[jl]])
        k.op("act", lambda e: e.copy(out=Simb[jl][:, :], in_=Sim[:, :]), reads=[Sim], writes=[Simb[jl]])
        if jl == 0:
            continue
        for a in range(0, T, 512):
            n = min(512, T - a)
            p = py[(a // 512) % 2]
            for q in range(2):
                stq = st - 1 + q
                k.op("pe", lambda e: e.matmul(out=p[pr, :n], lhsT=Cre[:, stq, :], rhs=Sreb[q][:, a:a + n], start=(q == 0), stop=False), reads=[Cre, Sreb[q]], writes=[p], sig=False)
                k.op("pe", lambda e: e.matmul(out=p[pr, :n], lhsT=Cimn[:, stq, :], rhs=Simb[q][:, a:a + n], start=False, stop=(q == 1)), reads=[Cimn, Simb[q]], writes=[p], sig=(q == 1))
            uu, usem_ = ust5.get()
            k.dma("act", usem_, uu[pr, :n], uT[cc * 128 + 64 * jh:cc * 128 + 64 * jh + 64, a:a + n], dst=uu, src=uT)
            k.op("dve", lambda e: e.scalar_tensor_tensor(out=ya[pr, :n], in0=uu[pr, :n], scalar=dsk[pr, cc:cc + 1], in1=p[pr, :n], op0=ALU.mult, op1=ALU.add),
                 reads=[uu, dsk, p], writes=[ya])
            o, osem = yst.get()
            gelu_tanh(k, ya, yb, yc, o, pr, n)
            r0 = out_row0 + 128 * cc + 64 * jh
            if isinstance(outT, list):
                for a2 in range(a, a + n, 128):
                    op_ = outT[a2 // 128]
                    k.dma("sp", osem, op_[r0:r0 + 64, :], o[pr, a2 - a:a2 - a + 128], src=o, dst=op_)
            else:
                k.dma("sp", osem, outT[r0:r0 + 64, a:a + n], o[pr, :n], src=o, dst=outT)


def gelu_tanh(k, y, b1, b2, o, pr, n, eng2="pool"):
    k.op("dve", lambda e: e.tensor_tensor(out=b1[pr, :n], in0=y[pr, :n], in1=y[pr, :n], op=ALU.mult), reads=[y], writes=[b1])
    k.op("dve", lambda e: e.tensor_scalar(out=b1[pr, :n], in0=b1[pr, :n], scalar1=0.044715, scalar2=1.0, op0=ALU.mult, op1=ALU.add), reads=[b1], writes=[b1])
    k.op(eng2, lambda e: e.tensor_tensor(out=b1[pr, :n], in0=b1[pr, :n], in1=y[pr, :n], op=ALU.mult), reads=[b1, y], writes=[b1])
    k.op("act", lambda e: e.activation(out=b2[pr, :n], in_=b1[pr, :n], func=AF.Sigmoid, scale=1.5957691216057308), reads=[b1], writes=[b2])
    k.op(eng2, lambda e: e.tensor_tensor(out=o[pr, :n], in0=b2[pr, :n], in1=y[pr, :n], op=ALU.mult), reads=[b2, y], writes=[o])


def s5_host_layout(a_re, a_im, log_dt, b_re, b_im, c_re, c_im, d_skip):
    NG = a_re.shape[1]; NST = NG // 2; NCC = NG * 16 // 128
    par = np.zeros((128, 3, 2 * NST), np.float32)
    Bt = np.zeros((2, 128, NST, 128), np.float32)
    Ct = np.zeros((2, 128, NST, 64), np.float32)
    for st in range(NST):
        jh, jl = (st // 2) % 2, st % 2
        for gl in range(2):
            g = 2 * st + gl
            for d in range(2):
                par[64 * gl:64 * gl + 64, 0, d * NST + st] = a_re[d, g]
                par[64 * gl:64 * gl + 64, 1, d * NST + st] = a_im[d, g]
                par[64 * gl:64 * gl + 64, 2, d * NST + st] = log_dt[d, g]
            for ri, (bb, cmat) in enumerate(((b_re, c_re), (b_im, c_im))):
                r0 = 64 * jh + 32 * jl + 16 * gl
                Bt[ri, r0:r0 + 16, st, 64 * gl:64 * gl + 64] = bb[g].T
                Ct[ri, 64 * gl:64 * gl + 64, st, 32 * jl + 16 * gl:32 * jl + 16 * gl + 16] = cmat[g].T
    dsk = np.ascontiguousarray(d_skip.reshape(NCC, 128).T).astype(np.float32)
    return par, Bt[0], Bt[1], Ct[0], Ct[1], dsk


def na_phase(k, C, QT, KT, V, Tg, cmask_d, flags_d, attnT, NHEADS, scale):
    NKT = 2816
    s0 = k.dsem()
    cm = k.sb("nacm", [64, 64], F32); fl = k.sb("nafl", [128, 4], F32)
    k.dma("sp", s0, cm[:, :], cmask_d[:, :], dst=cm)
    k.dma("sp", s0, fl[:, :], flags_d[:, :], dst=fl)
    cm.w = list(fl.w)
    nb = 2
    qh = [k.sb("naq%d" % i, [128, 2048], BF16) for i in range(nb)]
    kh = [k.sb("nak%d" % i, [128, NKT], BF16) for i in range(nb)]
    ve = [k.sb("nave%d" % i, [128, 22, 128], BF16) for i in range(nb)]
    vo = [k.sb("navo%d" % i, [128, 21, 128], BF16) for i in range(nb)]
    th = [k.sb("nath%d" % i, [64, 15, 64], F32) for i in range(nb)]
    ao = [k.sb("naao%d" % i, [128, 2048], BF16) for i in range(nb)]
    lsem = [k.dsem() for _ in range(nb)]
    osem = [k.dsem() for _ in range(nb)]
    ps_s = [k.ps("naps%d" % i, [128, 512]) for i in range(2)]
    ps_c = k.ps("napc", [128, 512])
    ps_t = [k.ps("napt%d" % i, [128, 1024], BF16) for i in range(2)]
    ps_o = [k.ps("napo%d" % i, [128, 512]) for i in range(2)]
    ssb = [k.sb("nas%d" % i, [64, 768], F32) for i in range(2)]
    pnb = [k.sb("napn%d" % i, [64, 768], BF16) for i in range(2)]
    pT = [k.sb("napT%d" % i, [128, 6, 64], BF16) for i in range(2)]
    mx = [k.sb("namx%d" % i, [64, 1], F32) for i in range(2)]
    sm = [k.sb("nasm%d" % i, [64, 1], F32) for i in range(2)]
    tmpo = k.sb("natmp", [128, 64], F32)
    items = []
    for r in range(32):
        if r < 4:
            items.append((r, 0, 7 - r, ("first", 0)))
            items.append((r, r - 4, 3, ("second", 1)))
        elif r >= 29:
            items.append((r, 24, 24 - r + 7, ("first", 2)))
            items.append((r, r - 4, 3, ("second", 3)))
        else:
            items.append((r, r - 4, 3, None))
    it = 0
    for h in range(NHEADS):
        b = h % nb
        ls = lsem[b]
        k.dma("sp", ls, qh[b][:, :], QT[h * 128:(h + 1) * 128, :], dst=qh[b], src=QT)
        k.dma("sp", ls, kh[b][:, :], KT[h * 128:(h + 1) * 128, :], dst=kh[b], src=KT)
        k.dma("act", ls, ve[b][:, :, :], V[0:2816, h * 128:(h + 1) * 128].rearrange("(t p) d -> p t d", p=128), dst=ve[b], src=V)
        k.dma("act", ls, vo[b][:, :, :], V[64:64 + 2688, h * 128:(h + 1) * 128].rearrange("(t p) d -> p t d", p=128), dst=vo[b], src=V)
        k.dma("sp", ls, th[b][:, :, :], Tg[h], dst=th[b])
        for t in (qh[b], kh[b], ve[b], vo[b]):
            t.w = list(th[b].w)
        k.op("pool", lambda e: e.tensor_tensor(out=th[b][:, :, :], in0=th[b][:, :, :], in1=cm[:, :].unsqueeze(1).broadcast_to([64, 15, 64]), op=ALU.add),
             reads=[th[b], cm], writes=[th[b]])
        for (r, b0, ri0, blend) in items:
            j = it % 2; it += 1
            kc0 = (b0 + 4) * 64
            qv = qh[b][:, r * 64:(r + 1) * 64]
            k.op("pe", lambda e: e.matmul(out=ps_s[j][:64, :512], lhsT=qv, rhs=kh[b][:, kc0:kc0 + 512], start=True, stop=True), reads=[qh[b], kh[b]], writes=[ps_s[j]])
            k.op("pe", lambda e: e.matmul(out=ps_c[:64, :256], lhsT=qv, rhs=kh[b][:, 2560:2816], start=True, stop=True), reads=[qh[b], kh[b]], writes=[ps_c])
            s = ssb[j]
            k.op("dve", lambda e: e.scalar_tensor_tensor(out=s[:, 0:512], in0=ps_s[j][:64, :512], scalar=scale, in1=th[b][:, ri0:ri0 + 8, :].rearrange("p a b -> p (a b)"),
                                                        op0=ALU.mult, op1=ALU.add), reads=[ps_s[j], th[b]], writes=[s])
            k.op("act", lambda e: e.activation(out=s[:, 512:768], in_=ps_c[:64, :256], func=AF.Copy, scale=scale), reads=[ps_c, s], writes=[s])
            k.op("dve", lambda e: e.tensor_reduce(out=mx[j][:, :], in_=s[:, :], axis=AX.X, op=ALU.max, negate=True), reads=[s], writes=[mx[j]])
            k.op("act", lambda e: e.activation(out=s[:, :], in_=s[:, :], func=AF.Exp, bias=mx[j][:, 0:1], scale=1.0, accum_out=sm[j][:, 0:1]), reads=[s, mx[j]], writes=[s, sm[j]])
            k.op("dve", lambda e: e.reciprocal(out=sm[j][:, :], in_=sm[j][:, :]), reads=[sm[j]], writes=[sm[j]])
            k.op("dve", lambda e: e.tensor_scalar(out=pnb[j][:, :], in0=s[:, :], scalar1=sm[j][:, 0:1], scalar2=None, op0=ALU.mult), reads=[s, sm[j]], writes=[pnb[j]])
            for kk in range(6):
                k.op("pe", lambda e: e.transpose(out=ps_t[j][:, kk * 64:(kk + 1) * 64], in_=pnb[j][:, kk * 128:(kk + 1) * 128], identity=C["ident_bf"][:64, :64]),
                     reads=[pnb[j], C["ident_bf"]], writes=[ps_t[j]], sig=(kk == 5))
            k.op("act", lambda e: e.copy(out=pT[j][:, :, :], in_=ps_t[j][:, 0:384].rearrange("p (a b) -> p a b", a=6)), reads=[ps_t[j]], writes=[pT[j]])
            par = (b0 + 4) % 2
            for kk in range(6):
                if kk < 4:
                    vt = (ve[b][:, (b0 + 4) // 2 + kk, :], ve[b]) if par == 0 else (vo[b][:, (b0 + 3) // 2 + kk, :], vo[b])
                else:
                    vt = (ve[b][:, 20 + (kk - 4), :], ve[b])
                k.op("pe", lambda e: e.matmul(out=ps_o[j][:, :64], lhsT=vt[0], rhs=pT[j][:, kk, :], start=(kk == 0), stop=(kk == 5)), reads=[vt[1], pT[j]], writes=[ps_o[j]], sig=(kk == 5))
            dst = ao[b][:, r * 64:(r + 1) * 64]
            if blend is None:
                k.op("act", lambda e: e.copy(out=dst, in_=ps_o[j][:, :64]), reads=[ps_o[j]], writes=[ao[b]])
            elif blend[0] == "first":
                k.op("act", lambda e: e.activation(out=tmpo[:, :], in_=ps_o[j][:, :64], func=AF.Copy, scale=fl[:, blend[1]:blend[1] + 1]), reads=[ps_o[j], fl], writes=[tmpo])
            else:
                k.op("dve", lambda e: e.scalar_tensor_tensor(out=dst, in0=ps_o[j][:, :64], scalar=fl[:, blend[1]:blend[1] + 1], in1=tmpo[:, :], op0=ALU.mult, op1=ALU.add),
                     reads=[ps_o[j], fl, tmpo], writes=[ao[b]])
        k.dma("sp", osem[b], attnT[h * 128:(h + 1) * 128, :], ao[b][:, :], src=ao[b], dst=attnT)


def na_host_tables(rpb):
    col = np.arange(64)
    ci = np.clip(col[None, :] - col[:, None], -15, 15) + 15
    Tg = rpb[:, :, ci]
    Tg = np.ascontiguousarray(Tg.transpose(0, 2, 1, 3)).astype(np.float32)
    cs = np.clip(col - 8, 0, 64 - 16)
    valid = (col[None, :] >= cs[:, None]) & (col[None, :] < cs[:, None] + 16)
    cmask = np.where(valid, 0.0, -30000.0).astype(np.float32)
    return Tg, cmask


D = 4096
KC = 32
TCX = 128
TLAT = 2048
TOWN = TCX + TLAT
TP = 2 * TOWN
DFF = 11008
NJC = DFF // 128
PAIRS = [[0, 1], [2, 3], [4, 5], [6, 7]]
PARITY = [[0, 2, 4, 6], [1, 3, 5, 7]]
ALL8 = [list(range(8))]
WIN_N = 5632
WIN_U0 = 5120


def mod_phase(k, C, IN, MV):
    modsh = k.dram("modsh", [10, 3072], F32)
    modall = k.dram("modall", [80, 3072], F32)
    modpair = k.dram("modpair", [20, 3072], F32)
    with k.phase():
        s0 = k.dsem()
        cond = k.sb("cond", [128, 32, 5], F32); sc = k.sb("scond", [128, 32, 5], F32)
        mb = k.sb("mb", [5, 6144], F32)
        k.dma("sp", s0, cond[:, :, :], IN["condT"][:, :, :], dst=cond)
        k.dma("sp", s0, mb[:, :], IN["modb"][0:1, :].partition_broadcast(5), dst=mb)
        cond.w = list(mb.w)
        k.op("act", lambda e: e.activation(out=sc[:, :, :], in_=cond[:, :, :], func=AF.Silu), reads=[cond], writes=[sc])
        wb = [k.sb("modw%d" % i, [128, 32, 512], F32) for i in range(2)]
        ws = [k.dsem() for _ in range(2)]
        stg = [k.sb("modst%d" % i, [5, 3072], F32) for i in range(2)]
        ss = [k.dsem() for _ in range(2)]
        pp = PsPool(k, 2, "modps")
        it = 0
        for l in range(2):
            for blk in range(6):
                w = wb[it % 2]
                k.dma("sp" if it % 2 == 0 else "act", ws[it % 2], w[:, :, :], IN["modw"][l, :, blk * 512:(blk + 1) * 512].rearrange("(kc p) n -> p kc n", p=128), dst=w)
                it += 1
                ps = pp.get()
                for kc in range(32):
                    k.op("pe", lambda e: e.matmul(out=ps[:5, :512], lhsT=sc[:, kc, :], rhs=w[:, kc, :], start=(kc == 0), stop=(kc == 31)), reads=[sc, w], writes=[ps], sig=(kc == 31))
                k.op("dve", lambda e: e.tensor_tensor(out=stg[l][:5, blk * 512:(blk + 1) * 512], in0=ps[:5, :512], in1=mb[:5, l * 3072 + blk * 512:l * 3072 + (blk + 1) * 512], op=ALU.add),
                     reads=[ps, mb], writes=[stg[l]])
            k.dma("sp", ss[l], modsh[l * 5:(l + 1) * 5, :], stg[l][:5, :], src=stg[l], dst=modsh)
    k.allgather(k.dsem(), modpair, modsh, PAIRS)
    k.allgather(k.dsem(), modall, modpair, PARITY)
    with k.phase():
        s0 = k.dsem()
        oh = k.sb("oh", [128, 5], F32)
        k.dma("sp", s0, oh[:, :], IN["onehot"][:, :], dst=oh)
        gm = k.sb("gmix", [128, 2, 32], F32); gf = k.sb("gffn", [128, 2, 32], F32)
        k.dma("sp", s0, gm[:, :, :], IN["gmix"][:, :, :], dst=gm)
        k.dma("sp", s0, gf[:, :, :], IN["gffn"][:, :, :], dst=gf)
        oh.w = list(gf.w); gm.w = list(gf.w)
        G = [k.sb("modG%d" % i, [32, 5, 128], F32) for i in range(2)]
        gs = [k.dsem() for _ in range(2)]
        sel = [k.sb("modsel%d" % i, [32, 128], F32) for i in range(2)]
        pp = PsPool(k, 2, "modps2")
        it = 0
        for l in range(2):
            for which in range(6):
                g = G[it % 2]; it += 1
                for r in range(8):
                    k.dma("sp", gs[(it - 1) % 2], g[4 * r:4 * r + 4, :, :],
                          modall[r * 10 + l * 5:r * 10 + l * 5 + 5, which * 512:(which + 1) * 512].rearrange("r (q p) -> q r p", p=128), dst=g, src=modall)
                sl = sel[(it - 1) % 2]
                k.op("dve", lambda e: e.tensor_scalar(out=sl[:, :], in0=g[:, 0, :], scalar1=oh[:32, 0:1], scalar2=None, op0=ALU.mult), reads=[g, oh], writes=[sl])
                for r in range(1, 4):
                    k.op("dve", lambda e: e.scalar_tensor_tensor(out=sl[:, :], in0=g[:, r, :], scalar=oh[:32, r:r + 1], in1=sl[:, :], op0=ALU.mult, op1=ALU.add), reads=[g, oh, sl], writes=[sl])
                for si, src in enumerate((sl[:, :], g[:, 4, :])):
                    ps = pp.get()
                    k.op("pe", lambda e: e.transpose(out=ps[:, 0:32], in_=src, identity=C["ident"][:32, :32]), reads=[sl, g, C["ident"]], writes=[ps])
                    dst = MV[l, which, si]
                    k.op("act", lambda e: e.copy(out=dst[:, :], in_=ps[:, 0:32]), reads=[ps], writes=[dst])
        for l in range(2):
            for si in range(2):
                for (nm, which, gt) in (("gscm", 1, gm), ("gscf", 4, gf)):
                    d = MV[l, nm, si]
                    k.op("dve", lambda e: e.tensor_scalar(out=d[:, :], in0=MV[l, which, si][:, :], scalar1=1.0, scalar2=None, op0=ALU.add), reads=[MV[l, which, si]], writes=[d])
                    k.op("dve", lambda e: e.tensor_tensor(out=d[:, :], in0=d[:, :], in1=gt[:, l, :], op=ALU.mult), reads=[d, gt], writes=[d])


def blend_cols(k, dst, jobs, fl, ca, cb, R):
    nb = 3
    CW = 128
    ta = [k.sb("bla%d" % i, [128, CW], BF16) for i in range(nb)]; tb = [k.sb("blb%d" % i, [128, CW], BF16) for i in range(nb)]
    tf = [k.sb("blf%d" % i, [128, CW], F32) for i in range(nb)]; to = [k.sb("blo%d" % i, [128, CW], BF16) for i in range(nb)]
    si = [k.dsem() for _ in range(nb)]; so = [k.dsem() for _ in range(nb)]
    i = 0
    for (dcol0, A, B) in jobs:
        for r0 in range(0, R, 128):
            b = i % nb; i += 1
            k.dma("sp", si[b], ta[b][:, :], A[r0:r0 + 128, :], dst=ta[b], src=A)
            k.dma("act", si[b], tb[b][:, :], B[r0:r0 + 128, :], dst=tb[b], src=B)
            ta[b].w = list(tb[b].w)
            k.op("pool", lambda e: e.tensor_scalar(out=tf[b][:, :], in0=ta[b][:, :], scalar1=fl[:, ca:ca + 1], scalar2=None, op0=ALU.mult), reads=[ta[b], fl], writes=[tf[b]])
            k.op("dve", lambda e: e.scalar_tensor_tensor(out=to[b][:, :], in0=tb[b][:, :], scalar=fl[:, cb:cb + 1], in1=tf[b][:, :], op0=ALU.mult, op1=ALU.add),
                 reads=[tb[b], fl, tf[b]], writes=[to[b]])
            k.dma("sp", so[b], dst[r0:r0 + 128, dcol0:dcol0 + CW], to[b][:, :], src=to[b], dst=dst)


def scale_cols(k, dst, dcol0, src, scol0, fl, ca, R, ncols):
    nb = 2
    ta = [k.sb("sca%d" % i, [128, ncols], BF16) for i in range(nb)]; to = [k.sb("sco%d" % i, [128, ncols], BF16) for i in range(nb)]
    si = [k.dsem() for _ in range(nb)]; so = [k.dsem() for _ in range(nb)]
    i = 0
    for r0 in range(0, R, 128):
        b = i % nb; i += 1
        k.dma("sp", si[b], ta[b][:, :], src[r0:r0 + 128, scol0:scol0 + ncols], dst=ta[b], src=src)
        k.op("dve", lambda e: e.tensor_scalar(out=to[b][:, :], in0=ta[b][:, :], scalar1=fl[:, ca:ca + 1], scalar2=None, op0=ALU.mult), reads=[ta[b], fl], writes=[to[b]])
        k.dma("act", so[b], dst[r0:r0 + 128, dcol0:dcol0 + ncols], to[b][:, :], src=to[b], dst=dst)


def resid_epilogue(k, xsrc, xdst, gate, nstage=3, tcol0=0):
    xi = Stage(k, nstage, "rxi", [128, 512], F32)
    xo = Stage(k, nstage, "rxo", [128, 512], F32)

    def epi(ps, m, n, f0, t0, **kw):
        fc = f0 // 128
        t0 = t0 + tcol0
        si = 1 if t0 < TCX else 0
        a, asem = xi.get()
        k.dma("act", asem, a[:m, :n], xsrc[f0:f0 + m, t0:t0 + n], dst=a, src=xsrc)
        o, osem = xo.get()
        k.op("dve", lambda e: e.scalar_tensor_tensor(out=o[:m, :n], in0=ps[:m, :n], scalar=gate[si][:m, fc:fc + 1], in1=a[:m, :n], op0=ALU.mult, op1=ALU.add),
             reads=[ps, gate[si], a], writes=[o])
        k.dma("act", osem, xdst[f0:f0 + m, t0:t0 + n], o[:m, :n], src=o, dst=xdst)
    return epi


def rope_phase(k, C, q_tm, k_tm, ropeT, qT, kT, ktm_r, NH):
    W = NH * 256
    nb = 2
    xq = [k.sb("rq%d" % i, [128, NH, 2, 2, 64], F32) for i in range(nb)]
    xk = [k.sb("rk%d" % i, [128, NH, 2, 2, 64], F32) for i in range(nb)]
    tb = [k.sb("rt%d" % i, [128, 2, 2, 64], F32) for i in range(nb)]
    ls = [k.dsem() for _ in range(nb)]
    rq = [k.sb("rrq%d" % i, [128, NH, 2, 2, 64], F32) for i in range(nb)]
    rk = [k.sb("rrk%d" % i, [128, NH, 2, 2, 64], F32) for i in range(nb)]
    t1 = k.sb("rt1", [128, NH, 2, 64], F32); t2 = k.sb("rt2", [128, NH, 2, 64], F32)
    oq = [k.sb("roq%d" % i, [128, NH * 2, 128], F32) for i in range(nb)]
    ok = [k.sb("rok%d" % i, [128, NH * 2, 128], F32) for i in range(nb)]
    sq = [k.dsem() for _ in range(nb)]; sk = [k.dsem() for _ in range(nb)]; sr = [k.dsem() for _ in range(nb)]
    pp = PsPool(k, 4, "rps")
    flat = "p a b c d -> p (a b c d)"
    for ci in range(TP // 128):
        b = ci % nb
        t0 = ci * 128
        k.dma("sp", ls[b], xq[b][:].rearrange(flat), q_tm[t0:t0 + 128, :], dst=xq[b], src=q_tm)
        k.dma("act", ls[b], xk[b][:].rearrange(flat), k_tm[t0:t0 + 128, :], dst=xk[b], src=k_tm)
        k.dma("sp", ls[b], tb[b][:].rearrange("p a b c -> p (a b c)"), ropeT[t0:t0 + 128, :], dst=tb[b], src=ropeT)
        xq[b].w = list(tb[b].w); xk[b].w = list(tb[b].w)
        cosv = tb[b][:, 0, :, :].unsqueeze(1).broadcast_to([128, NH, 2, 64])
        sinv = tb[b][:, 1, :, :].unsqueeze(1).broadcast_to([128, NH, 2, 64])
        for (x, r) in ((xq[b], rq[b]), (xk[b], rk[b])):
            x1, x2 = x[:, :, :, 0, :], x[:, :, :, 1, :]
            k.op("dve", lambda e: e.tensor_tensor(out=t1[:], in0=x1, in1=cosv, op=ALU.mult), reads=[x, tb[b]], writes=[t1])
            k.op("pool", lambda e: e.tensor_tensor(out=t2[:], in0=x2, in1=sinv, op=ALU.mult), reads=[x, tb[b]], writes=[t2])
            k.op("dve", lambda e: e.tensor_tensor(out=r[:, :, :, 0, :], in0=t1[:], in1=t2[:], op=ALU.subtract), reads=[t1, t2], writes=[r])
            k.op("dve", lambda e: e.tensor_tensor(out=t1[:], in0=x1, in1=sinv, op=ALU.mult), reads=[x, tb[b], r], writes=[t1])
            k.op("pool", lambda e: e.tensor_tensor(out=t2[:], in0=x2, in1=cosv, op=ALU.mult), reads=[x, tb[b], r], writes=[t2])
            k.op("dve", lambda e: e.tensor_tensor(out=r[:, :, :, 1, :], in0=t1[:], in1=t2[:], op=ALU.add), reads=[t1, t2], writes=[r])
        k.dma("act", sr[b], ktm_r[t0:t0 + 128, :], rk[b][:].rearrange(flat), src=rk[b], dst=ktm_r)
        for (r, o, osem, dstT, scl) in ((rq[b], oq[b], sq[b], qT, 1.0 / 16.0), (rk[b], ok[b], sk[b], kT, 1.0)):
            rf = r[:].rearrange(flat)
            for half in range(NH * 2 // 3):
                ps = pp.get()
                for j in range(3):
                    c = half * 3 + j
                    k.op("pe", lambda e: e.transpose(out=ps[:, j * 128:(j + 1) * 128], in_=rf[:, c * 128:(c + 1) * 128], identity=C["ident"][:, :]),
                         reads=[r, C["ident"]], writes=[ps], sig=(j == 2))
                k.op("act", lambda e: e.activation(out=o[:, half * 3:half * 3 + 3, :], in_=ps[:, 0:384].rearrange("p (a b) -> p a b", a=3), func=AF.Copy, scale=scl),
                     reads=[ps], writes=[o])
            k.dma("sp", osem, dstT[:, t0:t0 + 128].rearrange("(c p) t -> p c t", p=128), o[:, :, :], src=o, dst=dstT)


def mlstm_readout(k, C, hdir, o_tm, hg_d, mixT, NH):
    W = NH * 512
    s0 = k.dsem()
    hg = k.sb("rohg", [128, W], F32)
    k.dma("sp", s0, hg[:, :], hg_d[0:1, :].partition_broadcast(128), dst=hg)
    nb = 2
    h0 = [k.sb("roh0%d" % i, [128, W], F32) for i in range(nb)]
    h1 = [k.sb("roh1%d" % i, [128, W], F32) for i in range(nb)]
    ot = [k.sb("roo%d" % i, [128, W], F32) for i in range(nb)]
    ls = [k.dsem() for _ in range(nb)]
    sqj = k.sb("rosq", [128, 512], F32)
    ss = k.sb("ross", [128, NH], F32)
    mo = [k.sb("romo%d" % i, [128, W // 128, 128], BF16) for i in range(nb)]
    ms = [k.dsem() for _ in range(nb)]
    pp = PsPool(k, 4, "rops")
    for ci in range(TP // 128):
        b = ci % nb
        t0 = ci * 128
        k.dma("sp", ls[b], h0[b][:, :], hdir[0][t0:t0 + 128, :], dst=h0[b], src=hdir[0])
        k.dma("act", ls[b], h1[b][:, :], hdir[1][t0:t0 + 128, :], dst=h1[b], src=hdir[1])
        k.dma("sp", ls[b], ot[b][:, :], o_tm[t0:t0 + 128, :], dst=ot[b], src=o_tm)
        h0[b].w = list(ot[b].w); h1[b].w = list(ot[b].w)
        k.op("dve", lambda e: e.tensor_tensor(out=h0[b][:, :], in0=h0[b][:, :], in1=h1[b][:, :], op=ALU.add), reads=[h0[b], h1[b]], writes=[h0[b]])
        k.op("act", lambda e: e.activation(out=ot[b][:, :], in_=ot[b][:, :], func=AF.Sigmoid), reads=[ot[b]], writes=[ot[b]])
        for h in range(NH):
            k.op("act", lambda e: e.activation(out=sqj[:, :], in_=h0[b][:, h * 512:(h + 1) * 512], func=AF.Square, accum_out=ss[:, h:h + 1]), reads=[h0[b]], writes=[sqj, ss])
        k.op("act", lambda e: e.activation(out=ss[:, :], in_=ss[:, :], func=AF.Sqrt, scale=1.0 / 512.0, bias=k.eps_t[:, 0:1]), reads=[ss, k.eps_t], writes=[ss])
        k.op("dve", lambda e: e.reciprocal(out=ss[:, :], in_=ss[:, :]), reads=[ss], writes=[ss])
        for h in range(NH):
            sl = slice(h * 512, (h + 1) * 512)
            k.op("dve", lambda e: e.scalar_tensor_tensor(out=h1[b][:, sl], in0=h0[b][:, sl], scalar=ss[:, h:h + 1], in1=hg[:, sl], op0=ALU.mult, op1=ALU.mult),
                 reads=[h0[b], ss, hg], writes=[h1[b]])
        k.op("pool", lambda e: e.tensor_tensor(out=h1[b][:, :], in0=h1[b][:, :], in1=ot[b][:, :], op=ALU.mult), reads=[h1[b], ot[b]], writes=[h1[b]])
        for grp in range(W // 512):
            ps = pp.get()
            for j in range(4):
                c = grp * 4 + j
                k.op("pe", lambda e: e.transpose(out=ps[:, j * 128:(j + 1) * 128], in_=h1[b][:, c * 128:(c + 1) * 128], identity=C["ident"][:, :]),
                     reads=[h1[b], C["ident"]], writes=[ps], sig=(j == 3))
            k.op("act", lambda e: e.copy(out=mo[b][:, grp * 4:grp * 4 + 4, :], in_=ps[:, :].rearrange("p (a b) -> p a b", a=4)), reads=[ps], writes=[mo[b]])
        mp = mixT[ci]
        k.dma("sp", ms[b], mp[0:W, :].rearrange("(c p) t -> p c t", p=128), mo[b][:, :, :], src=mo[b], dst=mp)


def ffn_up_phase(k, C, h2T, halo, Wup, cw, cb, uffn, segs):
    NBW = 256
    blocks = []
    for (t0, n, left, right) in segs:
        nblk = ceil_div(n, 510)
        sz = ceil_div(n, nblk)
        x = t0
        while x < t0 + n:
            c = min(sz, t0 + n - x)
            blocks.append((x, c, left if x == t0 else ("int",), right if x + c == t0 + n else ("int",)))
            x += c
    half_n = ceil_div(len(blocks), 2)
    supers = [blocks[:half_n], blocks[half_n:]] if len(blocks) > 3 else [blocks]
    maxb = max(len(s) for s in supers)
    A = [k.sb("fuA%d" % i, [128, KC, 512], BF16) for i in range(maxb)]
    As = [k.dsem() for _ in range(maxb)]
    wa = [k.sb("fuwa%d" % i, [128, KC, NBW], BF16) for i in range(2)]
    wg = [k.sb("fuwg%d" % i, [128, KC, NBW], BF16) for i in range(2)]
    wsm = [k.dsem() for _ in range(2)]
    pg = [k.ps("fupg%d" % i, [128, 512]) for i in range(3)]
    pa = [k.ps("fupa%d" % i, [128, 512]) for i in range(3)]
    c1 = [k.sb("fuc%d" % i, [128, 512], F32) for i in range(2)]
    y1 = [k.sb("fuy%d" % i, [128, 512], F32) for i in range(2)]
    y2 = [k.sb("fuz%d" % i, [128, 512], F32) for i in range(2)]
    ust = Stage(k, 3, "fuu", [128, 512], BF16)
    allr = slice(0, 128)
    it = 0
    nwb = DFF // NBW
    for sblocks in supers:
        for bi, (x, c, left, right) in enumerate(sblocks):
            a = A[bi]
            lo = x - 1 if left[0] == "int" else x
            hi = x + c + 1 if right[0] == "int" else x + c
            k.dma("act", As[bi], a[:, :, 1 - (x - lo):1 + c + (hi - x - c)], h2T[:, lo:hi].rearrange("(kc p) t -> p kc t", p=128), dst=a, src=h2T)
            if left[0] == "halo":
                k.op("dve", lambda e: e.tensor_copy(out=a[:, :, 0:1], in_=halo[:, :, left[1]:left[1] + 1]), reads=[halo], writes=[a])
            if right[0] == "halo":
                k.op("dve", lambda e: e.tensor_copy(out=a[:, :, c + 1:c + 2], in_=halo[:, :, right[1]:right[1] + 1]), reads=[halo], writes=[a])

        def issue_w(wi):
            j = wi % 2
            tka, apa = Wup(wi * NBW, NBW)
            tkg, apg = Wup(DFF + wi * NBW, NBW)
            k.dma("sp", wsm[j], wa[j][:, :, :], apa.rearrange("(kc p) n -> p kc n", p=128), dst=wa[j], src=tka)
            k.dma("sp", wsm[j], wg[j][:, :, :], apg.rearrange("(kc p) n -> p kc n", p=128), dst=wg[j], src=tkg)
            wa[j].w = list(wg[j].w)
        issue_w(0)
        for wi in range(nwb):
            if wi + 1 < nwb:
                issue_w(wi + 1)
            j = wi % 2
            for cj in range(NBW // 128):
                jc = wi * (NBW // 128) + cj
                for bi, (x, c, left, right) in enumerate(sblocks):
                    a = A[bi]
                    q = it % 3; q2 = it % 2; it += 1
                    for kc in range(KC):
                        k.op("pe", lambda e: e.matmul(out=pg[q][:, :c + 2], lhsT=wg[j][:, kc, cj * 128:(cj + 1) * 128], rhs=a[:, kc, 0:c + 2], start=(kc == 0), stop=(kc == KC - 1)),
                             reads=[wg[j], a], writes=[pg[q]], sig=(kc == KC - 1))
                    for kc in range(KC):
                        k.op("pe", lambda e: e.matmul(out=pa[q][:, :c], lhsT=wa[j][:, kc, cj * 128:(cj + 1) * 128], rhs=a[:, kc, 1:c + 1], start=(kc == 0), stop=(kc == KC - 1)),
                             reads=[wa[j], a], writes=[pa[q]], sig=(kc == KC - 1))
                    cc = c1[q2]
                    k.op("dve", lambda e: e.tensor_scalar(out=cc[:, :c], in0=pg[q][:, 0:c], scalar1=cw[:, 0, jc:jc + 1], scalar2=None, op0=ALU.mult), reads=[pg[q], cw], writes=[cc])
                    k.op("dve", lambda e: e.scalar_tensor_tensor(out=cc[:, :c], in0=pg[q][:, 1:c + 1], scalar=cw[:, 1, jc:jc + 1], in1=cc[:, :c], op0=ALU.mult, op1=ALU.add), reads=[pg[q], cw, cc], writes=[cc])
                    k.op("dve", lambda e: e.scalar_tensor_tensor(out=cc[:, :c], in0=pg[q][:, 2:c + 2], scalar=cw[:, 2, jc:jc + 1], in1=cc[:, :c], op0=ALU.mult, op1=ALU.add), reads=[pg[q], cw, cc], writes=[cc])
                    k.op("act", lambda e: e.activation(out=cc[:, :c], in_=cc[:, :c], func=AF.Identity, bias=cb[:, jc:jc + 1], scale=1.0), reads=[cc, cb], writes=[cc])
                    gl = y2[q2]
                    gelu_tanh(k, cc, y1[q2], gl, gl, allr, c, eng2="pool")
                    u, usem = ust.get()
                    k.op("dve", lambda e: e.tensor_tensor(out=u[:, :c], in0=pa[q][:, :c], in1=gl[:, :c], op=ALU.mult), reads=[pa[q], gl], writes=[u])
                    k.dma("act", usem, uffn[jc * 128:(jc + 1) * 128, x:x + c], u[:, :c], src=u, dst=uffn)


def copy_epilogue(k, dst, dt, tm=False, nstage=3, eng=("act", "dve"), dcol0=0, drow0=0):
    st = Stage(k, nstage, "cpe", [128, 512], dt)
    cnt = [0]

    def epi(ps, m, n, a0, b0, **kw):
        s, sem = st.get()
        e = eng[cnt[0] % len(eng)]; cnt[0] += 1
        if e == "act":
            k.op("act", lambda en: en.copy(out=s[:m, :n], in_=ps[:m, :n]), reads=[ps], writes=[s])
        else:
            k.op("dve", lambda en: en.tensor_copy(out=s[:m, :n], in_=ps[:m, :n]), reads=[ps], writes=[s])
        k.dma("act", sem, dst[drow0 + a0:drow0 + a0 + m, dcol0 + b0:dcol0 + b0 + n], s[:m, :n], src=s, dst=dst)
    return epi


def dram_copy(k, q, dst, dst_ap, src, src_ap, **kw):
    k.dma(q, k.dsem(), dst_ap, src_ap, dst=dst, src=src, **kw)


class _Stop(Exception):
    pass


def build_program(stop_after=None, debug=(), internal_inputs=False):
    nc = bass.Bass("TRN2", target_bir_lowering=False, num_devices=8)
    k = K(nc)
    try:
        _build_body(nc, k, stop_after, debug, internal_inputs)
    except _Stop:
        pass
    return nc


def _build_body(nc, k, stop_after, debug, internal_inputs):
    IN = {}

    def inp(name, shape, dt=F32):
        ext = (not internal_inputs) or (isinstance(internal_inputs, (list, tuple)) and name in internal_inputs)
        IN[name] = k.dram(name, shape, dt, kind="ExternalInput" if ext else "Internal")

    def ck(tag):
        if stop_after == tag:
            dd = k.dram("dummy_out", [128, 4], F32, kind="ExternalOutput")
            k.dma("sp", k.dsem(), dd[:, :], fl[:, :], src=fl, dst=dd)
            k.barrier()
            k.close()
            raise _Stop()
    inp("xin", [D, TOWN]); inp("condT", [128, 32, 5]); inp("onehot", [128, 5]); inp("flags", [128, 4])
    inp("modw", [2, D, 3072]); inp("modb", [1, 6144]); inp("gmix", [128, 2, 32]); inp("gffn", [128, 2, 32]); inp("gfin", [128, 32])
    inp("consts", [128, 640])
    wspec = [("w_in", 1024, WIN_N, PARITY), ("w_glu", 128, 1024, ALL8), ("w_out0", 512, D, ALL8), ("w_up0", 512, 2 * DFF, ALL8),
             ("w_dn0", DFF // 8, D, ALL8), ("w_qkv", 512, 3 * D, ALL8), ("w_out1", 512, D, ALL8), ("w_up1", 512, 2 * DFF, ALL8), ("w_dn1", DFF // 8, D, ALL8)]
    for (nm, R, N, grp) in wspec:
        inp(nm, [R, N])
    inp("gate_b", [1, 12]); inp("head_g", [1, 1536]); inp("ropeT", [TP, 256])
    inp("s5par", [128, 3, 32]); inp("s5Bre", [128, 16, 128]); inp("s5Bim", [128, 16, 128]); inp("s5Cre", [128, 16, 64]); inp("s5Cim", [128, 16, 64])
    inp("s5dsk", [128, 4]); inp("glu_b", [128, 8]); inp("conv_w", [128, 2, 3, NJC]); inp("conv_b", [128, 2, NJC])
    inp("rpbT", [32, 64, 15, 64]); inp("cmask", [64, 64])
    outT = k.dram("outT", [D, TLAT], F32, kind="ExternalOutput")
    DBG = {}
    for (nm, shape, dt) in debug:
        DBG[nm] = k.dram("dbg_" + nm, shape, dt, kind="ExternalOutput")

    C = load_consts(k, IN["consts"])
    s0 = k.dsem()
    fl = k.sb("flags", [128, 4], F32); gfin = k.sb("gfin", [128, 32], F32); glub = k.sb("glub", [128, 8], F32)
    cw = k.sb("convw", [128, 2, 3, NJC], F32); cb = k.sb("convb", [128, 2, NJC], F32)
    k.dma("sp", s0, fl[:, :], IN["flags"][:, :], dst=fl); k.dma("sp", s0, gfin[:, :], IN["gfin"][:, :], dst=gfin)
    k.dma("sp", s0, glub[:, :], IN["glu_b"][:, :], dst=glub)
    k.dma("sp", s0, cw[:].rearrange("p a b c -> p (a b c)"), IN["conv_w"][:].rearrange("p a b c -> p (a b c)"), dst=cw)
    k.dma("sp", s0, cb[:].rearrange("p a b -> p (a b)"), IN["conv_b"][:].rearrange("p a b -> p (a b)"), dst=cb)
    for t in (fl, gfin, glub, cw):
        t.w = list(cb.w)
    MV = {}
    for l in range(2):
        for si in range(2):
            for key in (0, 1, 2, 3, 4, 5, "gscm", "gscf"):
                MV[l, key, si] = k.sb("mv", [128, 32], F32)

    WT = {}
    PW = {"w_in": 512, "w_glu": 1024, "w_out0": 512, "w_up0": 512, "w_dn0": 128, "w_qkv": 512, "w_out1": 512, "w_up1": 512, "w_dn1": 128}
    semA = k.dsem(); semB = k.dsem()
    st1_in = {}; st1_out = {}; full = {}
    for (nm, R, N, grp) in wspec:
        w = PW[nm]
        npc = N // w
        st1_in[nm] = [k.dram("%s_a%d" % (nm, j), [R, w], BF16) for j in range(npc)]
        with k.phase():
            cast_rows_pieces(k, IN[nm], st1_in[nm], R, N, w)
    ck('cast')
    for (nm, R, N, grp) in wspec:
        w = PW[nm]
        npc = N // w
        if nm == "w_in":
            full[nm] = [k.dram("%s_f%d" % (nm, j), [4 * R, w], BF16) for j in range(npc)]
            for j in range(npc):
                k.allgather(semB, full[nm][j], st1_in[nm][j], PARITY)
        else:
            st1_out[nm] = [k.dram("%s_b%d" % (nm, j), [2 * R, w], BF16) for j in range(npc)]
            for j in range(npc):
                k.allgather(semA, st1_out[nm][j], st1_in[nm][j], PAIRS)
    for nm in st1_out:
        for t in st1_out[nm]:
            t.w = [(semA, k.cnt[semA])]
    for (nm, R, N, grp) in wspec:
        if nm == "w_in":
            continue
        w = PW[nm]
        npc = N // w
        full[nm] = [k.dram("%s_f%d" % (nm, j), [8 * R, w], BF16) for j in range(npc)]
        for j in range(npc):
            k.allgather(semB, full[nm][j], st1_out[nm][j], PARITY)
    for nm in full:
        for t in full[nm]:
            t.w = [(semB, k.cnt[semB])]

    def wget(nm):
        w = PW[nm]

        def f(c0, n):
            tk = full[nm][c0 // w]
            off = c0 % w
            assert off + n <= w, (nm, c0, n)
            return tk, tk[:, off:off + n]
        return f
    for nm in full:
        WT[nm] = wget(nm)
    w_in_ap = WT["w_in"]
    if "wdump" in DBG:
        with k.phase():
            for i_, j_ in enumerate((0, 8, 9, 10)):
                dram_copy(k, "sp", DBG["wdump"], DBG["wdump"][i_ * D:(i_ + 1) * D, :], full["w_in"][j_], full["w_in"][j_][:, :])
    ck('weights')
    mod_phase(k, C, IN, MV)
    ck('mod')

    def done(tag):
        return stop_after == tag

    xA = k.dram("xA", [D, TOWN], F32); xB = k.dram("xB", [D, TOWN], F32)
    hT_own = k.dram("hT_own", [D, TOWN], BF16)
    NPC = TOWN // 128
    hT_own_p = [k.dram("hTo%d" % j, [D, 128], BF16) for j in range(NPC)]
    hT_pair_p = [k.dram("hTp%d" % j, [2 * D, 128], BF16) for j in range(NPC)]
    h2T = k.dram("h2T", [D, TOWN], BF16); uffn = k.dram("uffn", [DFF, TOWN], BF16)
    hb_own = k.dram("hb_own", [D, 64], BF16); hb_pair = k.dram("hb_pair", [2 * D, 64], BF16)
    SEG01 = [(0, TCX, 1), (TCX, TLAT, 0)]

    def ffn(l, xsrc, xdst, with_ctx):
        segn = SEG01 if with_ctx else [(TCX, TLAT, 0)]
        with k.phase():
            norm_mod(k, xsrc, h2T, segn, {0: MV[l, "gscf", 0], 1: MV[l, "gscf", 1]}, {0: MV[l, 3, 0], 1: MV[l, 3, 1]}, C["ones_bf"], D)
        with k.phase():
            for j, col in enumerate((0, TCX - 1, TCX, TOWN - 1)):
                if with_ctx or j >= 2:
                    dram_copy(k, "sp", hb_own, hb_own[:, j:j + 1], h2T, h2T[:, col:col + 1], allow_slow_non_contiguous=True)
        ck('ffnhalo%d' % l)
        k.allgather(k.dsem(), hb_pair, hb_own, PAIRS)
        with k.phase():
            hraw = k.sb("hraw", [128, KC, 2, 4], BF16); halo = k.sb("halo", [128, KC, 4], BF16)
            hs = k.dsem()
            for r in range(2):
                k.dma("sp", hs, hraw[:, :, r, :], hb_pair[r * D:(r + 1) * D, 0:4].rearrange("(kc p) j -> p kc j", p=128), dst=hraw, src=hb_pair)
            for (o, r, cidx, fcol) in ((0, 0, 1, 1), (1, 1, 0, 0), (2, 0, 3, 1), (3, 1, 2, 0)):
                k.op("dve", lambda e: e.tensor_scalar(out=halo[:, :, o:o + 1], in0=hraw[:, :, r, cidx:cidx + 1], scalar1=fl[:, fcol:fcol + 1], scalar2=None, op0=ALU.mult),
                     reads=[hraw, fl], writes=[halo])
            segs = []
            if with_ctx:
                segs.append((0, TCX, ("halo", 0), ("halo", 1)))
            segs.append((TCX, TLAT, ("halo", 2), ("halo", 3)))
            ffn_up_phase(k, C, h2T, halo, WT["w_up%d" % l], _Sub3(cw, l), _Sub2(cb, l), uffn, segs)
        ck('ffnup%d' % l)
        with k.phase():
            pp = PsPool(k, 4, "fdps")
            t00 = 0 if with_ctx else TCX
            nt = TOWN - t00
            blocks = mk_blocks(nt, 512, 512, bounds=(TCX - t00,))

            def load_a(dst, sem, t0, n):
                load_rows(k, "act", sem, dst, dst[:, :, :n], uffn, uffn[:, t00 + t0:t00 + t0 + n])
            epi = resid_epilogue(k, xsrc, xdst, [MV[l, 5, 0], MV[l, 5, 1]], tcol0=t00)
            gemm(k, "fm", NJC, blocks, load_a, WT["w_dn%d" % l], 0, D, 128, epi, pp, tag="fd")

    with k.phase():
        def wr_h(h, a, m, sem):
            for a2 in range(a, a + m, 128):
                pc = hT_own_p[a2 // 128]
                k.dma("act", sem, pc[:, :].rearrange("(kc p) t -> p kc t", p=128), h[:, :, a2 - a:a2 - a + 128], src=h, dst=pc)
        norm_mod(k, IN["xin"], None, SEG01, {0: MV[0, "gscm", 0], 1: MV[0, "gscm", 1]}, {0: MV[0, 0, 0], 1: MV[0, 0, 1]}, C["ones_bf"], D, writer=wr_h)
    for j in range(NPC):
        k.allgather(k.dsem(), hT_pair_p[j], hT_own_p[j], PAIRS)
    ck('norm0')
    q_tm = k.dram("q_tm", [TP, 768], F32); k_tm = k.dram("k_tm", [TP, 768], F32); v_tm = k.dram("v_tm", [TP, 1536], F32)
    o_tm = k.dram("o_tm", [TP, 1536], F32); gates = k.dram("gates", [TP, 12], F32); uT = k.dram("uT", [512, TP], F32)
    pblocks_tm = []
    pblocks_fm = []
    for r in range(2):
        for (t0, n, subs) in mk_blocks(TOWN, 1152, 128):
            pblocks_tm.append((r * TOWN + t0, n, subs))
        for (t0, n, subs) in mk_blocks(TOWN, 1152, 512):
            pblocks_fm.append((r * TOWN + t0, n, subs))

    def load_pair(dst, sem, t0, n):
        r, pos = t0 // TOWN, t0 % TOWN
        for a2 in range(pos, pos + n, 128):
            pc = hT_pair_p[a2 // 128]
            k.dma("act", sem, dst[:, :, a2 - pos:a2 - pos + 128], pc[r * D:(r + 1) * D, :].rearrange("(kc p) t -> p kc t", p=128), dst=dst, src=pc)
    with k.phase():
        pp = PsPool(k, 4, "ipps")
        st = Stage(k, 3, "ipst", [128, 512], F32)
        dests = [(0, 768, q_tm), (768, 1536, k_tm), (1536, 3072, v_tm), (3072, 4608, o_tm), (4608, 4620, gates)]

        def epi_ip(ps, m, n, t0, c0, **kw):
            s, sem = st.get()
            k.op("act", lambda e: e.copy(out=s[:m, :n], in_=ps[:m, :n]), reads=[ps], writes=[s])
            for (a, b_, dtk) in dests:
                lo, hi = max(a, c0), min(b_, c0 + n)
                if lo < hi:
                    k.dma("act", sem, dtk[t0:t0 + m, lo - a:hi - a], s[:m, lo - c0:hi - c0], src=s, dst=dtk)
        gemm(k, "tm", KC, pblocks_tm, load_pair, w_in_ap, 0, 4620, 512, epi_ip, pp, tag="ip")
    with k.phase():
        pp = PsPool(k, 4, "iups")
        gemm(k, "fm", KC, pblocks_fm, load_pair, w_in_ap, WIN_U0, 512, 512, copy_epilogue(k, uT, F32), pp, tag="iu")
    ck('inproj')
    qT = k.dram("qT", [768, TP], F32); kT = k.dram("kT", [768, TP], F32); ktm_r = k.dram("ktm_r", [TP, 768], F32)
    with k.phase():
        rope_phase(k, C, q_tm, k_tm, IN["ropeT"], qT, kT, ktm_r, 3)
    ck('rope')
    hdir = [k.dram("hdir0", [TP, 1536], F32), k.dram("hdir1", [TP, 1536], F32)]
    with k.phase():
        gb_bc = k.sb("gb_bc", [128, 12], F32)
        k.dma("sp", k.dsem(), gb_bc[:, :], IN["gate_b"][0:1, :].partition_broadcast(128), dst=gb_bc)
        fwd = [0, 17] + list(range(1, 17)) + list(range(18, 34))
        bwd = [17, 0] + list(range(33, 17, -1)) + list(range(16, 0, -1))
        mlstm_scan(k, C, qT, kT, ktm_r, v_tm, gates, gb_bc, hdir, [fwd, bwd], 3)
    ck('mlstm')
    mix_own = [k.dram("mixo%d" % j, [2048, 128], BF16) for j in range(TP // 128)]
    mix_pair = [k.dram("mixp%d" % j, [4096, 128], BF16) for j in range(TP // 128)]
    with k.phase():
        mlstm_readout(k, C, hdir, o_tm, IN["head_g"], mix_own, 3)
    ck('readout')
    with k.phase():
        lat0 = [(TCX + 512 * i, 512) for i in range(4)]
        lat1 = [(TOWN + TCX + 512 * i, 512) for i in range(4)]
        chf = [(0, TCX), (TOWN, TCX)] + lat0 + lat1
        chb = [(TOWN, TCX), (0, TCX)] + lat1[::-1] + lat0[::-1]
        s5_phase(k, C, uT, IN["s5par"], IN["s5Bre"], IN["s5Bim"], IN["s5Cre"], IN["s5Cim"], IN["s5dsk"], mix_own, 1536, TP, 32, [chf, chb])
    ck('s5')
    for j in range(TP // 128):
        k.allgather(k.dsem(), mix_pair[j], mix_own[j], PAIRS)
    mixsel = k.dram("mixsel", [4096, TOWN], BF16); ysT = k.dram("ysT", [1024, TOWN], BF16)
    with k.phase():
        blend_cols(k, mixsel, [(j * 128, mix_pair[j], mix_pair[NPC + j]) for j in range(NPC)], fl, 0, 1, 4096)
    ck('blend')
    own_blocks = mk_blocks(TOWN, 1088, 512, bounds=(TCX,))
    with k.phase():
        pp = PsPool(k, 4, "glps")
        st = Stage(k, 3, "glst", [128, 512], BF16)
        sg = [k.sb("glsg%d" % i, [128, 512], F32) for i in range(2)]
        cnt = [0]

        def load_g(dst, sem, t0, n):
            for r in range(2):
                k.dma("act", sem, dst[:, 4 * r:4 * r + 4, :n], mixsel[r * 2048 + 1536:r * 2048 + 2048, t0:t0 + n].rearrange("(kc p) t -> p kc t", p=128), dst=dst, src=mixsel)

        def epi_glu(ps, m, n, f0, t0, at=None, ts=None):
            fc = f0 // 128
            g_ = sg[cnt[0] % 2]; cnt[0] += 1
            k.op("act", lambda e: e.activation(out=g_[:m, :n], in_=ps[:m, :n], func=AF.Sigmoid, bias=glub[:m, fc:fc + 1], scale=1.0), reads=[ps, glub], writes=[g_])
            s, sem = st.get()
            k.op("dve", lambda e: e.tensor_tensor(out=s[:m, :n], in0=g_[:m, :n], in1=at[:m, fc, ts:ts + n], op=ALU.mult), reads=[g_, at], writes=[s])
            k.dma("act", sem, ysT[f0:f0 + m, t0:t0 + n], s[:m, :n], src=s, dst=ysT)
        gemm(k, "fm", 8, own_blocks, load_g, WT["w_glu"], 0, 1024, 512, epi_glu, pp, tag="gl")
    with k.phase():
        pp = PsPool(k, 4, "o0ps")

        def load_mix(dst, sem, t0, n):
            for r in range(2):
                k.dma("act", sem, dst[:, 12 * r:12 * r + 12, :n], mixsel[r * 2048:r * 2048 + 1536, t0:t0 + n].rearrange("(kc p) t -> p kc t", p=128), dst=dst, src=mixsel)
            k.dma("act", sem, dst[:, 24:32, :n], ysT[:, t0:t0 + n].rearrange("(kc p) t -> p kc t", p=128), dst=dst, src=ysT)
        gemm(k, "fm", KC, own_blocks, load_mix, WT["w_out0"], 0, D, 512, resid_epilogue(k, IN["xin"], xA, [MV[0, 2, 0], MV[0, 2, 1]]), pp, tag="o0")
    ck('outproj0')
    if DBG:
        with k.phase():
            mvd = DBG["mv"]
            for i_, key in enumerate(((0, "gscm", 0), (0, 0, 0), (0, 2, 0), (0, 2, 1), (0, 1, 0), (0, "gscm", 1), (1, 2, 0), (0, 5, 0))):
                k.dma("sp", k.dsem(), mvd[:, i_ * 32:(i_ + 1) * 32], MV[key][:, :], src=MV[key], dst=mvd)
            dram_copy(k, "sp", DBG["hT1"], DBG["hT1"][:, :], hT_own_p[1], hT_own_p[1][:, :])
            dram_copy(k, "sp", DBG["qtm"], DBG["qtm"][:, :], q_tm, q_tm[0:512, :])
            dram_copy(k, "sp", DBG["ktm"], DBG["ktm"][:, :], k_tm, k_tm[0:512, :])
            dram_copy(k, "sp", DBG["vtm"], DBG["vtm"][:, :], v_tm, v_tm[0:512, :])
            dram_copy(k, "sp", DBG["gates"], DBG["gates"][:, :], gates, gates[0:512, :])
            dram_copy(k, "sp", DBG["uTd"], DBG["uTd"][:, :], uT, uT[:, 0:512])
            dram_copy(k, "sp", DBG["qTd"], DBG["qTd"][:, :], qT, qT[:, 0:512])
            dram_copy(k, "sp", DBG["hd0"], DBG["hd0"][:, :], hdir[0], hdir[0][0:512, :])
            dram_copy(k, "sp", DBG["hd1"], DBG["hd1"][:, :], hdir[1], hdir[1][0:512, :])
            dram_copy(k, "sp", DBG["mixsel"], DBG["mixsel"][:, :], mixsel, mixsel[:, 0:256])
            dram_copy(k, "sp", DBG["ysT"], DBG["ysT"][:, :], ysT, ysT[:, 0:256])
            dram_copy(k, "sp", DBG["xA0"], DBG["xA0"][:, :], xA, xA[:, :])
    ffn(0, xA, xB, True)
    ck('ffn0')
    if DBG:
        with k.phase():
            dram_copy(k, "sp", DBG["xB0"], DBG["xB0"][:, :], xB, xB[:, :])

    h1T_own = k.dram("h1T_own", [D, TOWN], BF16)
    h1b_own = [k.dram("h1bo%d" % j, [D, 128], BF16) for j in range(5)]
    h1b_pair = [k.dram("h1bp%d" % j, [2 * D, 128], BF16) for j in range(5)]
    hk = k.dram("hk", [D, 2816], BF16)
    with k.phase():
        norm_mod(k, xB, h1T_own, SEG01, {0: MV[1, "gscm", 0], 1: MV[1, "gscm", 1]}, {0: MV[1, 0, 0], 1: MV[1, 0, 1]}, C["ones_bf"], D)
    with k.phase():
        for j, c0 in enumerate((TCX, TCX + 128, TOWN - 256, TOWN - 128, 0)):
            dram_copy(k, "sp", h1b_own[j], h1b_own[j][:, :], h1T_own, h1T_own[:, c0:c0 + 128])
        dram_copy(k, "act", hk, hk[:, 256:2304], h1T_own, h1T_own[:, TCX:TOWN])
    for j in range(5):
        k.allgather(k.dsem(), h1b_pair[j], h1b_own[j], PAIRS)
    with k.phase():
        scale_cols(k, hk, 0, _RowWin(h1b_pair[2], 0, D), 0, fl, 1, D, 128)
        scale_cols(k, hk, 128, _RowWin(h1b_pair[3], 0, D), 0, fl, 1, D, 128)
        scale_cols(k, hk, 2304, _RowWin(h1b_pair[0], D, 2 * D), 0, fl, 0, D, 128)
        scale_cols(k, hk, 2432, _RowWin(h1b_pair[1], D, 2 * D), 0, fl, 0, D, 128)
        dram_copy(k, "pool", hk, hk[:, 2560:2688], h1b_pair[4], h1b_pair[4][0:D, :])
        dram_copy(k, "pool", hk, hk[:, 2688:2816], h1b_pair[4], h1b_pair[4][D:2 * D, :])
    ck('hk')
    QT = k.dram("QT", [D, TLAT], BF16); KT = k.dram("KT", [D, 2816], BF16); Vn = k.dram("Vn", [2816, D], BF16); attnT = k.dram("attnT", [D, TLAT], BF16)

    def load_hk(off):
        def f(dst, sem, t0, n):
            load_rows(k, "act", sem, dst, dst[:, :, :n], hk, hk[:, off + t0:off + t0 + n])
        return f
    with k.phase():
        gemm(k, "fm", KC, mk_blocks(TLAT, 1024, 512), load_hk(256), WT["w_qkv"], 0, D, 512, copy_epilogue(k, QT, BF16), PsPool(k, 4, "qps"), tag="q")
    with k.phase():
        gemm(k, "fm", KC, mk_blocks(2816, 1024, 512), load_hk(0), WT["w_qkv"], D, D, 512, copy_epilogue(k, KT, BF16), PsPool(k, 4, "kps"), tag="kk")
    with k.phase():
        gemm(k, "tm", KC, mk_blocks(2816, 1024, 128), load_hk(0), WT["w_qkv"], 2 * D, D, 512, copy_epilogue(k, Vn, BF16), PsPool(k, 4, "vps"), tag="vv")
    ck('qkv')
    with k.phase():
        na_phase(k, C, QT, KT, Vn, IN["rpbT"], IN["cmask"], IN["flags"], attnT, 32, 128 ** -0.5)
    with k.phase():
        def load_at(dst, sem, t0, n):
            load_rows(k, "act", sem, dst, dst[:, :, :n], attnT, attnT[:, t0:t0 + n])
        gemm(k, "fm", KC, mk_blocks(TLAT, 1024, 512), load_at, WT["w_out1"], 0, D, 512,
             resid_epilogue(k, xB, xA, [MV[1, 2, 0], MV[1, 2, 1]], tcol0=TCX), PsPool(k, 4, "o1ps"), tag="o1")
    ck('outproj1')
    if DBG:
        with k.phase():
            dram_copy(k, "sp", DBG["xA1"], DBG["xA1"][:, :], xA, xA[:, :])
    ffn(1, xA, xB, False)
    ck('ffn1')
    with k.phase():
        norm_mod(k, xB, None, [(TCX, TLAT, 0)], {0: gfin}, None, C["ones_bf"], D, hT_col0=-TCX, out_f32=outT)
    k.close()


class _RowWin:
    def __init__(self, tk, r0, r1):
        self.tk = tk; self.r0 = r0; self.r1 = r1

    @property
    def w(self):
        return self.tk.w

    @w.setter
    def w(self, v):
        self.tk.w = v

    @property
    def r(self):
        return self.tk.r

    @r.setter
    def r(self, v):
        self.tk.r = v

    def __getitem__(self, idx):
        return self.tk.t[self.r0:self.r1, :][idx]


class _Sub3:
    def __init__(self, tk, l):
        self.tk = tk; self.l = l; self.w = tk.w; self.r = tk.r

    def __getitem__(self, idx):
        return self.tk.t[:, self.l, :, :][idx]


class _Sub2:
    def __init__(self, tk, l):
        self.tk = tk; self.l = l; self.w = tk.w; self.r = tk.r

    def __getitem__(self, idx):
        return self.tk.t[:, self.l, :][idx]


_NC_CACHE = {}


def _rope_table():
    inv = (10000.0 ** (-np.arange(0, 128, 2, dtype=np.float32) / np.float32(128))).astype(np.float32)
    tab = np.zeros((TP, 2, 2, 64), np.float32)
    tab[:, 0] = 1.0
    for r in range(2):
        t = r * TLAT + np.arange(TLAT)
        row = (t // 64).astype(np.float32)[:, None] * inv[None, :]
        col = (t % 64).astype(np.float32)[:, None] * inv[None, :]
        base = r * TOWN + TCX
        tab[base:base + TLAT, 0, 0] = np.cos(row); tab[base:base + TLAT, 0, 1] = np.cos(col)
        tab[base:base + TLAT, 1, 0] = np.sin(row); tab[base:base + TLAT, 1, 1] = np.sin(col)
    return tab.reshape(TP, 256)


def make_in_maps(x, c, ctx, c_ctx, mod_w, mod_b, norm_mix_g, norm_ffn_g, ab_w_in, mlstm_gate_b,
                 mlstm_head_g, s5_a_re, s5_a_im, s5_log_dt, s5_b_re, s5_b_im, s5_c_re, s5_c_im,
                 s5_d, s5_glu_w, s5_glu_b, ab_w_out, na_w_qkv, na_rpb, na_w_out,
                 ffn_w_up, ffn_conv_w, ffn_conv_b, ffn_w_down, final_norm_g):
    f = lambda a: np.asarray(a, dtype=np.float32)
    x, c, ctx, c_ctx = f(x), f(c), f(ctx), f(c_ctx)
    cond = np.concatenate([c, c_ctx[None, :]], 0)
    condT = np.ascontiguousarray(cond.reshape(5, 32, 128).transpose(2, 1, 0))
    mod_w, mod_b = f(mod_w), f(mod_b)
    pl = lambda g: np.ascontiguousarray(f(g).reshape(2, 32, 128).transpose(2, 0, 1))
    gmix, gffn = pl(norm_mix_g), pl(norm_ffn_g)
    gfin = np.ascontiguousarray(f(final_norm_g).reshape(32, 128).T)
    consts = host_consts()
    ropeT = _rope_table()
    Tg, cmask = na_host_tables(f(na_rpb)[0])
    conv_w = np.ascontiguousarray(f(ffn_conv_w).reshape(2, 3, NJC, 128).transpose(3, 0, 1, 2))
    conv_b = np.ascontiguousarray(f(ffn_conv_b).reshape(2, NJC, 128).transpose(2, 0, 1))
    glu_b = np.ascontiguousarray(f(s5_glu_b)[0].reshape(8, 128).T)
    w_in = f(ab_w_in)[0]; gate_b = f(mlstm_gate_b)[0]; head_g = f(mlstm_head_g)[0]
    in_maps = []
    for core in range(8):
        b, g = core // 2, core % 2
        m = {}
        m["xin"] = np.ascontiguousarray(np.concatenate([ctx[b, g * TCX:(g + 1) * TCX], x[b, g * TLAT:(g + 1) * TLAT]], 0).T)
        m["condT"] = condT
        oh = np.zeros((128, 5), np.float32); oh[:, b] = 1.0
        m["onehot"] = oh
        fl = np.zeros((128, 4), np.float32); fl[:, 0] = 1 - g; fl[:, 1] = g; fl[:, 2] = g; fl[:, 3] = 1 - g
        m["flags"] = fl
        m["modw"] = np.stack([np.concatenate([mod_w[l][:, w * D + core * 512:w * D + (core + 1) * 512] for w in range(6)], 1) for l in range(2)], 0)
        m["modb"] = np.concatenate([np.concatenate([mod_b[l][w * D + core * 512:w * D + (core + 1) * 512] for w in range(6)]) for l in range(2)])[None, :]
        m["gmix"] = gmix; m["gffn"] = gffn; m["gfin"] = gfin; m["consts"] = consts
        gcols = [9216 + d_ * 12 + i_ * 6 + 3 * g + h_ for d_ in range(2) for i_ in range(2) for h_ in range(3)]
        cols = np.concatenate([np.arange(g * 768, (g + 1) * 768), 1536 + np.arange(g * 768, (g + 1) * 768),
                               3072 + np.arange(g * 1536, (g + 1) * 1536), 6144 + np.arange(g * 1536, (g + 1) * 1536),
                               np.array(gcols), 9240 + np.arange(g * 512, (g + 1) * 512)])
        wi = np.zeros((1024, WIN_N), np.float32)
        wsel = w_in[b * 1024:(b + 1) * 1024][:, cols]
        wi[:, :4620] = wsel[:, :4620]
        wi[:, WIN_U0:WIN_U0 + 512] = wsel[:, 4620:5132]
        m["w_in"] = wi
        m["w_glu"] = f(s5_glu_w)[0][core * 128:(core + 1) * 128]
        m["w_out0"] = f(ab_w_out)[0][core * 512:(core + 1) * 512]
        m["w_up0"] = f(ffn_w_up)[0][core * 512:(core + 1) * 512]; m["w_up1"] = f(ffn_w_up)[1][core * 512:(core + 1) * 512]
        r8 = DFF // 8
        m["w_dn0"] = f(ffn_w_down)[0][core * r8:(core + 1) * r8]; m["w_dn1"] = f(ffn_w_down)[1][core * r8:(core + 1) * r8]
        m["w_qkv"] = f(na_w_qkv)[0][core * 512:(core + 1) * 512]
        m["w_out1"] = f(na_w_out)[0][core * 512:(core + 1) * 512]
        m["gate_b"] = np.ascontiguousarray(gate_b[:, :, 3 * g:3 * g + 3]).reshape(1, 12)
        m["head_g"] = head_g[g * 1536:(g + 1) * 1536][None, :]
        m["ropeT"] = ropeT
        gs = slice(32 * g, 32 * g + 32)
        par, Bre, Bim, Cre, Cim, dsk = s5_host_layout(f(s5_a_re)[0][:, gs], f(s5_a_im)[0][:, gs], f(s5_log_dt)[0][:, gs], f(s5_b_re)[0][gs], f(s5_b_im)[0][gs],
                                                      f(s5_c_re)[0][gs], f(s5_c_im)[0][gs], f(s5_d)[0][g * 512:(g + 1) * 512])
        m["s5par"] = par; m["s5Bre"] = Bre; m["s5Bim"] = Bim; m["s5Cre"] = Cre; m["s5Cim"] = Cim; m["s5dsk"] = dsk
        m["glu_b"] = glu_b; m["conv_w"] = conv_w; m["conv_b"] = conv_b; m["rpbT"] = Tg; m["cmask"] = cmask
        in_maps.append(m)
    return in_maps


def kernel(**inputs):
    if "nc" not in _NC_CACHE:
        _NC_CACHE["nc"] = build_program()
    nc = _NC_CACHE["nc"]
    in_maps = make_in_maps(**inputs)
    res = run_bass_kernel_spmd(nc, in_maps, core_ids=list(range(8)))
    out = np.empty((4, 4096, 4096), np.float32)
    for core in range(8):
        b, g = core // 2, core % 2
        out[b, g * TLAT:(g + 1) * TLAT, :] = res.results[core]["outT"].T
    return out
```

```python
import contextlib
import numpy as np
import ml_dtypes
import concourse.bass as bass
import concourse.mybir as mybir
from concourse.bass_utils import run_bass_kernel_spmd

F32 = mybir.dt.float32
BF16 = mybir.dt.bfloat16
AF = mybir.ActivationFunctionType
ALU = mybir.AluOpType
AX = mybir.AxisListType
NPBF = ml_dtypes.bfloat16


class Tk:
    __slots__ = ("t", "w", "r", "name")

    def __init__(self, t, name=""):
        self.t = t
        self.w = []
        self.r = {}
        self.name = name

    def __getitem__(self, idx):
        return self.t[idx]


class K:
    def __init__(self, nc):
        self.nc = nc
        self.es = contextlib.ExitStack()
        self.eng = {"pe": nc.tensor, "act": nc.scalar, "dve": nc.vector, "pool": nc.gpsimd, "sp": nc.sync}
        self.sems = {}
        self.cnt = {}
        self.waited = {e: {} for e in self.eng}
        for e in ("pe", "act", "dve", "pool", "sp"):
            self.sems[e] = self.es.enter_context(nc.semaphore("s_" + e))
            self.cnt[e] = 0
        self.free_dsems = []
        self.phase_dsems = []
        self.nd = 0
        self.pst = None
        self.uid = 0

    def _stack(self):
        return self.pst if self.pst is not None else self.es

    def sb(self, name, shape, dt):
        self.uid += 1
        return Tk(self._stack().enter_context(self.nc.sbuf_tensor("%s_%d" % (name, self.uid), list(shape), dt)), name)

    def ps(self, name, shape, dt=F32):
        self.uid += 1
        return Tk(self._stack().enter_context(self.nc.psum_tensor("%s_%d" % (name, self.uid), list(shape), dt)), name)

    def dram(self, name, shape, dt, kind="Internal"):
        return Tk(self.nc.dram_tensor(name, list(shape), dt, kind=kind).ap(), name)

    def dsem(self):
        if self.free_dsems:
            key = self.free_dsems.pop()
        else:
            self.nd += 1
            key = "d%d" % self.nd
            self.sems[key] = self.es.enter_context(self.nc.semaphore(key))
            self.cnt[key] = 0
        if self.pst is not None:
            self.phase_dsems.append(key)
        return key

    @contextlib.contextmanager
    def phase(self):
        assert self.pst is None
        self.pst = contextlib.ExitStack()
        self.phase_dsems = []
        yield
        self.barrier()
        self.pst.close()
        self.pst = None
        self.free_dsems += self.phase_dsems
        self.phase_dsems = []

    def _wait(self, e, deps):
        best = {}
        for (s, v) in deps:
            if s == "pe" and e == "pe":
                continue
            if best.get(s, 0) < v:
                best[s] = v
        for s, v in best.items():
            if self.waited[e].get(s, 0) < v:
                self.eng[e].wait_ge(self.sems[s], v)
                self.waited[e][s] = v

    def op(self, e, fn, reads=(), writes=(), sig=True):
        deps = []
        for t in reads:
            deps += t.w
        for t in writes:
            deps += t.w
            deps += list(t.r.items())
        self._wait(e, deps)
        inst = fn(self.eng[e])
        if sig:
            self.cnt[e] += 1
            inst.then_inc(self.sems[e], 1)
            v = self.cnt[e]
        else:
            v = self.cnt[e] + 1
        for t in reads:
            if t.r.get(e, 0) < v:
                t.r[e] = v
        for t in writes:
            t.w = [(e, v)]
            t.r = {}
        return inst

    def dma(self, q, sem, out_ap, in_ap, dst=None, src=None, **kw):
        deps = []
        if src is not None:
            deps += src.w
        if dst is not None:
            deps += dst.w
            deps += list(dst.r.items())
        self._wait(q, deps)
        inst = self.eng[q].dma_start(out=out_ap, in_=in_ap, **kw)
        inst.then_inc(self.sems[sem], 16)
        self.cnt[sem] += 16
        v = self.cnt[sem]
        if src is not None and src.r.get(sem, 0) < v:
            src.r[sem] = v
        if dst is not None:
            dst.w = [(sem, v)]
            dst.r = {}
        return inst

    def allgather(self, sem, out_tk, in_tk, groups):
        deps = list(in_tk.w) + list(out_tk.w) + list(out_tk.r.items())
        self._wait("pool", deps)
        inst = self.nc.gpsimd.collective_compute("AllGather", op=ALU.bypass, replica_groups=groups,
                                                 ins=[in_tk[:]], outs=[out_tk[:]])
        inst.then_inc(self.sems[sem], 1)
        self.cnt[sem] += 1
        v = self.cnt[sem]
        in_tk.r[sem] = v
        out_tk.w = [(sem, v)]
        out_tk.r = {}

    def barrier(self):
        sp = self.eng["sp"]
        lazy = getattr(self, "lazy", set())
        for s, v in self.cnt.items():
            if s == "sp" or v == 0 or s in lazy:
                continue
            if self.waited["sp"].get(s, 0) < v:
                sp.wait_ge(self.sems[s], v)
                self.waited["sp"][s] = v
        self.cnt["sp"] += 1
        sp.nop().then_inc(self.sems["sp"], 1)
        for e in ("pe", "act", "dve", "pool"):
            self.eng[e].wait_ge(self.sems["sp"], self.cnt["sp"])
            for s, v in self.cnt.items():
                if s not in lazy:
                    self.waited[e][s] = v
        for s, v in self.cnt.items():
            if s not in lazy:
                self.waited["sp"][s] = v

    def close(self):
        self.es.close()


def ceil_div(a, b):
    return (a + b - 1) // b


class PsPool:
    def __init__(self, k, n, name="ps", shape=(128, 512), dt=F32):
        self.t = [k.ps("%s%d" % (name, i), shape, dt) for i in range(n)]
        self.i = 0

    def get(self):
        t = self.t[self.i % len(self.t)]
        self.i += 1
        return t


def load_rows(k, q, sem, dst_tk, dst_ap, src_tk, src_ap):
    k.dma(q, sem, dst_ap, src_ap.rearrange("(kc p) t -> p kc t", p=128), dst=dst_tk, src=src_tk)


def gemm(k, mode, KC, blocks, load_a, W, n0, N, NB, epi, pp, abufs=1, wbufs=2, tag="g", wq="sp", TBmax=None):
    TBmax = TBmax or max(b[1] for b in blocks)
    A = [k.sb(tag + "A%d" % i, [128, KC, TBmax], BF16) for i in range(abufs)]
    Asem = [k.dsem() for _ in range(abufs)]
    Wt = [k.sb(tag + "W%d" % i, [128, KC, NB], BF16) for i in range(wbufs)]
    Wsem = [k.dsem() for _ in range(wbufs)]
    wblocks = [(wb, min(NB, N - wb)) for wb in range(0, N, NB)]
    steps = [(bi, wi) for bi in range(len(blocks)) for wi in range(len(wblocks))]

    def issue_a(bi):
        tb0, ntb, _ = blocks[bi]
        load_a(A[bi % abufs], Asem[bi % abufs], tb0, ntb)

    def issue_w(si):
        bi, wi = steps[si]
        wb, ncols = wblocks[wi]
        wt = Wt[si % wbufs]
        if callable(W):
            wtk, wap = W(n0 + wb, ncols)
        else:
            wtk, wap = W, W[:, n0 + wb:n0 + wb + ncols]
        k.dma(wq, Wsem[si % wbufs], wt[:, :, :ncols], wap.rearrange("(kc p) n -> p kc n", p=128), dst=wt, src=wtk)

    issue_a(0)
    issue_w(0)
    for si, (bi, wi) in enumerate(steps):
        nxt_a = si + 1 < len(steps) and steps[si + 1][1] == 0
        if si + 1 < len(steps):
            if nxt_a and abufs >= 2:
                issue_a(steps[si + 1][0])
            issue_w(si + 1)
        tb0, ntb, subs = blocks[bi]
        wb, ncols = wblocks[wi]
        at = A[bi % abufs]
        wt = Wt[si % wbufs]
        if mode == "fm":
            for nch in range(0, ncols, 128):
                m = min(128, ncols - nch)
                for (ts, n) in subs:
                    ps = pp.get()
                    for kc in range(KC):
                        k.op("pe", lambda e: e.matmul(out=ps[:m, :n], lhsT=wt[:, kc, nch:nch + m],
                                                     rhs=at[:, kc, ts:ts + n], start=(kc == 0), stop=(kc == KC - 1)),
                             reads=[wt, at], writes=[ps], sig=(kc == KC - 1))
                    epi(ps, m, n, wb + nch, tb0 + ts, at=at, ts=ts)
        else:
            for (ts, mtok) in subs:
                ps = pp.get()
                for kc in range(KC):
                    k.op("pe", lambda e: e.matmul(out=ps[:mtok, :ncols], lhsT=at[:, kc, ts:ts + mtok],
                                                 rhs=wt[:, kc, :ncols], start=(kc == 0), stop=(kc == KC - 1)),
                         reads=[wt, at], writes=[ps], sig=(kc == KC - 1))
                epi(ps, mtok, ncols, tb0 + ts, wb, at=at, ts=ts)
        if nxt_a and abufs < 2:
            issue_a(steps[si + 1][0])


def mk_blocks(T, TB, sub, bounds=()):
    blocks = []
    cuts = sorted(set([0, T] + [b for b in bounds if 0 < b < T]))
    segs = [(cuts[i], cuts[i + 1]) for i in range(len(cuts) - 1)]
    t = 0
    while t < T:
        n = min(TB, T - t)
        subs = []
        for (s0, s1) in segs:
            a, b = max(s0, t), min(s1, t + n)
            x = a
            while x < b:
                c = min(sub, b - x)
                subs.append((x - t, c))
                x += c
        blocks.append((t, n, subs))
        t += n
    return blocks


class Stage:
    def __init__(self, k, n, name, shape, dt):
        self.t = [k.sb("%s%d" % (name, i), shape, dt) for i in range(n)]
        self.s = [k.dsem() for _ in range(n)]
        self.i = 0

    def get(self):
        j = self.i % len(self.t)
        self.i += 1
        return self.t[j], self.s[j]


def cast_rows(k, src, dst, R, N, cw=2048, q_in="sp", q_out="act", sc0=0, bufs=None):
    nb = 3
    if bufs is None:
        bufs = cast_bufs(k, cw)
    tin, tout, sin, sout, ctr = bufs
    i = ctr[0]
    for r0 in range(0, R, 128):
        rr = min(128, R - r0)
        for c0 in range(0, N, cw):
            cc = min(cw, N - c0)
            b = i % nb
            k.dma(q_in, sin[b], tin[b][:rr, :cc], src[r0:r0 + rr, sc0 + c0:sc0 + c0 + cc], dst=tin[b], src=src)
            if i % 2 == 0:
                k.op("act", lambda e: e.copy(out=tout[b][:rr, :cc], in_=tin[b][:rr, :cc]), reads=[tin[b]], writes=[tout[b]])
            else:
                k.op("dve", lambda e: e.tensor_copy(out=tout[b][:rr, :cc], in_=tin[b][:rr, :cc]), reads=[tin[b]], writes=[tout[b]])
            k.dma(q_out, sout[b], dst[r0:r0 + rr, c0:c0 + cc], tout[b][:rr, :cc], src=tout[b], dst=dst)
            i += 1
    ctr[0] = i


def cast_rows_pieces(k, src, pieces, R, N, w, cw=2048):
    tin, tout, sin, sout, ctr = cast_bufs(k, cw)
    nb = 3
    i = 0
    for r0 in range(0, R, 128):
        rr = min(128, R - r0)
        for c0 in range(0, N, cw):
            cc = min(cw, N - c0)
            b = i % nb
            k.dma("sp", sin[b], tin[b][:rr, :cc], src[r0:r0 + rr, c0:c0 + cc], dst=tin[b], src=src)
            if i % 2 == 0:
                k.op("act", lambda e: e.copy(out=tout[b][:rr, :cc], in_=tin[b][:rr, :cc]), reads=[tin[b]], writes=[tout[b]])
            else:
                k.op("dve", lambda e: e.tensor_copy(out=tout[b][:rr, :cc], in_=tin[b][:rr, :cc]), reads=[tin[b]], writes=[tout[b]])
            for p0 in range(c0, c0 + cc, w):
                pc = pieces[p0 // w]
                k.dma("act", sout[b], pc[r0:r0 + rr, :], tout[b][:rr, p0 - c0:p0 - c0 + w], src=tout[b], dst=pc)
            i += 1


def cast_bufs(k, cw=2048):
    nb = 3
    return ([k.sb("cin%d" % i, [128, cw], F32) for i in range(nb)], [k.sb("cout%d" % i, [128, cw], BF16) for i in range(nb)],
            [k.dsem() for _ in range(nb)], [k.dsem() for _ in range(nb)], [0])


def norm_mod(k, xT, hT, segs, gsc, sh, ones_bf, D, eps=1e-6, hT_col0=0, TBN=256, out_f32=None, writer=None):
    KC = D // 128
    nb = 2
    xt = [k.sb("nx%d" % i, [128, KC, TBN], F32) for i in range(nb)]
    xs = [k.dsem() for _ in range(nb)]
    sq = k.sb("nsq", [128, KC, TBN], BF16)
    ht = [k.sb("nh%d" % i, [128, KC, TBN], BF16 if out_f32 is None else F32) for i in range(nb)]
    hs = [k.dsem() for _ in range(nb)]
    rs = k.sb("nrs", [128, TBN], F32)
    pp = PsPool(k, 2, "nps")
    i = 0
    for (t0, n, st) in segs:
        for a in range(t0, t0 + n, TBN):
            m = min(TBN, t0 + n - a)
            b = i % nb
            x = xt[b]
            k.dma("sp", xs[b], x[:, :, :m], xT[:, a:a + m].rearrange("(kc p) t -> p kc t", p=128), dst=x, src=xT)
            k.op("act", lambda e: e.activation(out=sq[:, :, :m], in_=x[:, :, :m], func=AF.Square), reads=[x], writes=[sq])
            ps = pp.get()
            for kc in range(KC):
                k.op("pe", lambda e: e.matmul(out=ps[:, :m], lhsT=ones_bf[:, :], rhs=sq[:, kc, :m], start=(kc == 0), stop=(kc == KC - 1)),
                     reads=[sq, ones_bf], writes=[ps], sig=(kc == KC - 1))
            k.op("act", lambda e: e.activation(out=rs[:, :m], in_=ps[:, :m], func=AF.Sqrt, scale=1.0 / D, bias=k.eps_t[:, 0:1]), reads=[ps, k.eps_t], writes=[rs])
            k.op("dve", lambda e: e.reciprocal(out=rs[:, :m], in_=rs[:, :m]), reads=[rs], writes=[rs])
            k.op("dve", lambda e: e.tensor_tensor(out=x[:, :, :m], in0=x[:, :, :m], in1=rs[:, :m].unsqueeze(1).broadcast_to([128, KC, m]), op=ALU.mult),
                 reads=[x, rs], writes=[x])
            h = ht[b]
            k.op("dve", lambda e: e.tensor_tensor(out=x[:, :, :m], in0=x[:, :, :m], in1=gsc[st][:, :].unsqueeze(2).broadcast_to([128, KC, m]), op=ALU.mult),
                 reads=[x, gsc[st]], writes=[x])
            if sh is not None:
                k.op("pool", lambda e: e.tensor_tensor(out=h[:, :, :m], in0=x[:, :, :m], in1=sh[st][:, :].unsqueeze(2).broadcast_to([128, KC, m]), op=ALU.add),
                     reads=[x, sh[st]], writes=[h])
            else:
                k.op("act", lambda e: e.copy(out=h[:, :, :m], in_=x[:, :, :m]), reads=[x], writes=[h])
            dstT = hT if out_f32 is None else out_f32
            if writer is not None:
                writer(h, a, m, hs[b])
            else:
                k.dma("act", hs[b], dstT[:, hT_col0 + a:hT_col0 + a + m].rearrange("(kc p) t -> p kc t", p=128), h[:, :, :m], src=h, dst=dstT)
            i += 1


def mlstm_scan(k, C, qT, kT, ktm, vtm, gates, gb_bc, hdir, orders, NH):
    L = 128
    pg = k.ps("mg", [128, 512]); psc = k.ps("msc", [128, 512]); pden = k.ps("mden", [128, 512])
    pnum = [k.ps("mnum%d" % i, [128, 512]) for i in range(2)]
    pS = [k.ps("mS%d" % i, [128, 512]) for i in range(2)]
    S32 = {}; Sbf = {}; n32 = {}; nbf = {}
    for d in range(2):
        for h in range(NH):
            for dc in range(2):
                S32[d, h, dc] = k.sb("S32", [128, 512], F32); Sbf[d, h, dc] = S32[d, h, dc]
                n32[d, h, dc] = k.sb("n32", [128, 2], F32); nbf[d, h, dc] = n32[d, h, dc]
                for t in (S32[d, h, dc], n32[d, h, dc]):
                    k.op("dve", lambda e: e.memset(t[:, :], 0.0), writes=[t])
    nb = 2
    bufs = []
    for i in range(nb):
        bufs.append(dict(q=k.sb("mq", [128, NH * 2, L], F32), kt=k.sb("mkt", [128, NH * 2, L], F32),
                         km=k.sb("mkm", [128, NH * 256], F32), v=k.sb("mv", [128, NH * 512], F32),
                         g=k.sb("mgt", [128, 4 * NH], F32), sem=k.dsem()))
    G = k.sb("mG", [128, 2 * NH], F32); nlf = k.sb("mnlf", [128, NH], F32); lf = k.sb("mlf", [128, NH], F32)
    Fc = k.sb("mFc", [128, NH], F32); aa = k.sb("maa", [128, NH], F32)
    ly0 = k.sb("mly0", [128, NH], F32); lt = k.sb("mlt", [128, NH], F32)
    Lt = [k.sb("mLt%d" % i, [128, 128], F32) for i in range(2)]
    Dm = [k.sb("mDm%d" % i, [128, 128], F32) for i in range(2)]
    Wm = [k.sb("mWm%d" % i, [128, 128], F32) for i in range(2)]
    EF = [k.sb("mEF%d" % i, [128, 128], F32) for i in range(2)]
    qs = [k.sb("mqs%d" % i, [128, 2, 128], F32) for i in range(2)]
    scw = [k.sb("mscw%d" % i, [128, 128], F32) for i in range(2)]
    kw = [k.sb("mkw%d" % i, [128, 256], F32) for i in range(2)]
    wsrc = [k.sb("mws%d" % i, [128, 1], F32) for i in range(2)]
    rd = [k.sb("mrd%d" % i, [128, 1], F32) for i in range(2)]
    ho = Stage(k, 3, "mho", [128, 512], F32)
    it = 0
    nsteps = len(orders[0])
    for s in range(nsteps):
        for d in range(2):
            ci = orders[d][s]
            t0 = ci * L
            B = bufs[it % nb]; it += 1
            sem = B["sem"]
            k.dma("sp", sem, B["q"][:, :, :], qT[:, t0:t0 + L].rearrange("(c p) t -> p c t", p=128), dst=B["q"], src=qT)
            k.dma("sp", sem, B["kt"][:, :, :], kT[:, t0:t0 + L].rearrange("(c p) t -> p c t", p=128), dst=B["kt"], src=kT)
            k.dma("act", sem, B["km"][:, :], ktm[t0:t0 + L, :], dst=B["km"], src=ktm)
            k.dma("act", sem, B["v"][:, :], vtm[t0:t0 + L, :], dst=B["v"], src=vtm)
            k.dma("sp", sem, B["g"][:, :], gates[t0:t0 + L, :], dst=B["g"], src=gates)
            lastw = B["g"].w
            for nm in ("q", "kt", "km", "v"):
                B[nm].w = list(lastw)
            tri, mm = C["tri"][d], C["mm"][d]
            lc = L - 1 if d == 0 else 0
            k.op("dve", lambda e: e.tensor_tensor(out=G[:, :], in0=B["g"][:, d * 2 * NH:(d + 1) * 2 * NH], in1=gb_bc[:, d * 2 * NH:(d + 1) * 2 * NH], op=ALU.add),
                 reads=[B["g"], gb_bc], writes=[G])
            k.op("act", lambda e: e.activation(out=nlf[:, :], in_=G[:, NH:2 * NH], func=AF.Exp, scale=-1.0), reads=[G], writes=[nlf])
            k.op("act", lambda e: e.activation(out=ly0[:, :], in_=nlf[:, :], func=AF.Ln, bias=C["one_c"][:, 0:1], scale=1.0), reads=[nlf, C["one_c"]], writes=[ly0])
            k.op("act", lambda e: e.activation(out=lt[:, :], in_=ly0[:, :], func=AF.Exp, scale=-1.0), reads=[ly0], writes=[lt])
            k.op("dve", lambda e: e.scalar_tensor_tensor(out=lt[:, :], in0=nlf[:, :], scalar=1.0, in1=lt[:, :], op0=ALU.add, op1=ALU.mult), reads=[nlf, lt], writes=[lt])
            k.op("dve", lambda e: e.scalar_tensor_tensor(out=nlf[:, :], in0=lt[:, :], scalar=-1.0, in1=ly0[:, :], op0=ALU.add, op1=ALU.add), reads=[lt, ly0], writes=[nlf])
            k.op("dve", lambda e: e.tensor_scalar(out=lf[:, :], in0=nlf[:, :], scalar1=-1.0, scalar2=None, op0=ALU.mult), reads=[nlf], writes=[lf])
            k.op("pe", lambda e: e.matmul(out=pg[:, 384:384 + NH], lhsT=tri[:, :], rhs=lf[:, :], start=True, stop=True), reads=[tri, lf], writes=[pg])
            k.op("dve", lambda e: e.tensor_copy(out=Fc[:, :], in_=pg[:, 384:384 + NH]), reads=[pg], writes=[Fc])
            k.op("dve", lambda e: e.tensor_tensor(out=aa[:, :], in0=G[:, 0:NH], in1=Fc[:, :], op=ALU.subtract), reads=[G, Fc], writes=[aa])
            for h in range(NH):
                j = h % 2
                k.op("dve", lambda e: e.tensor_scalar(out=Lt[j][:, :], in0=C["ones_f"][:, :], scalar1=lf[:, h:h + 1], scalar2=None, op0=ALU.mult),
                     reads=[C["ones_f"], lf], writes=[Lt[j]])
                fr = pg[:, h * 128:(h + 1) * 128]
                k.op("pe", lambda e: e.matmul(out=fr, lhsT=Lt[j][:, :], rhs=tri[:, :], start=True, stop=True), reads=[Lt[j], tri], writes=[pg])
                k.op("dve", lambda e: e.scalar_tensor_tensor(out=Dm[j][:, :], in0=fr, scalar=Fc[:, h:h + 1], in1=mm[:, :], op0=ALU.subtract, op1=ALU.min),
                     reads=[pg, Fc, mm], writes=[Dm[j]])
                k.op("act", lambda e: e.activation(out=Wm[j][:, :], in_=Dm[j][:, :], func=AF.Exp, bias=G[:, h:h + 1], scale=1.0), reads=[Dm[j], G], writes=[Wm[j]])
                k.op("act", lambda e: e.activation(out=EF[j][:, :], in_=fr, func=AF.Exp), reads=[pg], writes=[EF[j]])
                k.op("act", lambda e: e.activation(out=wsrc[j][:, :], in_=pg[:, h * 128 + lc:h * 128 + lc + 1], func=AF.Exp, bias=aa[:, h:h + 1], scale=1.0),
                     reads=[pg, aa], writes=[wsrc[j]])
                k.op("dve", lambda e: e.tensor_tensor(out=qs[j][:, :, :], in0=B["q"][:, 2 * h:2 * h + 2, :], in1=EF[j][:, :].unsqueeze(1).broadcast_to([128, 2, 128]), op=ALU.mult),
                     reads=[B["q"], EF[j]], writes=[qs[j]])
                sc = psc[:, h * 128:(h + 1) * 128]
                for dc in range(2):
                    k.op("pe", lambda e: e.matmul(out=sc, lhsT=B["kt"][:, 2 * h + dc, :], rhs=B["q"][:, 2 * h + dc, :], start=(dc == 0), stop=(dc == 1)),
                         reads=[B["kt"], B["q"]], writes=[psc], sig=(dc == 1))
                k.op("dve", lambda e: e.tensor_tensor(out=scw[j][:, :], in0=sc, in1=Wm[j][:, :], op=ALU.mult), reads=[psc, Wm[j]], writes=[scw[j]])
                pn = pnum[h % 2]
                vv = B["v"][:, h * 512:(h + 1) * 512]
                k.op("pe", lambda e: e.matmul(out=pn[:, :], lhsT=scw[j][:, :], rhs=vv, start=True, stop=False), reads=[scw[j], B["v"]], writes=[pn], sig=False)
                for dc in range(2):
                    k.op("pe", lambda e: e.matmul(out=pn[:, :], lhsT=qs[j][:, dc, :], rhs=Sbf[d, h, dc][:, :], start=False, stop=(dc == 1)),
                         reads=[qs[j], Sbf[d, h, dc]], writes=[pn], sig=(dc == 1))
                dn = pden[:, 2 * h:2 * h + 2]
                k.op("pe", lambda e: e.matmul(out=dn, lhsT=scw[j][:, :], rhs=C["ones_f"][:, 0:2], start=True, stop=False), reads=[scw[j], C["ones_f"]], writes=[pden], sig=False)
                for dc in range(2):
                    k.op("pe", lambda e: e.matmul(out=dn, lhsT=qs[j][:, dc, :], rhs=nbf[d, h, dc][:, :], start=False, stop=(dc == 1)),
                         reads=[qs[j], nbf[d, h, dc]], writes=[pden], sig=(dc == 1))
                k.op("act", lambda e: e.activation(out=rd[j][:, :], in_=pden[:, 2 * h:2 * h + 1], func=AF.Abs), reads=[pden], writes=[rd[j]])
                k.op("dve", lambda e: e.tensor_scalar(out=rd[j][:, :], in0=rd[j][:, :], scalar1=1.0, scalar2=None, op0=ALU.max), reads=[rd[j]], writes=[rd[j]])
                k.op("dve", lambda e: e.reciprocal(out=rd[j][:, :], in_=rd[j][:, :]), reads=[rd[j]], writes=[rd[j]])
                st, ssem = ho.get()
                k.op("act", lambda e: e.activation(out=st[:, :], in_=pn[:, :], func=AF.Copy, scale=rd[j][:, 0:1]), reads=[pn, rd[j]], writes=[st])
                k.dma("sp", ssem, hdir[d][t0:t0 + L, h * 512:(h + 1) * 512], st[:, :], src=st, dst=hdir[d])
                k.op("pool", lambda e: e.tensor_scalar(out=kw[j][:, :], in0=B["km"][:, h * 256:(h + 1) * 256], scalar1=wsrc[j][:, 0:1], scalar2=None, op0=ALU.mult),
                     reads=[B["km"], wsrc[j]], writes=[kw[j]])
                for dc in range(2):
                    k.op("pe", lambda e: e.matmul(out=pS[dc][:, :], lhsT=kw[j][:, dc * 128:(dc + 1) * 128], rhs=vv, start=True, stop=True), reads=[kw[j], B["v"]], writes=[pS[dc]])
                    nn = pden[:, 16 + 4 * h + 2 * dc:16 + 4 * h + 2 * dc + 2]
                    k.op("pe", lambda e: e.matmul(out=nn, lhsT=kw[j][:, dc * 128:(dc + 1) * 128], rhs=C["ones_f"][:, 0:2], start=True, stop=True), reads=[kw[j], C["ones_f"]], writes=[pden])
                    dec = EF[j][:, lc:lc + 1]
                    k.op("dve", lambda e: e.scalar_tensor_tensor(out=S32[d, h, dc][:, :], in0=S32[d, h, dc][:, :], scalar=dec, in1=pS[dc][:, :], op0=ALU.mult, op1=ALU.add),
                         reads=[S32[d, h, dc], EF[j], pS[dc]], writes=[S32[d, h, dc]])
                    k.op("dve", lambda e: e.scalar_tensor_tensor(out=n32[d, h, dc][:, :], in0=n32[d, h, dc][:, :], scalar=dec, in1=nn, op0=ALU.mult, op1=ALU.add),
                         reads=[n32[d, h, dc], EF[j], pden], writes=[n32[d, h, dc]])


def host_consts():
    idx = np.arange(128)
    tri_f = (idx[:, None] <= idx[None, :]).astype(np.float32)
    tri_b = tri_f.T.copy()
    mm_f = (tri_f - 1.0) * 30000.0
    mm_b = (tri_b - 1.0) * 30000.0
    ident = np.eye(128, dtype=np.float32)
    return np.concatenate([tri_f, tri_b, mm_f, mm_b, ident], axis=1).astype(np.float32)


def load_consts(k, cdram):
    C = {}
    ct = k.sb("consts", [128, 640], F32)
    s = k.dsem()
    k.dma("sp", s, ct[:, :], cdram[:, :], dst=ct)
    C["raw"] = ct

    class View:
        def __init__(self, tk, a, b):
            self.tk = tk; self.a = a; self.b = b;
        def __getitem__(self, idx):
            return self.tk.t[:, self.a:self.b][idx]
    def view(a, b):
        v = Tk.__new__(Tk)
        v.t = _Sub(ct, a, b); v.w = ct.w; v.r = ct.r; v.name = "cv"
        return v
    C["tri"] = [view(0, 128), view(128, 256)]
    C["mm"] = [view(256, 384), view(384, 512)]
    C["ident"] = view(512, 640)
    of = k.sb("ones_f", [128, 128], F32); ob = k.sb("ones_bf", [128, 128], BF16); oc = k.sb("one_c", [128, 1], F32)
    ib = k.sb("ident_bf", [128, 128], BF16); ep = k.sb("eps_t", [128, 1], F32)
    k.op("dve", lambda e: e.memset(of[:, :], 1.0), writes=[of])
    k.op("dve", lambda e: e.memset(ob[:, :], 1.0), writes=[ob])
    k.op("dve", lambda e: e.memset(oc[:, :], 1.0), writes=[oc])
    k.op("dve", lambda e: e.memset(ep[:, :], 1e-6), writes=[ep])
    k.op("dve", lambda e: e.tensor_copy(out=ib[:, :], in_=ct[:, 512:640]), reads=[ct], writes=[ib])
    C["ones_f"] = of; C["ones_bf"] = ob; C["one_c"] = oc; C["ident_bf"] = ib
    k.eps_t = ep
    return C


class _Sub:
    def __init__(self, tk, a, b):
        self.tk = tk; self.a = a; self.b = b

    def __getitem__(self, idx):
        return self.tk.t[:, self.a:self.b][idx]


TWO_PI = 6.283185307179586


def _cmul_sc(k, eng, o_re, o_im, a_re, a_im, p_re, p_im, tmp, tks_r, tks_w):
    k.op(eng, lambda e: e.tensor_scalar(out=tmp, in0=a_im, scalar1=p_im, scalar2=None, op0=ALU.mult), reads=tks_r, writes=tks_w)
    k.op("dve", lambda e: e.scalar_tensor_tensor(out=o_re, in0=a_re, scalar=p_re, in1=tmp, op0=ALU.mult, op1=ALU.subtract), reads=tks_r + tks_w, writes=tks_w)
    k.op(eng, lambda e: e.tensor_scalar(out=tmp, in0=a_im, scalar1=p_re, scalar2=None, op0=ALU.mult), reads=tks_r + tks_w, writes=tks_w)
    k.op("dve", lambda e: e.scalar_tensor_tensor(out=o_im, in0=a_re, scalar=p_im, in1=tmp, op0=ALU.mult, op1=ALU.add), reads=tks_r + tks_w, writes=tks_w)


def s5_phase(k, C, uT, par, Btre_d, Btim_d, Ctre_d, Ctim_d, dsk_d, outT, out_row0, T, NG, chunks):
    NST = NG // 2
    NCC = NG * 16 // 128
    s0 = k.dsem()
    ub = k.sb("s5ub", [128, NCC, T], BF16)
    uft = [k.sb("s5uft%d" % i, [128, 1088], F32) for i in range(2)]
    ufs = [k.dsem() for _ in range(2)]
    iu = 0
    for c_ in range(NCC):
        for a_ in range(0, T, 1088):
            n_ = min(1088, T - a_)
            tt = uft[iu % 2]
            k.dma("sp", ufs[iu % 2], tt[:, :n_], uT[c_ * 128:(c_ + 1) * 128, a_:a_ + n_], dst=tt, src=uT)
            if iu % 2 == 0:
                k.op("act", lambda e: e.copy(out=ub[:, c_, a_:a_ + n_], in_=tt[:, :n_]), reads=[tt], writes=[ub])
            else:
                k.op("dve", lambda e: e.tensor_copy(out=ub[:, c_, a_:a_ + n_], in_=tt[:, :n_]), reads=[tt], writes=[ub])
            iu += 1
    P = k.sb("s5P", [128, 3, 2 * NST], F32)
    k.dma("act", s0, P[:, :, :], par[:, :, :], dst=P)
    Bre32 = k.sb("s5Bre32", [128, NST, 128], F32); Bim32 = k.sb("s5Bim32", [128, NST, 128], F32)
    Cre32 = k.sb("s5Cre32", [128, NST, 64], F32); Cim32 = k.sb("s5Cim32", [128, NST, 64], F32)
    dsk = k.sb("s5dsk", [128, NCC], F32)
    k.dma("sp", s0, Bre32[:, :, :], Btre_d[:, :, :], dst=Bre32); k.dma("sp", s0, Bim32[:, :, :], Btim_d[:, :, :], dst=Bim32)
    k.dma("act", s0, Cre32[:, :, :], Ctre_d[:, :, :], dst=Cre32); k.dma("act", s0, Cim32[:, :, :], Ctim_d[:, :, :], dst=Cim32)
    k.dma("sp", s0, dsk[:, :], dsk_d[:, :], dst=dsk)
    for t in (P, Bre32, Bim32, Cre32, Cim32, dsk):
        t.w = list(dsk.w)
    Bre = k.sb("s5Bre", [128, NST, 128], BF16); Bim = k.sb("s5Bim", [128, NST, 128], BF16)
    Cre = k.sb("s5Cre", [128, NST, 64], BF16); Cimn = k.sb("s5Cimn", [128, NST, 64], BF16)
    k.op("dve", lambda e: e.tensor_copy(out=Bre[:, :, :], in_=Bre32[:, :, :]), reads=[Bre32], writes=[Bre])
    k.op("dve", lambda e: e.tensor_copy(out=Bim[:, :, :], in_=Bim32[:, :, :]), reads=[Bim32], writes=[Bim])
    k.op("dve", lambda e: e.tensor_copy(out=Cre[:, :, :], in_=Cre32[:, :, :]), reads=[Cre32], writes=[Cre])
    k.op("dve", lambda e: e.tensor_scalar(out=Cimn[:, :, :], in0=Cim32[:, :, :], scalar1=-1.0, scalar2=None, op0=ALU.mult), reads=[Cim32], writes=[Cimn])
    W = 2 * NST
    def tl(nm):
        return k.sb("s5" + nm, [128, W], F32)
    dt, zr, zi, rmag, sn, cs, tmp, tmp2, kf, abr, abi, fre, fim, den = (tl(n) for n in
        ("dt", "zr", "zi", "rmag", "sn", "cs", "tmp", "tmp2", "kf", "abr", "abi", "fre", "fim", "den"))
    ki = k.sb("s5ki", [128, W], mybir.dt.int32)
    are, aim, ldt = P[:, 0, :], P[:, 1, :], P[:, 2, :]
    k.op("act", lambda e: e.activation(out=dt[:, :], in_=ldt, func=AF.Exp), reads=[P], writes=[dt])
    k.op("dve", lambda e: e.tensor_tensor(out=zr[:, :], in0=are, in1=dt[:, :], op=ALU.mult), reads=[P, dt], writes=[zr])
    k.op("dve", lambda e: e.tensor_tensor(out=zi[:, :], in0=aim, in1=dt[:, :], op=ALU.mult), reads=[P, dt], writes=[zi])
    k.op("act", lambda e: e.activation(out=rmag[:, :], in_=zr[:, :], func=AF.Exp), reads=[zr], writes=[rmag])

    def sin_of(dst, shift):
        k.op("dve", lambda e: e.tensor_scalar(out=tmp[:, :], in0=zi[:, :], scalar1=shift, scalar2=None, op0=ALU.add), reads=[zi], writes=[tmp])
        k.op("dve", lambda e: e.tensor_scalar(out=kf[:, :], in0=tmp[:, :], scalar1=1.0 / TWO_PI, scalar2=None, op0=ALU.mult), reads=[tmp], writes=[kf])
        k.op("dve", lambda e: e.tensor_copy(out=ki[:, :], in_=kf[:, :]), reads=[kf], writes=[ki])
        k.op("dve", lambda e: e.tensor_copy(out=kf[:, :], in_=ki[:, :]), reads=[ki], writes=[kf])
        k.op("dve", lambda e: e.scalar_tensor_tensor(out=tmp[:, :], in0=kf[:, :], scalar=-TWO_PI, in1=tmp[:, :], op0=ALU.mult, op1=ALU.add), reads=[kf, tmp], writes=[tmp])
        k.op("dve", lambda e: e.tensor_scalar(out=tmp2[:, :], in0=tmp[:, :], scalar1=3.141592653589793, scalar2=None, op0=ALU.is_gt), reads=[tmp], writes=[tmp2])
        k.op("dve", lambda e: e.scalar_tensor_tensor(out=tmp[:, :], in0=tmp2[:, :], scalar=-TWO_PI, in1=tmp[:, :], op0=ALU.mult, op1=ALU.add), reads=[tmp2, tmp], writes=[tmp])
        k.op("dve", lambda e: e.tensor_scalar(out=tmp2[:, :], in0=tmp[:, :], scalar1=-3.141592653589793, scalar2=None, op0=ALU.is_lt), reads=[tmp], writes=[tmp2])
        k.op("dve", lambda e: e.scalar_tensor_tensor(out=tmp[:, :], in0=tmp2[:, :], scalar=TWO_PI, in1=tmp[:, :], op0=ALU.mult, op1=ALU.add), reads=[tmp2, tmp], writes=[tmp])
        k.op("act", lambda e: e.activation(out=dst[:, :], in_=tmp[:, :], func=AF.Sin), reads=[tmp], writes=[dst])
    sin_of(sn, 0.0)
    sin_of(cs, 1.5707963267948966)
    k.op("dve", lambda e: e.tensor_tensor(out=abr[:, :], in0=rmag[:, :], in1=cs[:, :], op=ALU.mult), reads=[rmag, cs], writes=[abr])
    k.op("dve", lambda e: e.tensor_tensor(out=abi[:, :], in0=rmag[:, :], in1=sn[:, :], op=ALU.mult), reads=[rmag, sn], writes=[abi])
    k.op("dve", lambda e: e.tensor_scalar(out=abr[:, :], in0=abr[:, :], scalar1=-1.0, scalar2=None, op0=ALU.add), reads=[abr], writes=[abr])
    k.op("dve", lambda e: e.tensor_tensor(out=den[:, :], in0=are, in1=are, op=ALU.mult), reads=[P], writes=[den])
    k.op("dve", lambda e: e.tensor_tensor(out=tmp[:, :], in0=aim, in1=aim, op=ALU.mult), reads=[P], writes=[tmp])
    k.op("dve", lambda e: e.tensor_tensor(out=den[:, :], in0=den[:, :], in1=tmp[:, :], op=ALU.add), reads=[den, tmp], writes=[den])
    k.op("dve", lambda e: e.reciprocal(out=den[:, :], in_=den[:, :]), reads=[den], writes=[den])
    k.op("dve", lambda e: e.tensor_tensor(out=fre[:, :], in0=abr[:, :], in1=are, op=ALU.mult), reads=[abr, P], writes=[fre])
    k.op("dve", lambda e: e.tensor_tensor(out=tmp[:, :], in0=abi[:, :], in1=aim, op=ALU.mult), reads=[abi, P], writes=[tmp])
    k.op("dve", lambda e: e.tensor_tensor(out=fre[:, :], in0=fre[:, :], in1=tmp[:, :], op=ALU.add), reads=[fre, tmp], writes=[fre])
    k.op("dve", lambda e: e.tensor_tensor(out=fre[:, :], in0=fre[:, :], in1=den[:, :], op=ALU.mult), reads=[fre, den], writes=[fre])
    k.op("dve", lambda e: e.tensor_tensor(out=fim[:, :], in0=abi[:, :], in1=are, op=ALU.mult), reads=[abi, P], writes=[fim])
    k.op("dve", lambda e: e.tensor_tensor(out=tmp[:, :], in0=abr[:, :], in1=aim, op=ALU.mult), reads=[abr, P], writes=[tmp])
    k.op("dve", lambda e: e.tensor_tensor(out=fim[:, :], in0=fim[:, :], in1=tmp[:, :], op=ALU.subtract), reads=[fim, tmp], writes=[fim])
    k.op("dve", lambda e: e.tensor_tensor(out=fim[:, :], in0=fim[:, :], in1=den[:, :], op=ALU.mult), reads=[fim, den], writes=[fim])
    nsn = tl("nsn")
    k.op("dve", lambda e: e.tensor_scalar(out=nsn[:, :], in0=sn[:, :], scalar1=-1.0, scalar2=None, op0=ALU.mult), reads=[sn], writes=[nsn])
    TC = 512
    Rre = k.sb("s5Rre", [128, TC], F32); Rim = k.sb("s5Rim", [128, TC], F32)
    Ere = k.sb("s5Ere", [128, TC], F32); Eim = k.sb("s5Eim", [128, TC], F32)
    sc1 = k.sb("s5sc1", [128, TC], F32)
    Sre = k.sb("s5Sre", [128, T], F32); Sim = k.sb("s5Sim", [128, T], F32)
    Sreb = [k.sb("s5Sreb%d" % i, [128, T], BF16) for i in range(2)]; Simb = [k.sb("s5Simb%d" % i, [128, T], BF16) for i in range(2)]
    wre = k.sb("s5wre", [128, TC], F32); wim = k.sb("s5wim", [128, TC], F32)
    zre = k.sb("s5zre", [128, TC], F32); zim = k.sb("s5zim", [128, TC], F32)
    t1 = k.sb("s5t1", [128, TC], F32); t2 = k.sb("s5t2", [128, TC], F32)
    t3 = k.sb("s5t3", [128, TC], F32); t4 = k.sb("s5t4", [128, TC], F32)
    ire = k.sb("s5ire", [128, 1], F32); iim = k.sb("s5iim", [128, 1], F32)
    pbr = [k.ps("s5pbr%d" % i, [128, 512]) for i in range(2)]
    pbi = [k.ps("s5pbi%d" % i, [128, 512]) for i in range(2)]
    py = [k.ps("s5py%d" % i, [128, 512]) for i in range(2)]
    yst = Stage(k, 2, "s5yo", [128, 512], BF16)
    ust5 = Stage(k, 2, "s5uu", [128, 512], F32)
    ya = k.sb("s5ya", [128, 512], F32); yb = k.sb("s5yb", [128, 512], F32); yc = k.sb("s5yc", [128, 512], F32)
    ic = 0
    for st in range(NST):
        cc, jh, jl = st // 4, (st // 2) % 2, st % 2
        pr = slice(64 * jh, 64 * jh + 64)
        for d in range(2):
            col = d * NST + st
            k.op("dve", lambda e: e.tensor_copy(out=Rre[:, 0:1], in_=cs[:, col:col + 1]), reads=[cs], writes=[Rre])
            k.op("dve", lambda e: e.tensor_copy(out=Rim[:, 0:1], in_=nsn[:, col:col + 1]), reads=[nsn], writes=[Rim])
            m = 1
            while m < TC:
                _cmul_sc(k, "pool", Rre[:, m:2 * m], Rim[:, m:2 * m], Rre[:, 0:m], Rim[:, 0:m], Rre[:, m - 1:m], Rim[:, m - 1:m], sc1[:, 0:m], [Rre, Rim], [Rre, Rim, sc1])
                m *= 2
            _cmul_sc(k, "pool", Ere[:, :], Eim[:, :], Rre[:, :], Rim[:, :], fre[:, col:col + 1], fim[:, col:col + 1], sc1[:, :], [Rre, Rim, fre, fim], [Ere, Eim, sc1])
            k.op("dve", lambda e: e.memset(ire[:, :], 0.0), writes=[ire])
            k.op("dve", lambda e: e.memset(iim[:, :], 0.0), writes=[iim])
            for (t0, n) in chunks[d]:
                pb_r, pb_i = pbr[ic % 2], pbi[ic % 2]; ic += 1
                k.op("pe", lambda e: e.matmul(out=pb_r[:, :n], lhsT=Bre[pr, st, :], rhs=ub[pr, cc, t0:t0 + n], start=True, stop=True), reads=[Bre, ub], writes=[pb_r])
                k.op("pe", lambda e: e.matmul(out=pb_i[:, :n], lhsT=Bim[pr, st, :], rhs=ub[pr, cc, t0:t0 + n], start=True, stop=True), reads=[Bim, ub], writes=[pb_i])
                if d == 0:
                    br, bi = pb_r[:, 0:n], pb_i[:, 0:n]
                else:
                    br, bi = pb_r[:, n - 1::-1] if n > 1 else pb_r[:, 0:1], pb_i[:, n - 1::-1]
                k.op("dve", lambda e: e.tensor_tensor(out=t1[:, :n], in0=Ere[:, :n], in1=br, op=ALU.mult), reads=[Ere, pb_r], writes=[t1])
                k.op("dve", lambda e: e.tensor_tensor(out=t2[:, :n], in0=Eim[:, :n], in1=bi, op=ALU.mult), reads=[Eim, pb_i], writes=[t2])
                k.op("dve", lambda e: e.tensor_tensor(out=t3[:, :n], in0=Ere[:, :n], in1=bi, op=ALU.mult), reads=[Ere, pb_i], writes=[t3])
                k.op("dve", lambda e: e.tensor_tensor(out=t4[:, :n], in0=Eim[:, :n], in1=br, op=ALU.mult), reads=[Eim, pb_r], writes=[t4])
                k.op("pool", lambda e: e.tensor_tensor(out=wre[:, :n], in0=t1[:, :n], in1=t2[:, :n], op=ALU.subtract), reads=[t1, t2], writes=[wre])
                k.op("pool", lambda e: e.tensor_tensor(out=wim[:, :n], in0=t3[:, :n], in1=t4[:, :n], op=ALU.add), reads=[t3, t4], writes=[wim])
                rb = rmag[:, col:col + 1].broadcast_to([128, n])
                k.op("dve", lambda e: e.tensor_tensor_scan(out=zre[:, :n], data0=rb, data1=wre[:, :n], initial=ire[:, 0:1], op0=ALU.mult, op1=ALU.add), reads=[rmag, wre, ire], writes=[zre])
                k.op("dve", lambda e: e.tensor_tensor_scan(out=zim[:, :n], data0=rb, data1=wim[:, :n], initial=iim[:, 0:1], op0=ALU.mult, op1=ALU.add), reads=[rmag, wim, iim], writes=[zim])
                k.op("pool", lambda e: e.tensor_tensor(out=t1[:, :n], in0=Rre[:, :n], in1=zre[:, :n], op=ALU.mult), reads=[Rre, zre], writes=[t1])
                k.op("pool", lambda e: e.tensor_tensor(out=t2[:, :n], in0=Rim[:, :n], in1=zim[:, :n], op=ALU.mult), reads=[Rim, zim], writes=[t2])
                k.op("pool", lambda e: e.tensor_tensor(out=t3[:, :n], in0=Rre[:, :n], in1=zim[:, :n], op=ALU.mult), reads=[Rre, zim], writes=[t3])
                k.op("pool", lambda e: e.tensor_tensor(out=t4[:, :n], in0=Rim[:, :n], in1=zre[:, :n], op=ALU.mult), reads=[Rim, zre], writes=[t4])
                k.op("pool", lambda e: e.tensor_tensor(out=t1[:, :n], in0=t1[:, :n], in1=t2[:, :n], op=ALU.add), reads=[t1, t2], writes=[t1])
                k.op("pool", lambda e: e.tensor_tensor(out=t3[:, :n], in0=t3[:, :n], in1=t4[:, :n], op=ALU.subtract), reads=[t3, t4], writes=[t3])
                k.op("dve", lambda e: e.tensor_copy(out=ire[:, :], in_=t1[:, n - 1:n]), reads=[t1], writes=[ire])
                k.op("dve", lambda e: e.tensor_copy(out=iim[:, :], in_=t3[:, n - 1:n]), reads=[t3], writes=[iim])
                if d == 0:
                    k.op("act", lambda e: e.copy(out=Sre[:, t0:t0 + n], in_=t1[:, :n]), reads=[t1], writes=[Sre])
                    k.op("act", lambda e: e.copy(out=Sim[:, t0:t0 + n], in_=t3[:, :n]), reads=[t3], writes=[Sim])
                else:
                    k.op("pool", lambda e: e.tensor_tensor(out=Sre[:, t0:t0 + n], in0=Sre[:, t0:t0 + n], in1=t1[:, n - 1::-1], op=ALU.add), reads=[Sre, t1], writes=[Sre])
                    k.op("pool", lambda e: e.tensor_tensor(out=Sim[:, t0:t0 + n], in0=Sim[:, t0:t0 + n], in1=t3[:, n - 1::-1], op=ALU.add), reads=[Sim, t3], writes=[Sim])
        k.op("act", lambda e: e.copy(out=Sreb[jl][:, :], in_=Sre[:, :]), reads=[Sre], writes=[Sreb[jl]])
        k.op("act", lambda e: e.copy(out=Simb[jl][:, :], in_=Sim[:, :]), reads=[Sim], writes=[Simb[jl]])
        if jl == 0:
            continue
        for a in range(0, T, 512):
            n = min(512, T - a)
            p = py[(a // 512) % 2]
            for q in range(2):
                stq = st - 1 + q
                k.op("pe", lambda e: e.matmul(out=p[pr, :n], lhsT=Cre[:, stq, :], rhs=Sreb[q][:, a:a + n], start=(q == 0), stop=False), reads=[Cre, Sreb[q]], writes=[p], sig=False)
                k.op("pe", lambda e: e.matmul(out=p[pr, :n], lhsT=Cimn[:, stq, :], rhs=Simb[q][:, a:a + n], start=False, stop=(q == 1)), reads=[Cimn, Simb[q]], writes=[p], sig=(q == 1))
            uu, usem_ = ust5.get()
            k.dma("act", usem_, uu[pr, :n], uT[cc * 128 + 64 * jh:cc * 128 + 64 * jh + 64, a:a + n], dst=uu, src=uT)
            k.op("dve", lambda e: e.scalar_tensor_tensor(out=ya[pr, :n], in0=uu[pr, :n], scalar=dsk[pr, cc:cc + 1], in1=p[pr, :n], op0=ALU.mult, op1=ALU.add),
                 reads=[uu, dsk, p], writes=[ya])
            o, osem = yst.get()
            gelu_tanh(k, ya, yb, yc, o, pr, n)
            r0 = out_row0 + 128 * cc + 64 * jh
            if isinstance(outT, list):
                for a2 in range(a, a + n, 128):
                    op_ = outT[a2 // 128]
                    k.dma("sp", osem, op_[r0:r0 + 64, :], o[pr, a2 - a:a2 - a + 128], src=o, dst=op_)
            else:
                k.dma("sp", osem, outT[r0:r0 + 64, a:a + n], o[pr, :n], src=o, dst=outT)


def gelu_tanh(k, y, b1, b2, o, pr, n, eng2="pool"):
    k.op("dve", lambda e: e.tensor_tensor(out=b1[pr, :n], in0=y[pr, :n], in1=y[pr, :n], op=ALU.mult), reads=[y], writes=[b1])
    k.op("dve", lambda e: e.tensor_scalar(out=b1[pr, :n], in0=b1[pr, :n], scalar1=0.044715, scalar2=1.0, op0=ALU.mult, op1=ALU.add), reads=[b1], writes=[b1])
    k.op(eng2, lambda e: e.tensor_tensor(out=b1[pr, :n], in0=b1[pr, :n], in1=y[pr, :n], op=ALU.mult), reads=[b1, y], writes=[b1])
    k.op("act", lambda e: e.activation(out=b2[pr, :n], in_=b1[pr, :n], func=AF.Sigmoid, scale=1.5957691216057308), reads=[b1], writes=[b2])
    k.op(eng2, lambda e: e.tensor_tensor(out=o[pr, :n], in0=b2[pr, :n], in1=y[pr, :n], op=ALU.mult), reads=[b2, y], writes=[o])


def s5_host_layout(a_re, a_im, log_dt, b_re, b_im, c_re, c_im, d_skip):
    NG = a_re.shape[1]; NST = NG // 2; NCC = NG * 16 // 128
    par = np.zeros((128, 3, 2 * NST), np.float32)
    Bt = np.zeros((2, 128, NST, 128), np.float32)
    Ct = np.zeros((2, 128, NST, 64), np.float32)
    for st in range(NST):
        jh, jl = (st // 2) % 2, st % 2
        for gl in range(2):
            g = 2 * st + gl
            for d in range(2):
                par[64 * gl:64 * gl + 64, 0, d * NST + st] = a_re[d, g]
                par[64 * gl:64 * gl + 64, 1, d * NST + st] = a_im[d, g]
                par[64 * gl:64 * gl + 64, 2, d * NST + st] = log_dt[d, g]
            for ri, (bb, cmat) in enumerate(((b_re, c_re), (b_im, c_im))):
                r0 = 64 * jh + 32 * jl + 16 * gl
                Bt[ri, r0:r0 + 16, st, 64 * gl:64 * gl + 64] = bb[g].T
                Ct[ri, 64 * gl:64 * gl + 64, st, 32 * jl + 16 * gl:32 * jl + 16 * gl + 16] = cmat[g].T
    dsk = np.ascontiguousarray(d_skip.reshape(NCC, 128).T).astype(np.float32)
    return par, Bt[0], Bt[1], Ct[0], Ct[1], dsk


def na_phase(k, C, QT, KT, V, Tg, cmask_d, flags_d, attnT, NHEADS, scale):
    NKT = 2816
    s0 = k.dsem()
    cm = k.sb("nacm", [64, 64], F32); fl = k.sb("nafl", [128, 4], F32)
    k.dma("sp", s0, cm[:, :], cmask_d[:, :], dst=cm)
    k.dma("sp", s0, fl[:, :], flags_d[:, :], dst=fl)
    cm.w = list(fl.w)
    nb = 2
    qh = [k.sb("naq%d" % i, [128, 2048], BF16) for i in range(nb)]
    kh = [k.sb("nak%d" % i, [128, NKT], BF16) for i in range(nb)]
    ve = [k.sb("nave%d" % i, [128, 22, 128], BF16) for i in range(nb)]
    vo = [k.sb("navo%d" % i, [128, 21, 128], BF16) for i in range(nb)]
    th = [k.sb("nath%d" % i, [64, 15, 64], F32) for i in range(nb)]
    ao = [k.sb("naao%d" % i, [128, 2048], BF16) for i in range(nb)]
    lsem = [k.dsem() for _ in range(nb)]
    osem = [k.dsem() for _ in range(nb)]
    ps_s = [k.ps("naps%d" % i, [128, 1024]) for i in range(2)]
    ps_t = [k.ps("napt%d" % i, [128, 1024], BF16) for i in range(2)]
    ps_o = [k.ps("napo%d" % i, [128, 512]) for i in range(2)]
    ssb = [k.sb("nas%d" % i, [64, 768], F32) for i in range(2)]
    pnb = [k.sb("napn%d" % i, [64, 768], BF16) for i in range(2)]
    pT = [k.sb("napT%d" % i, [128, 6, 64], BF16) for i in range(2)]
    mx = [k.sb("namx%d" % i, [64, 1], F32) for i in range(2)]
    sm = [k.sb("nasm%d" % i, [64, 1], F32) for i in range(2)]
    tmpo = k.sb("natmp", [128, 64], F32)
    items = []
    for r in range(32):
        if r < 4:
            items.append((r, 0, 7 - r, ("first", 0)))
            items.append((r, r - 4, 3, ("second", 1)))
        elif r >= 29:
            items.append((r, 24, 24 - r + 7, ("first", 2)))
            items.append((r, r - 4, 3, ("second", 3)))
        else:
            items.append((r, r - 4, 3, None))
    it = 0
    for h in range(NHEADS):
        b = h % nb
        ls = lsem[b]
        k.dma("sp", ls, qh[b][:, :], QT[h * 128:(h + 1) * 128, :], dst=qh[b], src=QT)
        k.dma("sp", ls, kh[b][:, :], KT[h * 128:(h + 1) * 128, :], dst=kh[b], src=KT)
        k.dma("act", ls, ve[b][:, :, :], V[0:2816, h * 128:(h + 1) * 128].rearrange("(t p) d -> p t d", p=128), dst=ve[b], src=V)
        k.dma("act", ls, vo[b][:, :, :], V[64:64 + 2688, h * 128:(h + 1) * 128].rearrange("(t p) d -> p t d", p=128), dst=vo[b], src=V)
        k.dma("sp", ls, th[b][:, :, :], Tg[h], dst=th[b])
        for t in (qh[b], kh[b], ve[b], vo[b]):
            t.w = list(th[b].w)
        k.op("pool", lambda e: e.tensor_tensor(out=th[b][:, :, :], in0=th[b][:, :, :], in1=cm[:, :].unsqueeze(1).broadcast_to([64, 15, 64]), op=ALU.add),
             reads=[th[b], cm], writes=[th[b]])
        GI = 2
        for g0 in range(0, len(items), GI):
            grp = items[g0:g0 + GI]
            js = []
            for _ in grp:
                js.append(it % GI); it += 1
            for (r, b0, ri0, blend), j in zip(grp, js):
                kc0 = (b0 + 4) * 64
                qv = qh[b][:, r * 64:(r + 1) * 64]
                k.op("pe", lambda e: e.matmul(out=ps_s[j][:64, :512], lhsT=qv, rhs=kh[b][:, kc0:kc0 + 512], start=True, stop=True), reads=[qh[b], kh[b]], writes=[ps_s[j]])
                k.op("pe", lambda e: e.matmul(out=ps_s[j][:64, 512:768], lhsT=qv, rhs=kh[b][:, 2560:2816], start=True, stop=True), reads=[qh[b], kh[b]], writes=[ps_s[j]])
            for (r, b0, ri0, blend), j in zip(grp, js):
                s_ = ssb[j]
                k.op("dve", lambda e: e.scalar_tensor_tensor(out=s_[:, 0:512], in0=ps_s[j][:64, :512], scalar=scale, in1=th[b][:, ri0:ri0 + 8, :].rearrange("p a b -> p (a b)"),
                                                            op0=ALU.mult, op1=ALU.add), reads=[ps_s[j], th[b]], writes=[s_])
                k.op("act", lambda e: e.activation(out=s_[:, 512:768], in_=ps_s[j][:64, 512:768], func=AF.Copy, scale=scale), reads=[ps_s[j], s_], writes=[s_])
            for (r, b0, ri0, blend), j in zip(grp, js):
                s_ = ssb[j]
                k.op("dve", lambda e: e.tensor_reduce(out=mx[j][:, :], in_=s_[:, :], axis=AX.X, op=ALU.max, negate=True), reads=[s_], writes=[mx[j]])
            for (r, b0, ri0, blend), j in zip(grp, js):
                s_ = ssb[j]
                k.op("act", lambda e: e.activation(out=s_[:, :], in_=s_[:, :], func=AF.Exp, bias=mx[j][:, 0:1], scale=1.0, accum_out=sm[j][:, 0:1]), reads=[s_, mx[j]], writes=[s_, sm[j]])
            for (r, b0, ri0, blend), j in zip(grp, js):
                s_ = ssb[j]
                k.op("dve", lambda e: e.reciprocal(out=sm[j][:, :], in_=sm[j][:, :]), reads=[sm[j]], writes=[sm[j]])
                k.op("dve", lambda e: e.tensor_scalar(out=pnb[j][:, :], in0=s_[:, :], scalar1=sm[j][:, 0:1], scalar2=None, op0=ALU.mult), reads=[s_, sm[j]], writes=[pnb[j]])
            for (r, b0, ri0, blend), j in zip(grp, js):
                for kk in range(6):
                    k.op("pe", lambda e: e.transpose(out=ps_t[j][:, kk * 64:(kk + 1) * 64], in_=pnb[j][:, kk * 128:(kk + 1) * 128], identity=C["ident_bf"][:64, :64]),
                         reads=[pnb[j], C["ident_bf"]], writes=[ps_t[j]], sig=(kk == 5))
            for (r, b0, ri0, blend), j in zip(grp, js):
                k.op("act", lambda e: e.copy(out=pT[j][:, :, :], in_=ps_t[j][:, 0:384].rearrange("p (a b) -> p a b", a=6)), reads=[ps_t[j]], writes=[pT[j]])
            for (r, b0, ri0, blend), j in zip(grp, js):
                par = (b0 + 4) % 2
                jo = j
                for kk in range(6):
                    if kk < 4:
                        vt = (ve[b][:, (b0 + 4) // 2 + kk, :], ve[b]) if par == 0 else (vo[b][:, (b0 + 3) // 2 + kk, :], vo[b])
                    else:
                        vt = (ve[b][:, 20 + (kk - 4), :], ve[b])
                    k.op("pe", lambda e: e.matmul(out=ps_o[jo][:, :64], lhsT=vt[0], rhs=pT[j][:, kk, :], start=(kk == 0), stop=(kk == 5)), reads=[vt[1], pT[j]], writes=[ps_o[jo]], sig=(kk == 5))
                dst = ao[b][:, r * 64:(r + 1) * 64]
                if blend is None:
                    k.op("act", lambda e: e.copy(out=dst, in_=ps_o[jo][:, :64]), reads=[ps_o[jo]], writes=[ao[b]])
                elif blend[0] == "first":
                    k.op("act", lambda e: e.activation(out=tmpo[:, :], in_=ps_o[jo][:, :64], func=AF.Copy, scale=fl[:, blend[1]:blend[1] + 1]), reads=[ps_o[jo], fl], writes=[tmpo])
                else:
                    k.op("dve", lambda e: e.scalar_tensor_tensor(out=dst, in0=ps_o[jo][:, :64], scalar=fl[:, blend[1]:blend[1] + 1], in1=tmpo[:, :], op0=ALU.mult, op1=ALU.add),
                         reads=[ps_o[jo], fl, tmpo], writes=[ao[b]])
        k.dma("sp", osem[b], attnT[h * 128:(h + 1) * 128, :], ao[b][:, :], src=ao[b], dst=attnT)


def na_host_tables(rpb):
    col = np.arange(64)
    ci = np.clip(col[None, :] - col[:, None], -15, 15) + 15
    Tg = rpb[:, :, ci]
    Tg = np.ascontiguousarray(Tg.transpose(0, 2, 1, 3)).astype(np.float32)
    cs = np.clip(col - 8, 0, 64 - 16)
    valid = (col[None, :] >= cs[:, None]) & (col[None, :] < cs[:, None] + 16)
    cmask = np.where(valid, 0.0, -30000.0).astype(np.float32)
    return Tg, cmask


D = 4096
KC = 32
TCX = 128
TLAT = 2048
TOWN = TCX + TLAT
TP = 2 * TOWN
DFF = 11008
NJC = DFF // 128
PAIRS = [[0, 1], [2, 3], [4, 5], [6, 7]]
PARITY = [[0, 2, 4, 6], [1, 3, 5, 7]]
ALL8 = [list(range(8))]
WIN_N = 5632
WIN_U0 = 5120


def mod_phase(k, C, IN, MV):
    modsh = k.dram("modsh", [10, 3072], F32)
    modall = k.dram("modall", [80, 3072], F32)
    modpair = k.dram("modpair", [20, 3072], F32)
    with k.phase():
        s0 = k.dsem()
        cond = k.sb("cond", [128, 32, 5], F32); sc = k.sb("scond", [128, 32, 5], F32)
        mb = k.sb("mb", [5, 6144], F32)
        k.dma("sp", s0, cond[:, :, :], IN["condT"][:, :, :], dst=cond)
        k.dma("sp", s0, mb[:, :], IN["modb"][0:1, :].partition_broadcast(5), dst=mb)
        cond.w = list(mb.w)
        k.op("act", lambda e: e.activation(out=sc[:, :, :], in_=cond[:, :, :], func=AF.Silu), reads=[cond], writes=[sc])
        wb = [k.sb("modw%d" % i, [128, 32, 512], F32) for i in range(2)]
        ws = [k.dsem() for _ in range(2)]
        stg = [k.sb("modst%d" % i, [5, 3072], F32) for i in range(2)]
        ss = [k.dsem() for _ in range(2)]
        pp = PsPool(k, 2, "modps")
        it = 0
        for l in range(2):
            for blk in range(6):
                w = wb[it % 2]
                k.dma("sp" if it % 2 == 0 else "act", ws[it % 2], w[:, :, :], IN["modw"][l, :, blk * 512:(blk + 1) * 512].rearrange("(kc p) n -> p kc n", p=128), dst=w)
                it += 1
                ps = pp.get()
                for kc in range(32):
                    k.op("pe", lambda e: e.matmul(out=ps[:5, :512], lhsT=sc[:, kc, :], rhs=w[:, kc, :], start=(kc == 0), stop=(kc == 31)), reads=[sc, w], writes=[ps], sig=(kc == 31))
                k.op("dve", lambda e: e.tensor_tensor(out=stg[l][:5, blk * 512:(blk + 1) * 512], in0=ps[:5, :512], in1=mb[:5, l * 3072 + blk * 512:l * 3072 + (blk + 1) * 512], op=ALU.add),
                     reads=[ps, mb], writes=[stg[l]])
            k.dma("sp", ss[l], modsh[l * 5:(l + 1) * 5, :], stg[l][:5, :], src=stg[l], dst=modsh)
    k.allgather(k.dsem(), modpair, modsh, PAIRS)
    k.allgather(k.dsem(), modall, modpair, PARITY)
    with k.phase():
        s0 = k.dsem()
        oh = k.sb("oh", [128, 5], F32)
        k.dma("sp", s0, oh[:, :], IN["onehot"][:, :], dst=oh)
        gm = k.sb("gmix", [128, 2, 32], F32); gf = k.sb("gffn", [128, 2, 32], F32)
        k.dma("sp", s0, gm[:, :, :], IN["gmix"][:, :, :], dst=gm)
        k.dma("sp", s0, gf[:, :, :], IN["gffn"][:, :, :], dst=gf)
        oh.w = list(gf.w); gm.w = list(gf.w)
        G = [k.sb("modG%d" % i, [32, 5, 128], F32) for i in range(2)]
        gs = [k.dsem() for _ in range(2)]
        sel = [k.sb("modsel%d" % i, [32, 128], F32) for i in range(2)]
        pp = PsPool(k, 2, "modps2")
        it = 0
        for l in range(2):
            for which in range(6):
                g = G[it % 2]; it += 1
                for r in range(8):
                    k.dma("sp", gs[(it - 1) % 2], g[4 * r:4 * r + 4, :, :],
                          modall[r * 10 + l * 5:r * 10 + l * 5 + 5, which * 512:(which + 1) * 512].rearrange("r (q p) -> q r p", p=128), dst=g, src=modall)
                sl = sel[(it - 1) % 2]
                k.op("dve", lambda e: e.tensor_scalar(out=sl[:, :], in0=g[:, 0, :], scalar1=oh[:32, 0:1], scalar2=None, op0=ALU.mult), reads=[g, oh], writes=[sl])
                for r in range(1, 4):
                    k.op("dve", lambda e: e.scalar_tensor_tensor(out=sl[:, :], in0=g[:, r, :], scalar=oh[:32, r:r + 1], in1=sl[:, :], op0=ALU.mult, op1=ALU.add), reads=[g, oh, sl], writes=[sl])
                for si, src in enumerate((sl[:, :], g[:, 4, :])):
                    ps = pp.get()
                    k.op("pe", lambda e: e.transpose(out=ps[:, 0:32], in_=src, identity=C["ident"][:32, :32]), reads=[sl, g, C["ident"]], writes=[ps])
                    dst = MV[l, which, si]
                    k.op("act", lambda e: e.copy(out=dst[:, :], in_=ps[:, 0:32]), reads=[ps], writes=[dst])
        for l in range(2):
            for si in range(2):
                for (nm, which, gt) in (("gscm", 1, gm), ("gscf", 4, gf)):
                    d = MV[l, nm, si]
                    k.op("dve", lambda e: e.tensor_scalar(out=d[:, :], in0=MV[l, which, si][:, :], scalar1=1.0, scalar2=None, op0=ALU.add), reads=[MV[l, which, si]], writes=[d])
                    k.op("dve", lambda e: e.tensor_tensor(out=d[:, :], in0=d[:, :], in1=gt[:, l, :], op=ALU.mult), reads=[d, gt], writes=[d])


def blend_cols(k, dst, jobs, fl, ca, cb, R):
    nb = 3
    CW = 128
    ta = [k.sb("bla%d" % i, [128, CW], BF16) for i in range(nb)]; tb = [k.sb("blb%d" % i, [128, CW], BF16) for i in range(nb)]
    tf = [k.sb("blf%d" % i, [128, CW], F32) for i in range(nb)]; to = [k.sb("blo%d" % i, [128, CW], BF16) for i in range(nb)]
    si = [k.dsem() for _ in range(nb)]; so = [k.dsem() for _ in range(nb)]
    i = 0
    for (dcol0, A, B) in jobs:
        for r0 in range(0, R, 128):
            b = i % nb; i += 1
            k.dma("sp", si[b], ta[b][:, :], A[r0:r0 + 128, :], dst=ta[b], src=A)
            k.dma("act", si[b], tb[b][:, :], B[r0:r0 + 128, :], dst=tb[b], src=B)
            ta[b].w = list(tb[b].w)
            k.op("pool", lambda e: e.tensor_scalar(out=tf[b][:, :], in0=ta[b][:, :], scalar1=fl[:, ca:ca + 1], scalar2=None, op0=ALU.mult), reads=[ta[b], fl], writes=[tf[b]])
            k.op("dve", lambda e: e.scalar_tensor_tensor(out=to[b][:, :], in0=tb[b][:, :], scalar=fl[:, cb:cb + 1], in1=tf[b][:, :], op0=ALU.mult, op1=ALU.add),
                 reads=[tb[b], fl, tf[b]], writes=[to[b]])
            k.dma("sp", so[b], dst[r0:r0 + 128, dcol0:dcol0 + CW], to[b][:, :], src=to[b], dst=dst)


def scale_cols(k, dst, dcol0, src, scol0, fl, ca, R, ncols):
    nb = 2
    ta = [k.sb("sca%d" % i, [128, ncols], BF16) for i in range(nb)]; to = [k.sb("sco%d" % i, [128, ncols], BF16) for i in range(nb)]
    si = [k.dsem() for _ in range(nb)]; so = [k.dsem() for _ in range(nb)]
    i = 0
    for r0 in range(0, R, 128):
        b = i % nb; i += 1
        k.dma("sp", si[b], ta[b][:, :], src[r0:r0 + 128, scol0:scol0 + ncols], dst=ta[b], src=src)
        k.op("dve", lambda e: e.tensor_scalar(out=to[b][:, :], in0=ta[b][:, :], scalar1=fl[:, ca:ca + 1], scalar2=None, op0=ALU.mult), reads=[ta[b], fl], writes=[to[b]])
        k.dma("act", so[b], dst[r0:r0 + 128, dcol0:dcol0 + ncols], to[b][:, :], src=to[b], dst=dst)


def resid_epilogue(k, xsrc, xdst, gate, nstage=3, tcol0=0):
    xi = Stage(k, nstage, "rxi", [128, 512], F32)
    xo = Stage(k, nstage, "rxo", [128, 512], F32)

    def epi(ps, m, n, f0, t0, **kw):
        fc = f0 // 128
        t0 = t0 + tcol0
        si = 1 if t0 < TCX else 0
        a, asem = xi.get()
        k.dma("act", asem, a[:m, :n], xsrc[f0:f0 + m, t0:t0 + n], dst=a, src=xsrc)
        o, osem = xo.get()
        k.op("dve", lambda e: e.scalar_tensor_tensor(out=o[:m, :n], in0=ps[:m, :n], scalar=gate[si][:m, fc:fc + 1], in1=a[:m, :n], op0=ALU.mult, op1=ALU.add),
             reads=[ps, gate[si], a], writes=[o])
        k.dma("act", osem, xdst[f0:f0 + m, t0:t0 + n], o[:m, :n], src=o, dst=xdst)
    return epi


def rope_phase(k, C, q_tm, k_tm, ropeT, qT, kT, ktm_r, NH):
    W = NH * 256
    nb = 2
    xq = [k.sb("rq%d" % i, [128, NH, 2, 2, 64], F32) for i in range(nb)]
    xk = [k.sb("rk%d" % i, [128, NH, 2, 2, 64], F32) for i in range(nb)]
    tb = [k.sb("rt%d" % i, [128, 2, 2, 64], F32) for i in range(nb)]
    ls = [k.dsem() for _ in range(nb)]
    rq = [k.sb("rrq%d" % i, [128, NH, 2, 2, 64], F32) for i in range(nb)]
    rk = [k.sb("rrk%d" % i, [128, NH, 2, 2, 64], F32) for i in range(nb)]
    t1 = k.sb("rt1", [128, NH, 2, 64], F32); t2 = k.sb("rt2", [128, NH, 2, 64], F32)
    oq = [k.sb("roq%d" % i, [128, NH * 2, 128], F32) for i in range(nb)]
    ok = [k.sb("rok%d" % i, [128, NH * 2, 128], F32) for i in range(nb)]
    sq = [k.dsem() for _ in range(nb)]; sk = [k.dsem() for _ in range(nb)]; sr = [k.dsem() for _ in range(nb)]
    pp = PsPool(k, 4, "rps")
    flat = "p a b c d -> p (a b c d)"
    for ci in range(TP // 128):
        b = ci % nb
        t0 = ci * 128
        k.dma("sp", ls[b], xq[b][:].rearrange(flat), q_tm[t0:t0 + 128, :], dst=xq[b], src=q_tm)
        k.dma("act", ls[b], xk[b][:].rearrange(flat), k_tm[t0:t0 + 128, :], dst=xk[b], src=k_tm)
        k.dma("sp", ls[b], tb[b][:].rearrange("p a b c -> p (a b c)"), ropeT[t0:t0 + 128, :], dst=tb[b], src=ropeT)
        xq[b].w = list(tb[b].w); xk[b].w = list(tb[b].w)
        cosv = tb[b][:, 0, :, :].unsqueeze(1).broadcast_to([128, NH, 2, 64])
        sinv = tb[b][:, 1, :, :].unsqueeze(1).broadcast_to([128, NH, 2, 64])
        for (x, r) in ((xq[b], rq[b]), (xk[b], rk[b])):
            x1, x2 = x[:, :, :, 0, :], x[:, :, :, 1, :]
            k.op("dve", lambda e: e.tensor_tensor(out=t1[:], in0=x1, in1=cosv, op=ALU.mult), reads=[x, tb[b]], writes=[t1])
            k.op("pool", lambda e: e.tensor_tensor(out=t2[:], in0=x2, in1=sinv, op=ALU.mult), reads=[x, tb[b]], writes=[t2])
            k.op("dve", lambda e: e.tensor_tensor(out=r[:, :, :, 0, :], in0=t1[:], in1=t2[:], op=ALU.subtract), reads=[t1, t2], writes=[r])
            k.op("dve", lambda e: e.tensor_tensor(out=t1[:], in0=x1, in1=sinv, op=ALU.mult), reads=[x, tb[b], r], writes=[t1])
            k.op("pool", lambda e: e.tensor_tensor(out=t2[:], in0=x2, in1=cosv, op=ALU.mult), reads=[x, tb[b], r], writes=[t2])
            k.op("dve", lambda e: e.tensor_tensor(out=r[:, :, :, 1, :], in0=t1[:], in1=t2[:], op=ALU.add), reads=[t1, t2], writes=[r])
        k.dma("act", sr[b], ktm_r[t0:t0 + 128, :], rk[b][:].rearrange(flat), src=rk[b], dst=ktm_r)
        for (r, o, osem, dstT, scl) in ((rq[b], oq[b], sq[b], qT, 1.0 / 16.0), (rk[b], ok[b], sk[b], kT, 1.0)):
            rf = r[:].rearrange(flat)
            for half in range(NH * 2 // 3):
                ps = pp.get()
                for j in range(3):
                    c = half * 3 + j
                    k.op("pe", lambda e: e.transpose(out=ps[:, j * 128:(j + 1) * 128], in_=rf[:, c * 128:(c + 1) * 128], identity=C["ident"][:, :]),
                         reads=[r, C["ident"]], writes=[ps], sig=(j == 2))
                k.op("act", lambda e: e.activation(out=o[:, half * 3:half * 3 + 3, :], in_=ps[:, 0:384].rearrange("p (a b) -> p a b", a=3), func=AF.Copy, scale=scl),
                     reads=[ps], writes=[o])
            k.dma("sp", osem, dstT[:, t0:t0 + 128].rearrange("(c p) t -> p c t", p=128), o[:, :, :], src=o, dst=dstT)


def mlstm_readout(k, C, hdir, o_tm, hg_d, mixT, NH):
    W = NH * 512
    s0 = k.dsem()
    hg = k.sb("rohg", [128, W], F32)
    k.dma("sp", s0, hg[:, :], hg_d[0:1, :].partition_broadcast(128), dst=hg)
    nb = 2
    h0 = [k.sb("roh0%d" % i, [128, W], F32) for i in range(nb)]
    h1 = [k.sb("roh1%d" % i, [128, W], F32) for i in range(nb)]
    ot = [k.sb("roo%d" % i, [128, W], F32) for i in range(nb)]
    ls = [k.dsem() for _ in range(nb)]
    sqj = k.sb("rosq", [128, 512], F32)
    ss = k.sb("ross", [128, NH], F32)
    mo = [k.sb("romo%d" % i, [128, W // 128, 128], BF16) for i in range(nb)]
    ms = [k.dsem() for _ in range(nb)]
    pp = PsPool(k, 4, "rops")
    for ci in range(TP // 128):
        b = ci % nb
        t0 = ci * 128
        k.dma("sp", ls[b], h0[b][:, :], hdir[0][t0:t0 + 128, :], dst=h0[b], src=hdir[0])
        k.dma("act", ls[b], h1[b][:, :], hdir[1][t0:t0 + 128, :], dst=h1[b], src=hdir[1])
        k.dma("sp", ls[b], ot[b][:, :], o_tm[t0:t0 + 128, :], dst=ot[b], src=o_tm)
        h0[b].w = list(ot[b].w); h1[b].w = list(ot[b].w)
        k.op("dve", lambda e: e.tensor_tensor(out=h0[b][:, :], in0=h0[b][:, :], in1=h1[b][:, :], op=ALU.add), reads=[h0[b], h1[b]], writes=[h0[b]])
        k.op("act", lambda e: e.activation(out=ot[b][:, :], in_=ot[b][:, :], func=AF.Sigmoid), reads=[ot[b]], writes=[ot[b]])
        for h in range(NH):
            k.op("act", lambda e: e.activation(out=sqj[:, :], in_=h0[b][:, h * 512:(h + 1) * 512], func=AF.Square, accum_out=ss[:, h:h + 1]), reads=[h0[b]], writes=[sqj, ss])
        k.op("act", lambda e: e.activation(out=ss[:, :], in_=ss[:, :], func=AF.Sqrt, scale=1.0 / 512.0, bias=k.eps_t[:, 0:1]), reads=[ss, k.eps_t], writes=[ss])
        k.op("dve", lambda e: e.reciprocal(out=ss[:, :], in_=ss[:, :]), reads=[ss], writes=[ss])
        for h in range(NH):
            sl = slice(h * 512, (h + 1) * 512)
            k.op("dve", lambda e: e.scalar_tensor_tensor(out=h1[b][:, sl], in0=h0[b][:, sl], scalar=ss[:, h:h + 1], in1=hg[:, sl], op0=ALU.mult, op1=ALU.mult),
                 reads=[h0[b], ss, hg], writes=[h1[b]])
        k.op("pool", lambda e: e.tensor_tensor(out=h1[b][:, :], in0=h1[b][:, :], in1=ot[b][:, :], op=ALU.mult), reads=[h1[b], ot[b]], writes=[h1[b]])
        for grp in range(W // 512):
            ps = pp.get()
            for j in range(4):
                c = grp * 4 + j
                k.op("pe", lambda e: e.transpose(out=ps[:, j * 128:(j + 1) * 128], in_=h1[b][:, c * 128:(c + 1) * 128], identity=C["ident"][:, :]),
                     reads=[h1[b], C["ident"]], writes=[ps], sig=(j == 3))
            k.op("act", lambda e: e.copy(out=mo[b][:, grp * 4:grp * 4 + 4, :], in_=ps[:, :].rearrange("p (a b) -> p a b", a=4)), reads=[ps], writes=[mo[b]])
        mp = mixT[ci]
        k.dma("sp", ms[b], mp[0:W, :].rearrange("(c p) t -> p c t", p=128), mo[b][:, :, :], src=mo[b], dst=mp)


def ffn_up_phase(k, C, h2T, halo, Wup, cw, cb, uffn, segs):
    NBW = 256
    blocks = []
    for (t0, n, left, right) in segs:
        nblk = ceil_div(n, 510)
        sz = ceil_div(n, nblk)
        x = t0
        while x < t0 + n:
            c = min(sz, t0 + n - x)
            blocks.append((x, c, left if x == t0 else ("int",), right if x + c == t0 + n else ("int",)))
            x += c
    half_n = ceil_div(len(blocks), 2)
    supers = [blocks[:half_n], blocks[half_n:]] if len(blocks) > 3 else [blocks]
    maxb = max(len(s) for s in supers)
    A = [k.sb("fuA%d" % i, [128, KC, 512], BF16) for i in range(maxb)]
    As = [k.dsem() for _ in range(maxb)]
    wa = [k.sb("fuwa%d" % i, [128, KC, NBW], BF16) for i in range(2)]
    wg = [k.sb("fuwg%d" % i, [128, KC, NBW], BF16) for i in range(2)]
    wsm = [k.dsem() for _ in range(2)]
    pg = [k.ps("fupg%d" % i, [128, 512]) for i in range(3)]
    pa = [k.ps("fupa%d" % i, [128, 512]) for i in range(3)]
    c1 = [k.sb("fuc%d" % i, [128, 512], F32) for i in range(2)]
    y1 = [k.sb("fuy%d" % i, [128, 512], F32) for i in range(2)]
    y2 = [k.sb("fuz%d" % i, [128, 512], F32) for i in range(2)]
    ust = Stage(k, 3, "fuu", [128, 512], BF16)
    allr = slice(0, 128)
    it = 0
    nwb = DFF // NBW
    for sblocks in supers:
        for bi, (x, c, left, right) in enumerate(sblocks):
            a = A[bi]
            lo = x - 1 if left[0] == "int" else x
            hi = x + c + 1 if right[0] == "int" else x + c
            k.dma("act", As[bi], a[:, :, 1 - (x - lo):1 + c + (hi - x - c)], h2T[:, lo:hi].rearrange("(kc p) t -> p kc t", p=128), dst=a, src=h2T)
            if left[0] == "halo":
                k.op("dve", lambda e: e.tensor_copy(out=a[:, :, 0:1], in_=halo[:, :, left[1]:left[1] + 1]), reads=[halo], writes=[a])
            if right[0] == "halo":
                k.op("dve", lambda e: e.tensor_copy(out=a[:, :, c + 1:c + 2], in_=halo[:, :, right[1]:right[1] + 1]), reads=[halo], writes=[a])

        def issue_w(wi):
            j = wi % 2
            tka, apa = Wup(wi * NBW, NBW)
            tkg, apg = Wup(DFF + wi * NBW, NBW)
            k.dma("sp", wsm[j], wa[j][:, :, :], apa.rearrange("(kc p) n -> p kc n", p=128), dst=wa[j], src=tka)
            k.dma("sp", wsm[j], wg[j][:, :, :], apg.rearrange("(kc p) n -> p kc n", p=128), dst=wg[j], src=tkg)
            wa[j].w = list(wg[j].w)
        issue_w(0)
        for wi in range(nwb):
            if wi + 1 < nwb:
                issue_w(wi + 1)
            j = wi % 2
            for cj in range(NBW // 128):
                jc = wi * (NBW // 128) + cj
                for bi, (x, c, left, right) in enumerate(sblocks):
                    a = A[bi]
                    q = it % 3; q2 = it % 2; it += 1
                    for kc in range(KC):
                        k.op("pe", lambda e: e.matmul(out=pg[q][:, :c + 2], lhsT=wg[j][:, kc, cj * 128:(cj + 1) * 128], rhs=a[:, kc, 0:c + 2], start=(kc == 0), stop=(kc == KC - 1)),
                             reads=[wg[j], a], writes=[pg[q]], sig=(kc == KC - 1))
                    for kc in range(KC):
                        k.op("pe", lambda e: e.matmul(out=pa[q][:, :c], lhsT=wa[j][:, kc, cj * 128:(cj + 1) * 128], rhs=a[:, kc, 1:c + 1], start=(kc == 0), stop=(kc == KC - 1)),
                             reads=[wa[j], a], writes=[pa[q]], sig=(kc == KC - 1))
                    cc = c1[q2]
                    k.op("dve", lambda e: e.tensor_scalar(out=cc[:, :c], in0=pg[q][:, 0:c], scalar1=cw[:, 0, jc:jc + 1], scalar2=None, op0=ALU.mult), reads=[pg[q], cw], writes=[cc])
                    k.op("dve", lambda e: e.scalar_tensor_tensor(out=cc[:, :c], in0=pg[q][:, 1:c + 1], scalar=cw[:, 1, jc:jc + 1], in1=cc[:, :c], op0=ALU.mult, op1=ALU.add), reads=[pg[q], cw, cc], writes=[cc])
                    k.op("dve", lambda e: e.scalar_tensor_tensor(out=cc[:, :c], in0=pg[q][:, 2:c + 2], scalar=cw[:, 2, jc:jc + 1], in1=cc[:, :c], op0=ALU.mult, op1=ALU.add), reads=[pg[q], cw, cc], writes=[cc])
                    k.op("act", lambda e: e.activation(out=cc[:, :c], in_=cc[:, :c], func=AF.Identity, bias=cb[:, jc:jc + 1], scale=1.0), reads=[cc, cb], writes=[cc])
                    gl = y2[q2]
                    gelu_tanh(k, cc, y1[q2], gl, gl, allr, c, eng2="pool")
                    u, usem = ust.get()
                    k.op("dve", lambda e: e.tensor_tensor(out=u[:, :c], in0=pa[q][:, :c], in1=gl[:, :c], op=ALU.mult), reads=[pa[q], gl], writes=[u])
                    k.dma("act", usem, uffn[jc * 128:(jc + 1) * 128, x:x + c], u[:, :c], src=u, dst=uffn)


def copy_epilogue(k, dst, dt, tm=False, nstage=3, eng=("act", "dve"), dcol0=0, drow0=0):
    st = Stage(k, nstage, "cpe", [128, 512], dt)
    cnt = [0]

    def epi(ps, m, n, a0, b0, **kw):
        s, sem = st.get()
        e = eng[cnt[0] % len(eng)]; cnt[0] += 1
        if e == "act":
            k.op("act", lambda en: en.copy(out=s[:m, :n], in_=ps[:m, :n]), reads=[ps], writes=[s])
        else:
            k.op("dve", lambda en: en.tensor_copy(out=s[:m, :n], in_=ps[:m, :n]), reads=[ps], writes=[s])
        k.dma("act", sem, dst[drow0 + a0:drow0 + a0 + m, dcol0 + b0:dcol0 + b0 + n], s[:m, :n], src=s, dst=dst)
    return epi


def dram_copy(k, q, dst, dst_ap, src, src_ap, **kw):
    k.dma(q, k.dsem(), dst_ap, src_ap, dst=dst, src=src, **kw)


class _Stop(Exception):
    pass


def build_program(stop_after=None, debug=(), internal_inputs=False):
    nc = bass.Bass("TRN2", target_bir_lowering=False, num_devices=8)
    k = K(nc)
    try:
        _build_body(nc, k, stop_after, debug, internal_inputs)
    except _Stop:
        pass
    return nc


def _build_body(nc, k, stop_after, debug, internal_inputs):
    IN = {}

    def inp(name, shape, dt=F32):
        ext = (not internal_inputs) or (isinstance(internal_inputs, (list, tuple)) and name in internal_inputs)
        IN[name] = k.dram(name, shape, dt, kind="ExternalInput" if ext else "Internal")

    def ck(tag):
        if stop_after == tag:
            dd = k.dram("dummy_out", [128, 4], F32, kind="ExternalOutput")
            k.dma("sp", k.dsem(), dd[:, :], fl[:, :], src=fl, dst=dd)
            k.barrier()
            k.close()
            raise _Stop()
    inp("xin", [D, TOWN]); inp("condT", [128, 32, 5]); inp("onehot", [128, 5]); inp("flags", [128, 4])
    inp("modw", [2, D, 3072]); inp("modb", [1, 6144]); inp("gmix", [128, 2, 32]); inp("gffn", [128, 2, 32]); inp("gfin", [128, 32])
    inp("consts", [128, 640])
    wspec = [("w_in", 1024, WIN_N, PARITY), ("w_glu", 128, 1024, ALL8), ("w_out0", 512, D, ALL8), ("w_up0", 512, 2 * DFF, ALL8),
             ("w_dn0", DFF // 8, D, ALL8), ("w_qkv", 512, 3 * D, ALL8), ("w_out1", 512, D, ALL8), ("w_up1", 512, 2 * DFF, ALL8), ("w_dn1", DFF // 8, D, ALL8)]
    for (nm, R, N, grp) in wspec:
        inp(nm, [R, N])
    inp("gate_b", [1, 12]); inp("head_g", [1, 1536]); inp("ropeT", [TP, 256])
    inp("s5par", [128, 3, 32]); inp("s5Bre", [128, 16, 128]); inp("s5Bim", [128, 16, 128]); inp("s5Cre", [128, 16, 64]); inp("s5Cim", [128, 16, 64])
    inp("s5dsk", [128, 4]); inp("glu_b", [128, 8]); inp("conv_w", [128, 2, 3, NJC]); inp("conv_b", [128, 2, NJC])
    inp("rpbT", [32, 64, 15, 64]); inp("cmask", [64, 64])
    outT = k.dram("outT", [D, TLAT], F32, kind="ExternalOutput")
    DBG = {}
    for (nm, shape, dt) in debug:
        DBG[nm] = k.dram("dbg_" + nm, shape, dt, kind="ExternalOutput")

    C = load_consts(k, IN["consts"])
    s0 = k.dsem()
    fl = k.sb("flags", [128, 4], F32); gfin = k.sb("gfin", [128, 32], F32); glub = k.sb("glub", [128, 8], F32)
    cw = k.sb("convw", [128, 2, 3, NJC], F32); cb = k.sb("convb", [128, 2, NJC], F32)
    k.dma("sp", s0, fl[:, :], IN["flags"][:, :], dst=fl); k.dma("sp", s0, gfin[:, :], IN["gfin"][:, :], dst=gfin)
    k.dma("sp", s0, glub[:, :], IN["glu_b"][:, :], dst=glub)
    k.dma("sp", s0, cw[:].rearrange("p a b c -> p (a b c)"), IN["conv_w"][:].rearrange("p a b c -> p (a b c)"), dst=cw)
    k.dma("sp", s0, cb[:].rearrange("p a b -> p (a b)"), IN["conv_b"][:].rearrange("p a b -> p (a b)"), dst=cb)
    for t in (fl, gfin, glub, cw):
        t.w = list(cb.w)
    MV = {}
    for l in range(2):
        for si in range(2):
            for key in (0, 1, 2, 3, 4, 5, "gscm", "gscf"):
                MV[l, key, si] = k.sb("mv", [128, 32], F32)

    WT = {}
    PW = {"w_in": 512, "w_glu": 1024, "w_out0": 512, "w_up0": 512, "w_dn0": 128, "w_qkv": 512, "w_out1": 512, "w_up1": 512, "w_dn1": 128}
    semA = k.dsem(); semB = k.dsem()
    st1_in = {}; st1_out = {}; full = {}
    for (nm, R, N, grp) in wspec:
        w = PW[nm]
        npc = N // w
        st1_in[nm] = [k.dram("%s_a%d" % (nm, j), [R, w], BF16) for j in range(npc)]
        with k.phase():
            cast_rows_pieces(k, IN[nm], st1_in[nm], R, N, w)
    ck('cast')
    for (nm, R, N, grp) in wspec:
        w = PW[nm]
        npc = N // w
        if nm == "w_in":
            full[nm] = [k.dram("%s_f%d" % (nm, j), [4 * R, w], BF16) for j in range(npc)]
            for j in range(npc):
                k.allgather(semB, full[nm][j], st1_in[nm][j], PARITY)
        else:
            st1_out[nm] = [k.dram("%s_b%d" % (nm, j), [2 * R, w], BF16) for j in range(npc)]
            for j in range(npc):
                k.allgather(semA, st1_out[nm][j], st1_in[nm][j], PAIRS)
    for nm in st1_out:
        for t in st1_out[nm]:
            t.w = [(semA, k.cnt[semA])]
    for (nm, R, N, grp) in wspec:
        if nm == "w_in":
            continue
        w = PW[nm]
        npc = N // w
        full[nm] = [k.dram("%s_f%d" % (nm, j), [8 * R, w], BF16) for j in range(npc)]
    semC = k.dsem()
    L1W = ("w_qkv", "w_out1", "w_up1", "w_dn1")
    for lazy_pass in (False, True):
        for (nm, R, N, grp) in wspec:
            if nm == "w_in" or (nm in L1W) != lazy_pass:
                continue
            for j in range(N // PW[nm]):
                k.allgather(semC if lazy_pass else semB, full[nm][j], st1_out[nm][j], PARITY)
    for nm in full:
        for t in full[nm]:
            t.w = [(semC, k.cnt[semC])] if nm in L1W else [(semB, k.cnt[semB])]
    k.lazy = {semC}

    def wget(nm):
        w = PW[nm]

        def f(c0, n):
            tk = full[nm][c0 // w]
            off = c0 % w
            assert off + n <= w, (nm, c0, n)
            return tk, tk[:, off:off + n]
        return f
    for nm in full:
        WT[nm] = wget(nm)
    w_in_ap = WT["w_in"]
    if "wdump" in DBG:
        with k.phase():
            for i_, j_ in enumerate((0, 8, 9, 10)):
                dram_copy(k, "sp", DBG["wdump"], DBG["wdump"][i_ * D:(i_ + 1) * D, :], full["w_in"][j_], full["w_in"][j_][:, :])
    ck('weights')
    mod_phase(k, C, IN, MV)
    ck('mod')

    def done(tag):
        return stop_after == tag

    xA = k.dram("xA", [D, TOWN], F32); xB = k.dram("xB", [D, TOWN], F32)
    hT_own = k.dram("hT_own", [D, TOWN], BF16)
    NPC = TOWN // 128
    hT_own_p = [k.dram("hTo%d" % j, [D, 128], BF16) for j in range(NPC)]
    hT_pair_p = [k.dram("hTp%d" % j, [2 * D, 128], BF16) for j in range(NPC)]
    h2T = k.dram("h2T", [D, TOWN], BF16); uffn = k.dram("uffn", [DFF, TOWN], BF16)
    hb_own = k.dram("hb_own", [D, 64], BF16); hb_pair = k.dram("hb_pair", [2 * D, 64], BF16)
    SEG01 = [(0, TCX, 1), (TCX, TLAT, 0)]

    def ffn(l, xsrc, xdst, with_ctx):
        segn = SEG01 if with_ctx else [(TCX, TLAT, 0)]
        with k.phase():
            norm_mod(k, xsrc, h2T, segn, {0: MV[l, "gscf", 0], 1: MV[l, "gscf", 1]}, {0: MV[l, 3, 0], 1: MV[l, 3, 1]}, C["ones_bf"], D)
        with k.phase():
            for j, col in enumerate((0, TCX - 1, TCX, TOWN - 1)):
                if with_ctx or j >= 2:
                    dram_copy(k, "sp", hb_own, hb_own[:, j:j + 1], h2T, h2T[:, col:col + 1], allow_slow_non_contiguous=True)
        ck('ffnhalo%d' % l)
        k.allgather(k.dsem(), hb_pair, hb_own, PAIRS)
        with k.phase():
            hraw = k.sb("hraw", [128, KC, 2, 4], BF16); halo = k.sb("halo", [128, KC, 4], BF16)
            hs = k.dsem()
            for r in range(2):
                k.dma("sp", hs, hraw[:, :, r, :], hb_pair[r * D:(r + 1) * D, 0:4].rearrange("(kc p) j -> p kc j", p=128), dst=hraw, src=hb_pair)
            for (o, r, cidx, fcol) in ((0, 0, 1, 1), (1, 1, 0, 0), (2, 0, 3, 1), (3, 1, 2, 0)):
                k.op("dve", lambda e: e.tensor_scalar(out=halo[:, :, o:o + 1], in0=hraw[:, :, r, cidx:cidx + 1], scalar1=fl[:, fcol:fcol + 1], scalar2=None, op0=ALU.mult),
                     reads=[hraw, fl], writes=[halo])
            segs = []
            if with_ctx:
                segs.append((0, TCX, ("halo", 0), ("halo", 1)))
            segs.append((TCX, TLAT, ("halo", 2), ("halo", 3)))
            ffn_up_phase(k, C, h2T, halo, WT["w_up%d" % l], _Sub3(cw, l), _Sub2(cb, l), uffn, segs)
        ck('ffnup%d' % l)
        with k.phase():
            pp = PsPool(k, 4, "fdps")
            t00 = 0 if with_ctx else TCX
            nt = TOWN - t00
            blocks = mk_blocks(nt, 512, 512, bounds=(TCX - t00,))

            def load_a(dst, sem, t0, n):
                load_rows(k, "act", sem, dst, dst[:, :, :n], uffn, uffn[:, t00 + t0:t00 + t0 + n])
            epi = resid_epilogue(k, xsrc, xdst, [MV[l, 5, 0], MV[l, 5, 1]], tcol0=t00)
            gemm(k, "fm", NJC, blocks, load_a, WT["w_dn%d" % l], 0, D, 128, epi, pp, tag="fd")

    with k.phase():
        def wr_h(h, a, m, sem):
            for a2 in range(a, a + m, 128):
                pc = hT_own_p[a2 // 128]
                k.dma("act", sem, pc[:, :].rearrange("(kc p) t -> p kc t", p=128), h[:, :, a2 - a:a2 - a + 128], src=h, dst=pc)
        norm_mod(k, IN["xin"], None, SEG01, {0: MV[0, "gscm", 0], 1: MV[0, "gscm", 1]}, {0: MV[0, 0, 0], 1: MV[0, 0, 1]}, C["ones_bf"], D, writer=wr_h)
    for j in range(NPC):
        k.allgather(k.dsem(), hT_pair_p[j], hT_own_p[j], PAIRS)
    ck('norm0')
    q_tm = k.dram("q_tm", [TP, 768], F32); k_tm = k.dram("k_tm", [TP, 768], F32); v_tm = k.dram("v_tm", [TP, 1536], F32)
    o_tm = k.dram("o_tm", [TP, 1536], F32); gates = k.dram("gates", [TP, 12], F32); uT = k.dram("uT", [512, TP], F32)
    pblocks_tm = []
    pblocks_fm = []
    for r in range(2):
        for (t0, n, subs) in mk_blocks(TOWN, 1152, 128):
            pblocks_tm.append((r * TOWN + t0, n, subs))
        for (t0, n, subs) in mk_blocks(TOWN, 1152, 512):
            pblocks_fm.append((r * TOWN + t0, n, subs))

    def load_pair(dst, sem, t0, n):
        r, pos = t0 // TOWN, t0 % TOWN
        for a2 in range(pos, pos + n, 128):
            pc = hT_pair_p[a2 // 128]
            k.dma("act", sem, dst[:, :, a2 - pos:a2 - pos + 128], pc[r * D:(r + 1) * D, :].rearrange("(kc p) t -> p kc t", p=128), dst=dst, src=pc)
    with k.phase():
        pp = PsPool(k, 4, "ipps")
        st = Stage(k, 3, "ipst", [128, 512], F32)
        dests = [(0, 768, q_tm), (768, 1536, k_tm), (1536, 3072, v_tm), (3072, 4608, o_tm), (4608, 4620, gates)]

        def epi_ip(ps, m, n, t0, c0, **kw):
            s, sem = st.get()
            k.op("act", lambda e: e.copy(out=s[:m, :n], in_=ps[:m, :n]), reads=[ps], writes=[s])
            for (a, b_, dtk) in dests:
                lo, hi = max(a, c0), min(b_, c0 + n)
                if lo < hi:
                    k.dma("act", sem, dtk[t0:t0 + m, lo - a:hi - a], s[:m, lo - c0:hi - c0], src=s, dst=dtk)
        gemm(k, "tm", KC, pblocks_tm, load_pair, w_in_ap, 0, 4620, 512, epi_ip, pp, tag="ip")
    with k.phase():
        pp = PsPool(k, 4, "iups")
        gemm(k, "fm", KC, pblocks_fm, load_pair, w_in_ap, WIN_U0, 512, 512, copy_epilogue(k, uT, F32), pp, tag="iu")
    ck('inproj')
    qT = k.dram("qT", [768, TP], F32); kT = k.dram("kT", [768, TP], F32); ktm_r = k.dram("ktm_r", [TP, 768], F32)
    with k.phase():
        rope_phase(k, C, q_tm, k_tm, IN["ropeT"], qT, kT, ktm_r, 3)
    ck('rope')
    hdir = [k.dram("hdir0", [TP, 1536], F32), k.dram("hdir1", [TP, 1536], F32)]
    with k.phase():
        gb_bc = k.sb("gb_bc", [128, 12], F32)
        k.dma("sp", k.dsem(), gb_bc[:, :], IN["gate_b"][0:1, :].partition_broadcast(128), dst=gb_bc)
        fwd = [0, 17] + list(range(1, 17)) + list(range(18, 34))
        bwd = [17, 0] + list(range(33, 17, -1)) + list(range(16, 0, -1))
        mlstm_scan(k, C, qT, kT, ktm_r, v_tm, gates, gb_bc, hdir, [fwd, bwd], 3)
    ck('mlstm')
    mix_own = [k.dram("mixo%d" % j, [2048, 128], BF16) for j in range(TP // 128)]
    mix_pair = [k.dram("mixp%d" % j, [4096, 128], BF16) for j in range(TP // 128)]
    with k.phase():
        mlstm_readout(k, C, hdir, o_tm, IN["head_g"], mix_own, 3)
    ck('readout')
    with k.phase():
        lat0 = [(TCX + 512 * i, 512) for i in range(4)]
        lat1 = [(TOWN + TCX + 512 * i, 512) for i in range(4)]
        chf = [(0, TCX), (TOWN, TCX)] + lat0 + lat1
        chb = [(TOWN, TCX), (0, TCX)] + lat1[::-1] + lat0[::-1]
        s5_phase(k, C, uT, IN["s5par"], IN["s5Bre"], IN["s5Bim"], IN["s5Cre"], IN["s5Cim"], IN["s5dsk"], mix_own, 1536, TP, 32, [chf, chb])
    ck('s5')
    for j in range(TP // 128):
        k.allgather(k.dsem(), mix_pair[j], mix_own[j], PAIRS)
    mixsel = k.dram("mixsel", [4096, TOWN], BF16); ysT = k.dram("ysT", [1024, TOWN], BF16)
    with k.phase():
        blend_cols(k, mixsel, [(j * 128, mix_pair[j], mix_pair[NPC + j]) for j in range(NPC)], fl, 0, 1, 4096)
    ck('blend')
    own_blocks = mk_blocks(TOWN, 1088, 512, bounds=(TCX,))
    with k.phase():
        pp = PsPool(k, 4, "glps")
        st = Stage(k, 3, "glst", [128, 512], BF16)
        sg = [k.sb("glsg%d" % i, [128, 512], F32) for i in range(2)]
        cnt = [0]

        def load_g(dst, sem, t0, n):
            for r in range(2):
                k.dma("act", sem, dst[:, 4 * r:4 * r + 4, :n], mixsel[r * 2048 + 1536:r * 2048 + 2048, t0:t0 + n].rearrange("(kc p) t -> p kc t", p=128), dst=dst, src=mixsel)

        def epi_glu(ps, m, n, f0, t0, at=None, ts=None):
            fc = f0 // 128
            g_ = sg[cnt[0] % 2]; cnt[0] += 1
            k.op("act", lambda e: e.activation(out=g_[:m, :n], in_=ps[:m, :n], func=AF.Sigmoid, bias=glub[:m, fc:fc + 1], scale=1.0), reads=[ps, glub], writes=[g_])
            s, sem = st.get()
            k.op("dve", lambda e: e.tensor_tensor(out=s[:m, :n], in0=g_[:m, :n], in1=at[:m, fc, ts:ts + n], op=ALU.mult), reads=[g_, at], writes=[s])
            k.dma("act", sem, ysT[f0:f0 + m, t0:t0 + n], s[:m, :n], src=s, dst=ysT)
        gemm(k, "fm", 8, own_blocks, load_g, WT["w_glu"], 0, 1024, 512, epi_glu, pp, tag="gl")
    with k.phase():
        pp = PsPool(k, 4, "o0ps")

        def load_mix(dst, sem, t0, n):
            for r in range(2):
                k.dma("act", sem, dst[:, 12 * r:12 * r + 12, :n], mixsel[r * 2048:r * 2048 + 1536, t0:t0 + n].rearrange("(kc p) t -> p kc t", p=128), dst=dst, src=mixsel)
            k.dma("act", sem, dst[:, 24:32, :n], ysT[:, t0:t0 + n].rearrange("(kc p) t -> p kc t", p=128), dst=dst, src=ysT)
        gemm(k, "fm", KC, own_blocks, load_mix, WT["w_out0"], 0, D, 512, resid_epilogue(k, IN["xin"], xA, [MV[0, 2, 0], MV[0, 2, 1]]), pp, tag="o0")
    ck('outproj0')
    if DBG:
        with k.phase():
            mvd = DBG["mv"]
            for i_, key in enumerate(((0, "gscm", 0), (0, 0, 0), (0, 2, 0), (0, 2, 1), (0, 1, 0), (0, "gscm", 1), (1, 2, 0), (0, 5, 0))):
                k.dma("sp", k.dsem(), mvd[:, i_ * 32:(i_ + 1) * 32], MV[key][:, :], src=MV[key], dst=mvd)
            dram_copy(k, "sp", DBG["hT1"], DBG["hT1"][:, :], hT_own_p[1], hT_own_p[1][:, :])
            dram_copy(k, "sp", DBG["qtm"], DBG["qtm"][:, :], q_tm, q_tm[0:512, :])
            dram_copy(k, "sp", DBG["ktm"], DBG["ktm"][:, :], k_tm, k_tm[0:512, :])
            dram_copy(k, "sp", DBG["vtm"], DBG["vtm"][:, :], v_tm, v_tm[0:512, :])
            dram_copy(k, "sp", DBG["gates"], DBG["gates"][:, :], gates, gates[0:512, :])
            dram_copy(k, "sp", DBG["uTd"], DBG["uTd"][:, :], uT, uT[:, 0:512])
            dram_copy(k, "sp", DBG["qTd"], DBG["qTd"][:, :], qT, qT[:, 0:512])
            dram_copy(k, "sp", DBG["hd0"], DBG["hd0"][:, :], hdir[0], hdir[0][0:512, :])
            dram_copy(k, "sp", DBG["hd1"], DBG["hd1"][:, :], hdir[1], hdir[1][0:512, :])
            dram_copy(k, "sp", DBG["mixsel"], DBG["mixsel"][:, :], mixsel, mixsel[:, 0:256])
            dram_copy(k, "sp", DBG["ysT"], DBG["ysT"][:, :], ysT, ysT[:, 0:256])
            dram_copy(k, "sp", DBG["xA0"], DBG["xA0"][:, :], xA, xA[:, :])
    ffn(0, xA, xB, True)
    ck('ffn0')
    if DBG:
        with k.phase():
            dram_copy(k, "sp", DBG["xB0"], DBG["xB0"][:, :], xB, xB[:, :])

    h1T_own = k.dram("h1T_own", [D, TOWN], BF16)
    h1b_own = [k.dram("h1bo%d" % j, [D, 128], BF16) for j in range(5)]
    h1b_pair = [k.dram("h1bp%d" % j, [2 * D, 128], BF16) for j in range(5)]
    hk = k.dram("hk", [D, 2816], BF16)
    with k.phase():
        norm_mod(k, xB, h1T_own, SEG01, {0: MV[1, "gscm", 0], 1: MV[1, "gscm", 1]}, {0: MV[1, 0, 0], 1: MV[1, 0, 1]}, C["ones_bf"], D)
    with k.phase():
        for j, c0 in enumerate((TCX, TCX + 128, TOWN - 256, TOWN - 128, 0)):
            dram_copy(k, "sp", h1b_own[j], h1b_own[j][:, :], h1T_own, h1T_own[:, c0:c0 + 128])
        dram_copy(k, "act", hk, hk[:, 256:2304], h1T_own, h1T_own[:, TCX:TOWN])
    for j in range(5):
        k.allgather(k.dsem(), h1b_pair[j], h1b_own[j], PAIRS)
    with k.phase():
        scale_cols(k, hk, 0, _RowWin(h1b_pair[2], 0, D), 0, fl, 1, D, 128)
        scale_cols(k, hk, 128, _RowWin(h1b_pair[3], 0, D), 0, fl, 1, D, 128)
        scale_cols(k, hk, 2304, _RowWin(h1b_pair[0], D, 2 * D), 0, fl, 0, D, 128)
        scale_cols(k, hk, 2432, _RowWin(h1b_pair[1], D, 2 * D), 0, fl, 0, D, 128)
        dram_copy(k, "pool", hk, hk[:, 2560:2688], h1b_pair[4], h1b_pair[4][0:D, :])
        dram_copy(k, "pool", hk, hk[:, 2688:2816], h1b_pair[4], h1b_pair[4][D:2 * D, :])
    ck('hk')
    QT = k.dram("QT", [D, TLAT], BF16); KT = k.dram("KT", [D, 2816], BF16); Vn = k.dram("Vn", [2816, D], BF16); attnT = k.dram("attnT", [D, TLAT], BF16)

    def load_hk(off):
        def f(dst, sem, t0, n):
            load_rows(k, "act", sem, dst, dst[:, :, :n], hk, hk[:, off + t0:off + t0 + n])
        return f
    with k.phase():
        gemm(k, "fm", KC, mk_blocks(TLAT, 1024, 512), load_hk(256), WT["w_qkv"], 0, D, 512, copy_epilogue(k, QT, BF16), PsPool(k, 4, "qps"), tag="q")
    with k.phase():
        gemm(k, "fm", KC, mk_blocks(2816, 1024, 512), load_hk(0), WT["w_qkv"], D, D, 512, copy_epilogue(k, KT, BF16), PsPool(k, 4, "kps"), tag="kk")
    with k.phase():
        gemm(k, "tm", KC, mk_blocks(2816, 1024, 128), load_hk(0), WT["w_qkv"], 2 * D, D, 512, copy_epilogue(k, Vn, BF16), PsPool(k, 4, "vps"), tag="vv")
    ck('qkv')
    with k.phase():
        na_phase(k, C, QT, KT, Vn, IN["rpbT"], IN["cmask"], IN["flags"], attnT, 32, 128 ** -0.5)
    with k.phase():
        def load_at(dst, sem, t0, n):
            load_rows(k, "act", sem, dst, dst[:, :, :n], attnT, attnT[:, t0:t0 + n])
        gemm(k, "fm", KC, mk_blocks(TLAT, 1024, 512), load_at, WT["w_out1"], 0, D, 512,
             resid_epilogue(k, xB, xA, [MV[1, 2, 0], MV[1, 2, 1]], tcol0=TCX), PsPool(k, 4, "o1ps"), tag="o1")
    ck('outproj1')
    if DBG:
        with k.phase():
            dram_copy(k, "sp", DBG["xA1"], DBG["xA1"][:, :], xA, xA[:, :])
    ffn(1, xA, xB, False)
    ck('ffn1')
    with k.phase():
        norm_mod(k, xB, None, [(TCX, TLAT, 0)], {0: gfin}, None, C["ones_bf"], D, hT_col0=-TCX, out_f32=outT)
    k.close()


class _RowWin:
    def __init__(self, tk, r0, r1):
        self.tk = tk; self.r0 = r0; self.r1 = r1

    @property
    def w(self):
        return self.tk.w

    @w.setter
    def w(self, v):
        self.tk.w = v

    @property
    def r(self):
        return self.tk.r

    @r.setter
    def r(self, v):
        self.tk.r = v

    def __getitem__(self, idx):
        return self.tk.t[self.r0:self.r1, :][idx]


class _Sub3:
    def __init__(self, tk, l):
        self.tk = tk; self.l = l; self.w = tk.w; self.r = tk.r

    def __getitem__(self, idx):
        return self.tk.t[:, self.l, :, :][idx]


class _Sub2:
    def __init__(self, tk, l):
        self.tk = tk; self.l = l; self.w = tk.w; self.r = tk.r

    def __getitem__(self, idx):
        return self.tk.t[:, self.l, :][idx]


_NC_CACHE = {}


def _rope_table():
    inv = (10000.0 ** (-np.arange(0, 128, 2, dtype=np.float32) / np.float32(128))).astype(np.float32)
    tab = np.zeros((TP, 2, 2, 64), np.float32)
    tab[:, 0] = 1.0
    for r in range(2):
        t = r * TLAT + np.arange(TLAT)
        row = (t // 64).astype(np.float32)[:, None] * inv[None, :]
        col = (t % 64).astype(np.float32)[:, None] * inv[None, :]
        base = r * TOWN + TCX
        tab[base:base + TLAT, 0, 0] = np.cos(row); tab[base:base + TLAT, 0, 1] = np.cos(col)
        tab[base:base + TLAT, 1, 0] = np.sin(row); tab[base:base + TLAT, 1, 1] = np.sin(col)
    return tab.reshape(TP, 256)


def make_in_maps(x, c, ctx, c_ctx, mod_w, mod_b, norm_mix_g, norm_ffn_g, ab_w_in, mlstm_gate_b,
                 mlstm_head_g, s5_a_re, s5_a_im, s5_log_dt, s5_b_re, s5_b_im, s5_c_re, s5_c_im,
                 s5_d, s5_glu_w, s5_glu_b, ab_w_out, na_w_qkv, na_rpb, na_w_out,
                 ffn_w_up, ffn_conv_w, ffn_conv_b, ffn_w_down, final_norm_g):
    f = lambda a: np.asarray(a, dtype=np.float32)
    x, c, ctx, c_ctx = f(x), f(c), f(ctx), f(c_ctx)
    cond = np.concatenate([c, c_ctx[None, :]], 0)
    condT = np.ascontiguousarray(cond.reshape(5, 32, 128).transpose(2, 1, 0))
    mod_w, mod_b = f(mod_w), f(mod_b)
    pl = lambda g: np.ascontiguousarray(f(g).reshape(2, 32, 128).transpose(2, 0, 1))
    gmix, gffn = pl(norm_mix_g), pl(norm_ffn_g)
    gfin = np.ascontiguousarray(f(final_norm_g).reshape(32, 128).T)
    consts = host_consts()
    ropeT = _rope_table()
    Tg, cmask = na_host_tables(f(na_rpb)[0])
    conv_w = np.ascontiguousarray(f(ffn_conv_w).reshape(2, 3, NJC, 128).transpose(3, 0, 1, 2))
    conv_b = np.ascontiguousarray(f(ffn_conv_b).reshape(2, NJC, 128).transpose(2, 0, 1))
    glu_b = np.ascontiguousarray(f(s5_glu_b)[0].reshape(8, 128).T)
    w_in = f(ab_w_in)[0]; gate_b = f(mlstm_gate_b)[0]; head_g = f(mlstm_head_g)[0]
    in_maps = []
    for core in range(8):
        b, g = core // 2, core % 2
        m = {}
        m["xin"] = np.ascontiguousarray(np.concatenate([ctx[b, g * TCX:(g + 1) * TCX], x[b, g * TLAT:(g + 1) * TLAT]], 0).T)
        m["condT"] = condT
        oh = np.zeros((128, 5), np.float32); oh[:, b] = 1.0
        m["onehot"] = oh
        fl = np.zeros((128, 4), np.float32); fl[:, 0] = 1 - g; fl[:, 1] = g; fl[:, 2] = g; fl[:, 3] = 1 - g
        m["flags"] = fl
        m["modw"] = np.stack([np.concatenate([mod_w[l][:, w * D + core * 512:w * D + (core + 1) * 512] for w in range(6)], 1) for l in range(2)], 0)
        m["modb"] = np.concatenate([np.concatenate([mod_b[l][w * D + core * 512:w * D + (core + 1) * 512] for w in range(6)]) for l in range(2)])[None, :]
        m["gmix"] = gmix; m["gffn"] = gffn; m["gfin"] = gfin; m["consts"] = consts
        gcols = [9216 + d_ * 12 + i_ * 6 + 3 * g + h_ for d_ in range(2) for i_ in range(2) for h_ in range(3)]
        cols = np.concatenate([np.arange(g * 768, (g + 1) * 768), 1536 + np.arange(g * 768, (g + 1) * 768),
                               3072 + np.arange(g * 1536, (g + 1) * 1536), 6144 + np.arange(g * 1536, (g + 1) * 1536),
                               np.array(gcols), 9240 + np.arange(g * 512, (g + 1) * 512)])
        wi = np.zeros((1024, WIN_N), np.float32)
        wsel = w_in[b * 1024:(b + 1) * 1024][:, cols]
        wi[:, :4620] = wsel[:, :4620]
        wi[:, WIN_U0:WIN_U0 + 512] = wsel[:, 4620:5132]
        m["w_in"] = wi
        m["w_glu"] = f(s5_glu_w)[0][core * 128:(core + 1) * 128]
        m["w_out0"] = f(ab_w_out)[0][core * 512:(core + 1) * 512]
        m["w_up0"] = f(ffn_w_up)[0][core * 512:(core + 1) * 512]; m["w_up1"] = f(ffn_w_up)[1][core * 512:(core + 1) * 512]
        r8 = DFF // 8
        m["w_dn0"] = f(ffn_w_down)[0][core * r8:(core + 1) * r8]; m["w_dn1"] = f(ffn_w_down)[1][core * r8:(core + 1) * r8]
        m["w_qkv"] = f(na_w_qkv)[0][core * 512:(core + 1) * 512]
        m["w_out1"] = f(na_w_out)[0][core * 512:(core + 1) * 512]
        m["gate_b"] = np.ascontiguousarray(gate_b[:, :, 3 * g:3 * g + 3]).reshape(1, 12)
        m["head_g"] = head_g[g * 1536:(g + 1) * 1536][None, :]
        m["ropeT"] = ropeT
        gs = slice(32 * g, 32 * g + 32)
        par, Bre, Bim, Cre, Cim, dsk = s5_host_layout(f(s5_a_re)[0][:, gs], f(s5_a_im)[0][:, gs], f(s5_log_dt)[0][:, gs], f(s5_b_re)[0][gs], f(s5_b_im)[0][gs],
                                                      f(s5_c_re)[0][gs], f(s5_c_im)[0][gs], f(s5_d)[0][g * 512:(g + 1) * 512])
        m["s5par"] = par; m["s5Bre"] = Bre; m["s5Bim"] = Bim; m["s5Cre"] = Cre; m["s5Cim"] = Cim; m["s5dsk"] = dsk
        m["glu_b"] = glu_b; m["conv_w"] = conv_w; m["conv_b"] = conv_b; m["rpbT"] = Tg; m["cmask"] = cmask
        in_maps.append(m)
    return in_maps


def kernel(**inputs):
    if "nc" not in _NC_CACHE:
        _NC_CACHE["nc"] = build_program()
    nc = _NC_CACHE["nc"]
    in_maps = make_in_maps(**inputs)
    res = run_bass_kernel_spmd(nc, in_maps, core_ids=list(range(8)))
    out = np.empty((4, 4096, 4096), np.float32)
    for core in range(8):
        b, g = core // 2, core % 2
        out[b, g * TLAT:(g + 1) * TLAT, :] = res.results[core]["outT"].T
    return out
```

```python
import contextlib
import numpy as np
import ml_dtypes
import concourse.bass as bass
import concourse.mybir as mybir
from concourse.bass_utils import run_bass_kernel_spmd

F32 = mybir.dt.float32
BF16 = mybir.dt.bfloat16
AF = mybir.ActivationFunctionType
ALU = mybir.AluOpType
AX = mybir.AxisListType
NPBF = ml_dtypes.bfloat16


class Tk:
    __slots__ = ("t", "w", "r", "name")

    def __init__(self, t, name=""):
        self.t = t
        self.w = []
        self.r = {}
        self.name = name

    def __getitem__(self, idx):
        return self.t[idx]


class K:
    def __init__(self, nc):
        self.nc = nc
        self.es = contextlib.ExitStack()
        self.eng = {"pe": nc.tensor, "act": nc.scalar, "dve": nc.vector, "pool": nc.gpsimd, "sp": nc.sync}
        self.sems = {}
        self.cnt = {}
        self.waited = {e: {} for e in self.eng}
        for e in ("pe", "act", "dve", "pool", "sp"):
            self.sems[e] = self.es.enter_context(nc.semaphore("s_" + e))
            self.cnt[e] = 0
        self.free_dsems = []
        self.phase_dsems = []
        self.nd = 0
        self.pst = None
        self.uid = 0

    def _stack(self):
        return self.pst if self.pst is not None else self.es

    def sb(self, name, shape, dt):
        self.uid += 1
        return Tk(self._stack().enter_context(self.nc.sbuf_tensor("%s_%d" % (name, self.uid), list(shape), dt)), name)

    def ps(self, name, shape, dt=F32):
        self.uid += 1
        return Tk(self._stack().enter_context(self.nc.psum_tensor("%s_%d" % (name, self.uid), list(shape), dt)), name)

    def dram(self, name, shape, dt, kind="Internal"):
        return Tk(self.nc.dram_tensor(name, list(shape), dt, kind=kind).ap(), name)

    def dsem(self):
        if self.free_dsems:
            key = self.free_dsems.pop()
        else:
            self.nd += 1
            key = "d%d" % self.nd
            self.sems[key] = self.es.enter_context(self.nc.semaphore(key))
            self.cnt[key] = 0
        if self.pst is not None:
            self.phase_dsems.append(key)
        return key

    @contextlib.contextmanager
    def phase(self):
        assert self.pst is None
        self.pst = contextlib.ExitStack()
        self.phase_dsems = []
        yield
        self.barrier()
        self.pst.close()
        self.pst = None
        self.free_dsems += self.phase_dsems
        self.phase_dsems = []

    def _wait(self, e, deps):
        best = {}
        for (s, v) in deps:
            if s == "pe" and e == "pe":
                continue
            if best.get(s, 0) < v:
                best[s] = v
        for s, v in best.items():
            if self.waited[e].get(s, 0) < v:
                self.eng[e].wait_ge(self.sems[s], v)
                self.waited[e][s] = v

    def op(self, e, fn, reads=(), writes=(), sig=True):
        deps = []
        for t in reads:
            deps += t.w
        for t in writes:
            deps += t.w
            deps += list(t.r.items())
        self._wait(e, deps)
        inst = fn(self.eng[e])
        if sig:
            self.cnt[e] += 1
            inst.then_inc(self.sems[e], 1)
            v = self.cnt[e]
        else:
            v = self.cnt[e] + 1
        for t in reads:
            if t.r.get(e, 0) < v:
                t.r[e] = v
        for t in writes:
            t.w = [(e, v)]
            t.r = {}
        return inst

    def dma(self, q, sem, out_ap, in_ap, dst=None, src=None, **kw):
        deps = []
        if src is not None:
            deps += src.w
        if dst is not None:
            deps += dst.w
            deps += list(dst.r.items())
        self._wait(q, deps)
        inst = self.eng[q].dma_start(out=out_ap, in_=in_ap, **kw)
        inst.then_inc(self.sems[sem], 16)
        self.cnt[sem] += 16
        v = self.cnt[sem]
        if src is not None and src.r.get(sem, 0) < v:
            src.r[sem] = v
        if dst is not None:
            dst.w = [(sem, v)]
            dst.r = {}
        return inst

    def allgather(self, sem, out_tk, in_tk, groups):
        deps = list(in_tk.w) + list(out_tk.w) + list(out_tk.r.items())
        self._wait("pool", deps)
        inst = self.nc.gpsimd.collective_compute("AllGather", op=ALU.bypass, replica_groups=groups,
                                                 ins=[in_tk[:]], outs=[out_tk[:]])
        inst.then_inc(self.sems[sem], 1)
        self.cnt[sem] += 1
        v = self.cnt[sem]
        in_tk.r[sem] = v
        out_tk.w = [(sem, v)]
        out_tk.r = {}

    def barrier(self):
        sp = self.eng["sp"]
        lazy = getattr(self, "lazy", set())
        for s, v in self.cnt.items():
            if s == "sp" or v == 0 or s in lazy:
                continue
            if self.waited["sp"].get(s, 0) < v:
                sp.wait_ge(self.sems[s], v)
                self.waited["sp"][s] = v
        self.cnt["sp"] += 1
        sp.nop().then_inc(self.sems["sp"], 1)
        for e in ("pe", "act", "dve", "pool"):
            self.eng[e].wait_ge(self.sems["sp"], self.cnt["sp"])
            for s, v in self.cnt.items():
                if s not in lazy:
                    self.waited[e][s] = v
        for s, v in self.cnt.items():
            if s not in lazy:
                self.waited["sp"][s] = v

    def close(self):
        self.es.close()


def ceil_div(a, b):
    return (a + b - 1) // b


class PsPool:
    def __init__(self, k, n, name="ps", shape=(128, 512), dt=F32):
        self.t = [k.ps("%s%d" % (name, i), shape, dt) for i in range(n)]
        self.i = 0

    def get(self):
        t = self.t[self.i % len(self.t)]
        self.i += 1
        return t


def load_rows(k, q, sem, dst_tk, dst_ap, src_tk, src_ap):
    k.dma(q, sem, dst_ap, src_ap.rearrange("(kc p) t -> p kc t", p=128), dst=dst_tk, src=src_tk)


def gemm(k, mode, KC, blocks, load_a, W, n0, N, NB, epi, pp, abufs=1, wbufs=2, tag="g", wq="sp", TBmax=None):
    TBmax = TBmax or max(b[1] for b in blocks)
    A = [k.sb(tag + "A%d" % i, [128, KC, TBmax], BF16) for i in range(abufs)]
    Asem = [k.dsem() for _ in range(abufs)]
    Wt = [k.sb(tag + "W%d" % i, [128, KC, NB], BF16) for i in range(wbufs)]
    Wsem = [k.dsem() for _ in range(wbufs)]
    wblocks = [(wb, min(NB, N - wb)) for wb in range(0, N, NB)]
    steps = [(bi, wi) for bi in range(len(blocks)) for wi in range(len(wblocks))]

    def issue_a(bi):
        tb0, ntb, _ = blocks[bi]
        load_a(A[bi % abufs], Asem[bi % abufs], tb0, ntb)

    def issue_w(si):
        bi, wi = steps[si]
        wb, ncols = wblocks[wi]
        wt = Wt[si % wbufs]
        if callable(W):
            wtk, wap = W(n0 + wb, ncols)
        else:
            wtk, wap = W, W[:, n0 + wb:n0 + wb + ncols]
        k.dma(wq, Wsem[si % wbufs], wt[:, :, :ncols], wap.rearrange("(kc p) n -> p kc n", p=128), dst=wt, src=wtk)

    issue_a(0)
    issue_w(0)
    for si, (bi, wi) in enumerate(steps):
        nxt_a = si + 1 < len(steps) and steps[si + 1][1] == 0
        if si + 1 < len(steps):
            if nxt_a and abufs >= 2:
                issue_a(steps[si + 1][0])
            issue_w(si + 1)
        tb0, ntb, subs = blocks[bi]
        wb, ncols = wblocks[wi]
        at = A[bi % abufs]
        wt = Wt[si % wbufs]
        if mode == "fm":
            for nch in range(0, ncols, 128):
                m = min(128, ncols - nch)
                for (ts, n) in subs:
                    ps = pp.get()
                    for kc in range(KC):
                        k.op("pe", lambda e: e.matmul(out=ps[:m, :n], lhsT=wt[:, kc, nch:nch + m],
                                                     rhs=at[:, kc, ts:ts + n], start=(kc == 0), stop=(kc == KC - 1)),
                             reads=[wt, at], writes=[ps], sig=(kc == KC - 1))
                    epi(ps, m, n, wb + nch, tb0 + ts, at=at, ts=ts)
        else:
            for (ts, mtok) in subs:
                ps = pp.get()
                for kc in range(KC):
                    k.op("pe", lambda e: e.matmul(out=ps[:mtok, :ncols], lhsT=at[:, kc, ts:ts + mtok],
                                                 rhs=wt[:, kc, :ncols], start=(kc == 0), stop=(kc == KC - 1)),
                         reads=[wt, at], writes=[ps], sig=(kc == KC - 1))
                epi(ps, mtok, ncols, tb0 + ts, wb, at=at, ts=ts)
        if nxt_a and abufs < 2:
            issue_a(steps[si + 1][0])


def mk_blocks(T, TB, sub, bounds=()):
    blocks = []
    cuts = sorted(set([0, T] + [b for b in bounds if 0 < b < T]))
    segs = [(cuts[i], cuts[i + 1]) for i in range(len(cuts) - 1)]
    t = 0
    while t < T:
        n = min(TB, T - t)
        subs = []
        for (s0, s1) in segs:
            a, b = max(s0, t), min(s1, t + n)
            x = a
            while x < b:
                c = min(sub, b - x)
                subs.append((x - t, c))
                x += c
        blocks.append((t, n, subs))
        t += n
    return blocks


class Stage:
    def __init__(self, k, n, name, shape, dt):
        self.t = [k.sb("%s%d" % (name, i), shape, dt) for i in range(n)]
        self.s = [k.dsem() for _ in range(n)]
        self.i = 0

    def get(self):
        j = self.i % len(self.t)
        self.i += 1
        return self.t[j], self.s[j]


def cast_rows(k, src, dst, R, N, cw=2048, q_in="sp", q_out="act", sc0=0, bufs=None):
    nb = 3
    if bufs is None:
        bufs = cast_bufs(k, cw)
    tin, tout, sin, sout, ctr = bufs
    i = ctr[0]
    for r0 in range(0, R, 128):
        rr = min(128, R - r0)
        for c0 in range(0, N, cw):
            cc = min(cw, N - c0)
            b = i % nb
            k.dma(q_in, sin[b], tin[b][:rr, :cc], src[r0:r0 + rr, sc0 + c0:sc0 + c0 + cc], dst=tin[b], src=src)
            if i % 2 == 0:
                k.op("act", lambda e: e.copy(out=tout[b][:rr, :cc], in_=tin[b][:rr, :cc]), reads=[tin[b]], writes=[tout[b]])
            else:
                k.op("dve", lambda e: e.tensor_copy(out=tout[b][:rr, :cc], in_=tin[b][:rr, :cc]), reads=[tin[b]], writes=[tout[b]])
            k.dma(q_out, sout[b], dst[r0:r0 + rr, c0:c0 + cc], tout[b][:rr, :cc], src=tout[b], dst=dst)
            i += 1
    ctr[0] = i


def cast_rows_pieces(k, src, pieces, R, N, w, cw=2048):
    tin, tout, sin, sout, ctr = cast_bufs(k, cw)
    nb = 3
    i = 0
    for r0 in range(0, R, 128):
        rr = min(128, R - r0)
        for c0 in range(0, N, cw):
            cc = min(cw, N - c0)
            b = i % nb
            k.dma("sp", sin[b], tin[b][:rr, :cc], src[r0:r0 + rr, c0:c0 + cc], dst=tin[b], src=src)
            if i % 2 == 0:
                k.op("act", lambda e: e.copy(out=tout[b][:rr, :cc], in_=tin[b][:rr, :cc]), reads=[tin[b]], writes=[tout[b]])
            else:
                k.op("dve", lambda e: e.tensor_copy(out=tout[b][:rr, :cc], in_=tin[b][:rr, :cc]), reads=[tin[b]], writes=[tout[b]])
            for p0 in range(c0, c0 + cc, w):
                pc = pieces[p0 // w]
                k.dma("act", sout[b], pc[r0:r0 + rr, :], tout[b][:rr, p0 - c0:p0 - c0 + w], src=tout[b], dst=pc)
            i += 1


def cast_bufs(k, cw=2048):
    nb = 3
    return ([k.sb("cin%d" % i, [128, cw], F32) for i in range(nb)], [k.sb("cout%d" % i, [128, cw], BF16) for i in range(nb)],
            [k.dsem() for _ in range(nb)], [k.dsem() for _ in range(nb)], [0])


def norm_mod(k, xT, hT, segs, gsc, sh, ones_bf, D, eps=1e-6, hT_col0=0, TBN=256, out_f32=None, writer=None):
    KC = D // 128
    nb = 2
    xt = [k.sb("nx%d" % i, [128, KC, TBN], F32) for i in range(nb)]
    xs = [k.dsem() for _ in range(nb)]
    sq = k.sb("nsq", [128, KC, TBN], BF16)
    ht = [k.sb("nh%d" % i, [128, KC, TBN], BF16 if out_f32 is None else F32) for i in range(nb)]
    hs = [k.dsem() for _ in range(nb)]
    rs = k.sb("nrs", [128, TBN], F32)
    pp = PsPool(k, 2, "nps")
    i = 0
    for (t0, n, st) in segs:
        for a in range(t0, t0 + n, TBN):
            m = min(TBN, t0 + n - a)
            b = i % nb
            x = xt[b]
            k.dma("sp", xs[b], x[:, :, :m], xT[:, a:a + m].rearrange("(kc p) t -> p kc t", p=128), dst=x, src=xT)
            k.op("act", lambda e: e.activation(out=sq[:, :, :m], in_=x[:, :, :m], func=AF.Square), reads=[x], writes=[sq])
            ps = pp.get()
            for kc in range(KC):
                k.op("pe", lambda e: e.matmul(out=ps[:, :m], lhsT=ones_bf[:, :], rhs=sq[:, kc, :m], start=(kc == 0), stop=(kc == KC - 1)),
                     reads=[sq, ones_bf], writes=[ps], sig=(kc == KC - 1))
            k.op("act", lambda e: e.activation(out=rs[:, :m], in_=ps[:, :m], func=AF.Sqrt, scale=1.0 / D, bias=k.eps_t[:, 0:1]), reads=[ps, k.eps_t], writes=[rs])
            k.op("dve", lambda e: e.reciprocal(out=rs[:, :m], in_=rs[:, :m]), reads=[rs], writes=[rs])
            k.op("dve", lambda e: e.tensor_tensor(out=x[:, :, :m], in0=x[:, :, :m], in1=rs[:, :m].unsqueeze(1).broadcast_to([128, KC, m]), op=ALU.mult),
                 reads=[x, rs], writes=[x])
            h = ht[b]
            k.op("dve", lambda e: e.tensor_tensor(out=x[:, :, :m], in0=x[:, :, :m], in1=gsc[st][:, :].unsqueeze(2).broadcast_to([128, KC, m]), op=ALU.mult),
                 reads=[x, gsc[st]], writes=[x])
            if sh is not None:
                k.op("pool", lambda e: e.tensor_tensor(out=h[:, :, :m], in0=x[:, :, :m], in1=sh[st][:, :].unsqueeze(2).broadcast_to([128, KC, m]), op=ALU.add),
                     reads=[x, sh[st]], writes=[h])
            else:
                k.op("act", lambda e: e.copy(out=h[:, :, :m], in_=x[:, :, :m]), reads=[x], writes=[h])
            dstT = hT if out_f32 is None else out_f32
            if writer is not None:
                writer(h, a, m, hs[b])
            else:
                k.dma("act", hs[b], dstT[:, hT_col0 + a:hT_col0 + a + m].rearrange("(kc p) t -> p kc t", p=128), h[:, :, :m], src=h, dst=dstT)
            i += 1


def mlstm_scan(k, C, qT, kT, ktm, vtm, gates, gb_bc, hdir, orders, NH):
    L = 128
    pg = k.ps("mg", [128, 512]); psc = k.ps("msc", [128, 512]); pden = k.ps("mden", [128, 512])
    pnum = [k.ps("mnum%d" % i, [128, 512]) for i in range(2)]
    pS = [k.ps("mS%d" % i, [128, 512]) for i in range(2)]
    S32 = {}; Sbf = {}; n32 = {}; nbf = {}
    for d in range(2):
        for h in range(NH):
            for dc in range(2):
                S32[d, h, dc] = k.sb("S32", [128, 512], F32); Sbf[d, h, dc] = S32[d, h, dc]
                n32[d, h, dc] = k.sb("n32", [128, 2], F32); nbf[d, h, dc] = n32[d, h, dc]
                for t in (S32[d, h, dc], n32[d, h, dc]):
                    k.op("dve", lambda e: e.memset(t[:, :], 0.0), writes=[t])
    nb = 2
    bufs = []
    for i in range(nb):
        bufs.append(dict(q=k.sb("mq", [128, NH * 2, L], F32), kt=k.sb("mkt", [128, NH * 2, L], F32),
                         km=k.sb("mkm", [128, NH * 256], F32), v=k.sb("mv", [128, NH * 512], F32),
                         g=k.sb("mgt", [128, 4 * NH], F32), sem=k.dsem()))
    G = k.sb("mG", [128, 2 * NH], F32); nlf = k.sb("mnlf", [128, NH], F32); lf = k.sb("mlf", [128, NH], F32)
    Fc = k.sb("mFc", [128, NH], F32); aa = k.sb("maa", [128, NH], F32)
    ly0 = k.sb("mly0", [128, NH], F32); lt = k.sb("mlt", [128, NH], F32)
    Lt = [k.sb("mLt%d" % i, [128, 128], F32) for i in range(2)]
    Dm = [k.sb("mDm%d" % i, [128, 128], F32) for i in range(2)]
    Wm = [k.sb("mWm%d" % i, [128, 128], F32) for i in range(2)]
    EF = [k.sb("mEF%d" % i, [128, 128], F32) for i in range(2)]
    qs = [k.sb("mqs%d" % i, [128, 2, 128], F32) for i in range(2)]
    scw = [k.sb("mscw%d" % i, [128, 128], F32) for i in range(2)]
    kw = [k.sb("mkw%d" % i, [128, 256], F32) for i in range(2)]
    wsrc = [k.sb("mws%d" % i, [128, 1], F32) for i in range(2)]
    rd = [k.sb("mrd%d" % i, [128, 1], F32) for i in range(2)]
    ho = Stage(k, 3, "mho", [128, 512], F32)
    it = 0
    nsteps = len(orders[0])
    for s in range(nsteps):
        for d in range(2):
            ci = orders[d][s]
            t0 = ci * L
            B = bufs[it % nb]; it += 1
            sem = B["sem"]
            k.dma("sp", sem, B["q"][:, :, :], qT[:, t0:t0 + L].rearrange("(c p) t -> p c t", p=128), dst=B["q"], src=qT)
            k.dma("sp", sem, B["kt"][:, :, :], kT[:, t0:t0 + L].rearrange("(c p) t -> p c t", p=128), dst=B["kt"], src=kT)
            k.dma("act", sem, B["km"][:, :], ktm[t0:t0 + L, :], dst=B["km"], src=ktm)
            k.dma("act", sem, B["v"][:, :], vtm[t0:t0 + L, :], dst=B["v"], src=vtm)
            k.dma("sp", sem, B["g"][:, :], gates[t0:t0 + L, :], dst=B["g"], src=gates)
            lastw = B["g"].w
            for nm in ("q", "kt", "km", "v"):
                B[nm].w = list(lastw)
            tri, mm = C["tri"][d], C["mm"][d]
            lc = L - 1 if d == 0 else 0
            k.op("dve", lambda e: e.tensor_tensor(out=G[:, :], in0=B["g"][:, d * 2 * NH:(d + 1) * 2 * NH], in1=gb_bc[:, d * 2 * NH:(d + 1) * 2 * NH], op=ALU.add),
                 reads=[B["g"], gb_bc], writes=[G])
            k.op("act", lambda e: e.activation(out=nlf[:, :], in_=G[:, NH:2 * NH], func=AF.Exp, scale=-1.0), reads=[G], writes=[nlf])
            k.op("act", lambda e: e.activation(out=ly0[:, :], in_=nlf[:, :], func=AF.Ln, bias=C["one_c"][:, 0:1], scale=1.0), reads=[nlf, C["one_c"]], writes=[ly0])
            k.op("act", lambda e: e.activation(out=lt[:, :], in_=ly0[:, :], func=AF.Exp, scale=-1.0), reads=[ly0], writes=[lt])
            k.op("dve", lambda e: e.scalar_tensor_tensor(out=lt[:, :], in0=nlf[:, :], scalar=1.0, in1=lt[:, :], op0=ALU.add, op1=ALU.mult), reads=[nlf, lt], writes=[lt])
            k.op("dve", lambda e: e.scalar_tensor_tensor(out=nlf[:, :], in0=lt[:, :], scalar=-1.0, in1=ly0[:, :], op0=ALU.add, op1=ALU.add), reads=[lt, ly0], writes=[nlf])
            k.op("dve", lambda e: e.tensor_scalar(out=lf[:, :], in0=nlf[:, :], scalar1=-1.0, scalar2=None, op0=ALU.mult), reads=[nlf], writes=[lf])
            k.op("pe", lambda e: e.matmul(out=pg[:, 384:384 + NH], lhsT=tri[:, :], rhs=lf[:, :], start=True, stop=True), reads=[tri, lf], writes=[pg])
            k.op("dve", lambda e: e.tensor_copy(out=Fc[:, :], in_=pg[:, 384:384 + NH]), reads=[pg], writes=[Fc])
            k.op("dve", lambda e: e.tensor_tensor(out=aa[:, :], in0=G[:, 0:NH], in1=Fc[:, :], op=ALU.subtract), reads=[G, Fc], writes=[aa])
            for h in range(NH):
                j = h % 2
                k.op("dve", lambda e: e.tensor_scalar(out=Lt[j][:, :], in0=C["ones_f"][:, :], scalar1=lf[:, h:h + 1], scalar2=None, op0=ALU.mult),
                     reads=[C["ones_f"], lf], writes=[Lt[j]])
                fr = pg[:, h * 128:(h + 1) * 128]
                k.op("pe", lambda e: e.matmul(out=fr, lhsT=Lt[j][:, :], rhs=tri[:, :], start=True, stop=True), reads=[Lt[j], tri], writes=[pg])
                k.op("dve", lambda e: e.scalar_tensor_tensor(out=Dm[j][:, :], in0=fr, scalar=Fc[:, h:h + 1], in1=mm[:, :], op0=ALU.subtract, op1=ALU.min),
                     reads=[pg, Fc, mm], writes=[Dm[j]])
                k.op("act", lambda e: e.activation(out=Wm[j][:, :], in_=Dm[j][:, :], func=AF.Exp, bias=G[:, h:h + 1], scale=1.0), reads=[Dm[j], G], writes=[Wm[j]])
                k.op("act", lambda e: e.activation(out=EF[j][:, :], in_=fr, func=AF.Exp), reads=[pg], writes=[EF[j]])
                k.op("act", lambda e: e.activation(out=wsrc[j][:, :], in_=pg[:, h * 128 + lc:h * 128 + lc + 1], func=AF.Exp, bias=aa[:, h:h + 1], scale=1.0),
                     reads=[pg, aa], writes=[wsrc[j]])
                k.op("dve", lambda e: e.tensor_tensor(out=qs[j][:, :, :], in0=B["q"][:, 2 * h:2 * h + 2, :], in1=EF[j][:, :].unsqueeze(1).broadcast_to([128, 2, 128]), op=ALU.mult),
                     reads=[B["q"], EF[j]], writes=[qs[j]])
                sc = psc[:, h * 128:(h + 1) * 128]
                for dc in range(2):
                    k.op("pe", lambda e: e.matmul(out=sc, lhsT=B["kt"][:, 2 * h + dc, :], rhs=B["q"][:, 2 * h + dc, :], start=(dc == 0), stop=(dc == 1)),
                         reads=[B["kt"], B["q"]], writes=[psc], sig=(dc == 1))
                k.op("dve", lambda e: e.tensor_tensor(out=scw[j][:, :], in0=sc, in1=Wm[j][:, :], op=ALU.mult), reads=[psc, Wm[j]], writes=[scw[j]])
                pn = pnum[h % 2]
                vv = B["v"][:, h * 512:(h + 1) * 512]
                k.op("pe", lambda e: e.matmul(out=pn[:, :], lhsT=scw[j][:, :], rhs=vv, start=True, stop=False), reads=[scw[j], B["v"]], writes=[pn], sig=False)
                for dc in range(2):
                    k.op("pe", lambda e: e.matmul(out=pn[:, :], lhsT=qs[j][:, dc, :], rhs=Sbf[d, h, dc][:, :], start=False, stop=(dc == 1)),
                         reads=[qs[j], Sbf[d, h, dc]], writes=[pn], sig=(dc == 1))
                dn = pden[:, 2 * h:2 * h + 2]
                k.op("pe", lambda e: e.matmul(out=dn, lhsT=scw[j][:, :], rhs=C["ones_f"][:, 0:2], start=True, stop=False), reads=[scw[j], C["ones_f"]], writes=[pden], sig=False)
                for dc in range(2):
                    k.op("pe", lambda e: e.matmul(out=dn, lhsT=qs[j][:, dc, :], rhs=nbf[d, h, dc][:, :], start=False, stop=(dc == 1)),
                         reads=[qs[j], nbf[d, h, dc]], writes=[pden], sig=(dc == 1))
                k.op("act", lambda e: e.activation(out=rd[j][:, :], in_=pden[:, 2 * h:2 * h + 1], func=AF.Abs), reads=[pden], writes=[rd[j]])
                k.op("dve", lambda e: e.tensor_scalar(out=rd[j][:, :], in0=rd[j][:, :], scalar1=1.0, scalar2=None, op0=ALU.max), reads=[rd[j]], writes=[rd[j]])
                k.op("dve", lambda e: e.reciprocal(out=rd[j][:, :], in_=rd[j][:, :]), reads=[rd[j]], writes=[rd[j]])
                st, ssem = ho.get()
                k.op("act", lambda e: e.activation(out=st[:, :], in_=pn[:, :], func=AF.Copy, scale=rd[j][:, 0:1]), reads=[pn, rd[j]], writes=[st])
                k.dma("sp", ssem, hdir[d][t0:t0 + L, h * 512:(h + 1) * 512], st[:, :], src=st, dst=hdir[d])
                k.op("pool", lambda e: e.tensor_scalar(out=kw[j][:, :], in0=B["km"][:, h * 256:(h + 1) * 256], scalar1=wsrc[j][:, 0:1], scalar2=None, op0=ALU.mult),
                     reads=[B["km"], wsrc[j]], writes=[kw[j]])
                for dc in range(2):
                    k.op("pe", lambda e: e.matmul(out=pS[dc][:, :], lhsT=kw[j][:, dc * 128:(dc + 1) * 128], rhs=vv, start=True, stop=True), reads=[kw[j], B["v"]], writes=[pS[dc]])
                    nn = pden[:, 16 + 4 * h + 2 * dc:16 + 4 * h + 2 * dc + 2]
                    k.op("pe", lambda e: e.matmul(out=nn, lhsT=kw[j][:, dc * 128:(dc + 1) * 128], rhs=C["ones_f"][:, 0:2], start=True, stop=True), reads=[kw[j], C["ones_f"]], writes=[pden])
                    dec = EF[j][:, lc:lc + 1]
                    k.op("dve", lambda e: e.scalar_tensor_tensor(out=S32[d, h, dc][:, :], in0=S32[d, h, dc][:, :], scalar=dec, in1=pS[dc][:, :], op0=ALU.mult, op1=ALU.add),
                         reads=[S32[d, h, dc], EF[j], pS[dc]], writes=[S32[d, h, dc]])
                    k.op("dve", lambda e: e.scalar_tensor_tensor(out=n32[d, h, dc][:, :], in0=n32[d, h, dc][:, :], scalar=dec, in1=nn, op0=ALU.mult, op1=ALU.add),
                         reads=[n32[d, h, dc], EF[j], pden], writes=[n32[d, h, dc]])


def host_consts():
    idx = np.arange(128)
    tri_f = (idx[:, None] <= idx[None, :]).astype(np.float32)
    tri_b = tri_f.T.copy()
    mm_f = (tri_f - 1.0) * 30000.0
    mm_b = (tri_b - 1.0) * 30000.0
    ident = np.eye(128, dtype=np.float32)
    return np.concatenate([tri_f, tri_b, mm_f, mm_b, ident], axis=1).astype(np.float32)


def load_consts(k, cdram):
    C = {}
    ct = k.sb("consts", [128, 640], F32)
    s = k.dsem()
    k.dma("sp", s, ct[:, :], cdram[:, :], dst=ct)
    C["raw"] = ct

    class View:
        def __init__(self, tk, a, b):
            self.tk = tk; self.a = a; self.b = b;
        def __getitem__(self, idx):
            return self.tk.t[:, self.a:self.b][idx]
    def view(a, b):
        v = Tk.__new__(Tk)
        v.t = _Sub(ct, a, b); v.w = ct.w; v.r = ct.r; v.name = "cv"
        return v
    C["tri"] = [view(0, 128), view(128, 256)]
    C["mm"] = [view(256, 384), view(384, 512)]
    C["ident"] = view(512, 640)
    of = k.sb("ones_f", [128, 128], F32); ob = k.sb("ones_bf", [128, 128], BF16); oc = k.sb("one_c", [128, 1], F32)
    ib = k.sb("ident_bf", [128, 128], BF16); ep = k.sb("eps_t", [128, 1], F32)
    k.op("dve", lambda e: e.memset(of[:, :], 1.0), writes=[of])
    k.op("dve", lambda e: e.memset(ob[:, :], 1.0), writes=[ob])
    k.op("dve", lambda e: e.memset(oc[:, :], 1.0), writes=[oc])
    k.op("dve", lambda e: e.memset(ep[:, :], 1e-6), writes=[ep])
    k.op("dve", lambda e: e.tensor_copy(out=ib[:, :], in_=ct[:, 512:640]), reads=[ct], writes=[ib])
    C["ones_f"] = of; C["ones_bf"] = ob; C["one_c"] = oc; C["ident_bf"] = ib
    k.eps_t = ep
    return C


class _Sub:
    def __init__(self, tk, a, b):
        self.tk = tk; self.a = a; self.b = b

    def __getitem__(self, idx):
        return self.tk.t[:, self.a:self.b][idx]


TWO_PI = 6.283185307179586


def _cmul_sc(k, eng, o_re, o_im, a_re, a_im, p_re, p_im, tmp, tks_r, tks_w):
    k.op(eng, lambda e: e.tensor_scalar(out=tmp, in0=a_im, scalar1=p_im, scalar2=None, op0=ALU.mult), reads=tks_r, writes=tks_w)
    k.op("dve", lambda e: e.scalar_tensor_tensor(out=o_re, in0=a_re, scalar=p_re, in1=tmp, op0=ALU.mult, op1=ALU.subtract), reads=tks_r + tks_w, writes=tks_w)
    k.op(eng, lambda e: e.tensor_scalar(out=tmp, in0=a_im, scalar1=p_re, scalar2=None, op0=ALU.mult), reads=tks_r + tks_w, writes=tks_w)
    k.op("dve", lambda e: e.scalar_tensor_tensor(out=o_im, in0=a_re, scalar=p_im, in1=tmp, op0=ALU.mult, op1=ALU.add), reads=tks_r + tks_w, writes=tks_w)


def s5_phase(k, C, uT, par, Btre_d, Btim_d, Ctre_d, Ctim_d, dsk_d, outT, out_row0, T, NG, chunks):
    NST = NG // 2
    NCC = NG * 16 // 128
    s0 = k.dsem()
    ub = k.sb("s5ub", [128, NCC, T], BF16)
    uft = [k.sb("s5uft%d" % i, [128, 1088], F32) for i in range(2)]
    ufs = [k.dsem() for _ in range(2)]
    iu = 0
    for c_ in range(NCC):
        for a_ in range(0, T, 1088):
            n_ = min(1088, T - a_)
            tt = uft[iu % 2]
            k.dma("sp", ufs[iu % 2], tt[:, :n_], uT[c_ * 128:(c_ + 1) * 128, a_:a_ + n_], dst=tt, src=uT)
            if iu % 2 == 0:
                k.op("act", lambda e: e.copy(out=ub[:, c_, a_:a_ + n_], in_=tt[:, :n_]), reads=[tt], writes=[ub])
            else:
                k.op("dve", lambda e: e.tensor_copy(out=ub[:, c_, a_:a_ + n_], in_=tt[:, :n_]), reads=[tt], writes=[ub])
            iu += 1
    P = k.sb("s5P", [128, 3, 2 * NST], F32)
    k.dma("act", s0, P[:, :, :], par[:, :, :], dst=P)
    Bre32 = k.sb("s5Bre32", [128, NST, 128], F32); Bim32 = k.sb("s5Bim32", [128, NST, 128], F32)
    Cre32 = k.sb("s5Cre32", [128, NST, 64], F32); Cim32 = k.sb("s5Cim32", [128, NST, 64], F32)
    dsk = k.sb("s5dsk", [128, NCC], F32)
    k.dma("sp", s0, Bre32[:, :, :], Btre_d[:, :, :], dst=Bre32); k.dma("sp", s0, Bim32[:, :, :], Btim_d[:, :, :], dst=Bim32)
    k.dma("act", s0, Cre32[:, :, :], Ctre_d[:, :, :], dst=Cre32); k.dma("act", s0, Cim32[:, :, :], Ctim_d[:, :, :], dst=Cim32)
    k.dma("sp", s0, dsk[:, :], dsk_d[:, :], dst=dsk)
    for t in (P, Bre32, Bim32, Cre32, Cim32, dsk):
        t.w = list(dsk.w)
    Bre = k.sb("s5Bre", [128, NST, 128], BF16); Bim = k.sb("s5Bim", [128, NST, 128], BF16)
    Cre = k.sb("s5Cre", [128, NST, 64], BF16); Cimn = k.sb("s5Cimn", [128, NST, 64], BF16)
    k.op("dve", lambda e: e.tensor_copy(out=Bre[:, :, :], in_=Bre32[:, :, :]), reads=[Bre32], writes=[Bre])
    k.op("dve", lambda e: e.tensor_copy(out=Bim[:, :, :], in_=Bim32[:, :, :]), reads=[Bim32], writes=[Bim])
    k.op("dve", lambda e: e.tensor_copy(out=Cre[:, :, :], in_=Cre32[:, :, :]), reads=[Cre32], writes=[Cre])
    k.op("dve", lambda e: e.tensor_scalar(out=Cimn[:, :, :], in0=Cim32[:, :, :], scalar1=-1.0, scalar2=None, op0=ALU.mult), reads=[Cim32], writes=[Cimn])
    W = 2 * NST
    def tl(nm):
        return k.sb("s5" + nm, [128, W], F32)
    dt, zr, zi, rmag, sn, cs, tmp, tmp2, kf, abr, abi, fre, fim, den = (tl(n) for n in
        ("dt", "zr", "zi", "rmag", "sn", "cs", "tmp", "tmp2", "kf", "abr", "abi", "fre", "fim", "den"))
    ki = k.sb("s5ki", [128, W], mybir.dt.int32)
    are, aim, ldt = P[:, 0, :], P[:, 1, :], P[:, 2, :]
    k.op("act", lambda e: e.activation(out=dt[:, :], in_=ldt, func=AF.Exp), reads=[P], writes=[dt])
    k.op("dve", lambda e: e.tensor_tensor(out=zr[:, :], in0=are, in1=dt[:, :], op=ALU.mult), reads=[P, dt], writes=[zr])
    k.op("dve", lambda e: e.tensor_tensor(out=zi[:, :], in0=aim, in1=dt[:, :], op=ALU.mult), reads=[P, dt], writes=[zi])
    k.op("act", lambda e: e.activation(out=rmag[:, :], in_=zr[:, :], func=AF.Exp), reads=[zr], writes=[rmag])

    def sin_of(dst, shift):
        k.op("dve", lambda e: e.tensor_scalar(out=tmp[:, :], in0=zi[:, :], scalar1=shift, scalar2=None, op0=ALU.add), reads=[zi], writes=[tmp])
        k.op("dve", lambda e: e.tensor_scalar(out=kf[:, :], in0=tmp[:, :], scalar1=1.0 / TWO_PI, scalar2=None, op0=ALU.mult), reads=[tmp], writes=[kf])
        k.op("dve", lambda e: e.tensor_copy(out=ki[:, :], in_=kf[:, :]), reads=[kf], writes=[ki])
        k.op("dve", lambda e: e.tensor_copy(out=kf[:, :], in_=ki[:, :]), reads=[ki], writes=[kf])
        k.op("dve", lambda e: e.scalar_tensor_tensor(out=tmp[:, :], in0=kf[:, :], scalar=-TWO_PI, in1=tmp[:, :], op0=ALU.mult, op1=ALU.add), reads=[kf, tmp], writes=[tmp])
        k.op("dve", lambda e: e.tensor_scalar(out=tmp2[:, :], in0=tmp[:, :], scalar1=3.141592653589793, scalar2=None, op0=ALU.is_gt), reads=[tmp], writes=[tmp2])
        k.op("dve", lambda e: e.scalar_tensor_tensor(out=tmp[:, :], in0=tmp2[:, :], scalar=-TWO_PI, in1=tmp[:, :], op0=ALU.mult, op1=ALU.add), reads=[tmp2, tmp], writes=[tmp])
        k.op("dve", lambda e: e.tensor_scalar(out=tmp2[:, :], in0=tmp[:, :], scalar1=-3.141592653589793, scalar2=None, op0=ALU.is_lt), reads=[tmp], writes=[tmp2])
        k.op("dve", lambda e: e.scalar_tensor_tensor(out=tmp[:, :], in0=tmp2[:, :], scalar=TWO_PI, in1=tmp[:, :], op0=ALU.mult, op1=ALU.add), reads=[tmp2, tmp], writes=[tmp])
        k.op("act", lambda e: e.activation(out=dst[:, :], in_=tmp[:, :], func=AF.Sin), reads=[tmp], writes=[dst])
    sin_of(sn, 0.0)
    sin_of(cs, 1.5707963267948966)
    k.op("dve", lambda e: e.tensor_tensor(out=abr[:, :], in0=rmag[:, :], in1=cs[:, :], op=ALU.mult), reads=[rmag, cs], writes=[abr])
    k.op("dve", lambda e: e.tensor_tensor(out=abi[:, :], in0=rmag[:, :], in1=sn[:, :], op=ALU.mult), reads=[rmag, sn], writes=[abi])
    k.op("dve", lambda e: e.tensor_scalar(out=abr[:, :], in0=abr[:, :], scalar1=-1.0, scalar2=None, op0=ALU.add), reads=[abr], writes=[abr])
    k.op("dve", lambda e: e.tensor_tensor(out=den[:, :], in0=are, in1=are, op=ALU.mult), reads=[P], writes=[den])
    k.op("dve", lambda e: e.tensor_tensor(out=tmp[:, :], in0=aim, in1=aim, op=ALU.mult), reads=[P], writes=[tmp])
    k.op("dve", lambda e: e.tensor_tensor(out=den[:, :], in0=den[:, :], in1=tmp[:, :], op=ALU.add), reads=[den, tmp], writes=[den])
    k.op("dve", lambda e: e.reciprocal(out=den[:, :], in_=den[:, :]), reads=[den], writes=[den])
    k.op("dve", lambda e: e.tensor_tensor(out=fre[:, :], in0=abr[:, :], in1=are, op=ALU.mult), reads=[abr, P], writes=[fre])
    k.op("dve", lambda e: e.tensor_tensor(out=tmp[:, :], in0=abi[:, :], in1=aim, op=ALU.mult), reads=[abi, P], writes=[tmp])
    k.op("dve", lambda e: e.tensor_tensor(out=fre[:, :], in0=fre[:, :], in1=tmp[:, :], op=ALU.add), reads=[fre, tmp], writes=[fre])
    k.op("dve", lambda e: e.tensor_tensor(out=fre[:, :], in0=fre[:, :], in1=den[:, :], op=ALU.mult), reads=[fre, den], writes=[fre])
    k.op("dve", lambda e: e.tensor_tensor(out=fim[:, :], in0=abi[:, :], in1=are, op=ALU.mult), reads=[abi, P], writes=[fim])
    k.op("dve", lambda e: e.tensor_tensor(out=tmp[:, :], in0=abr[:, :], in1=aim, op=ALU.mult), reads=[abr, P], writes=[tmp])
    k.op("dve", lambda e: e.tensor_tensor(out=fim[:, :], in0=fim[:, :], in1=tmp[:, :], op=ALU.subtract), reads=[fim, tmp], writes=[fim])
    k.op("dve", lambda e: e.tensor_tensor(out=fim[:, :], in0=fim[:, :], in1=den[:, :], op=ALU.mult), reads=[fim, den], writes=[fim])
    nsn = tl("nsn")
    k.op("dve", lambda e: e.tensor_scalar(out=nsn[:, :], in0=sn[:, :], scalar1=-1.0, scalar2=None, op0=ALU.mult), reads=[sn], writes=[nsn])
    TC = 512
    Rre = k.sb("s5Rre", [128, TC], F32); Rim = k.sb("s5Rim", [128, TC], F32)
    Ere = k.sb("s5Ere", [128, TC], F32); Eim = k.sb("s5Eim", [128, TC], F32)
    sc1 = k.sb("s5sc1", [128, TC], F32)
    Sre = k.sb("s5Sre", [128, T], F32); Sim = k.sb("s5Sim", [128, T], F32)
    Sreb = [k.sb("s5Sreb%d" % i, [128, T], BF16) for i in range(2)]; Simb = [k.sb("s5Simb%d" % i, [128, T], BF16) for i in range(2)]
    wre = k.sb("s5wre", [128, TC], F32); wim = k.sb("s5wim", [128, TC], F32)
    zre = k.sb("s5zre", [128, TC], F32); zim = k.sb("s5zim", [128, TC], F32)
    t1 = k.sb("s5t1", [128, TC], F32); t2 = k.sb("s5t2", [128, TC], F32)
    t3 = k.sb("s5t3", [128, TC], F32); t4 = k.sb("s5t4", [128, TC], F32)
    ire = k.sb("s5ire", [128, 1], F32); iim = k.sb("s5iim", [128, 1], F32)
    pbr = [k.ps("s5pbr%d" % i, [128, 512]) for i in range(2)]
    pbi = [k.ps("s5pbi%d" % i, [128, 512]) for i in range(2)]
    py = [k.ps("s5py%d" % i, [128, 512]) for i in range(2)]
    yst = Stage(k, 2, "s5yo", [128, 512], BF16)
    ust5 = Stage(k, 2, "s5uu", [128, 512], F32)
    ya = k.sb("s5ya", [128, 512], F32); yb = k.sb("s5yb", [128, 512], F32); yc = k.sb("s5yc", [128, 512], F32)
    ic = 0
    for st in range(NST):
        cc, jh, jl = st // 4, (st // 2) % 2, st % 2
        pr = slice(64 * jh, 64 * jh + 64)
        for d in range(2):
            col = d * NST + st
            k.op("dve", lambda e: e.tensor_copy(out=Rre[:, 0:1], in_=cs[:, col:col + 1]), reads=[cs], writes=[Rre])
            k.op("dve", lambda e: e.tensor_copy(out=Rim[:, 0:1], in_=nsn[:, col:col + 1]), reads=[nsn], writes=[Rim])
            m = 1
            while m < TC:
                _cmul_sc(k, "pool", Rre[:, m:2 * m], Rim[:, m:2 * m], Rre[:, 0:m], Rim[:, 0:m], Rre[:, m - 1:m], Rim[:, m - 1:m], sc1[:, 0:m], [Rre, Rim], [Rre, Rim, sc1])
                m *= 2
            _cmul_sc(k, "pool", Ere[:, :], Eim[:, :], Rre[:, :], Rim[:, :], fre[:, col:col + 1], fim[:, col:col + 1], sc1[:, :], [Rre, Rim, fre, fim], [Ere, Eim, sc1])
            k.op("dve", lambda e: e.memset(ire[:, :], 0.0), writes=[ire])
            k.op("dve", lambda e: e.memset(iim[:, :], 0.0), writes=[iim])
            for (t0, n) in chunks[d]:
                pb_r, pb_i = pbr[ic % 2], pbi[ic % 2]; ic += 1
                k.op("pe", lambda e: e.matmul(out=pb_r[:, :n], lhsT=Bre[pr, st, :], rhs=ub[pr, cc, t0:t0 + n], start=True, stop=True), reads=[Bre, ub], writes=[pb_r])
                k.op("pe", lambda e: e.matmul(out=pb_i[:, :n], lhsT=Bim[pr, st, :], rhs=ub[pr, cc, t0:t0 + n], start=True, stop=True), reads=[Bim, ub], writes=[pb_i])
                if d == 0:
                    br, bi = pb_r[:, 0:n], pb_i[:, 0:n]
                else:
                    br, bi = pb_r[:, n - 1::-1] if n > 1 else pb_r[:, 0:1], pb_i[:, n - 1::-1]
                k.op("dve", lambda e: e.tensor_tensor(out=t1[:, :n], in0=Ere[:, :n], in1=br, op=ALU.mult), reads=[Ere, pb_r], writes=[t1])
                k.op("dve", lambda e: e.tensor_tensor(out=t2[:, :n], in0=Eim[:, :n], in1=bi, op=ALU.mult), reads=[Eim, pb_i], writes=[t2])
                k.op("dve", lambda e: e.tensor_tensor(out=t3[:, :n], in0=Ere[:, :n], in1=bi, op=ALU.mult), reads=[Ere, pb_i], writes=[t3])
                k.op("dve", lambda e: e.tensor_tensor(out=t4[:, :n], in0=Eim[:, :n], in1=br, op=ALU.mult), reads=[Eim, pb_r], writes=[t4])
                k.op("pool", lambda e: e.tensor_tensor(out=wre[:, :n], in0=t1[:, :n], in1=t2[:, :n], op=ALU.subtract), reads=[t1, t2], writes=[wre])
                k.op("pool", lambda e: e.tensor_tensor(out=wim[:, :n], in0=t3[:, :n], in1=t4[:, :n], op=ALU.add), reads=[t3, t4], writes=[wim])
                rb = rmag[:, col:col + 1].broadcast_to([128, n])
                k.op("dve", lambda e: e.tensor_tensor_scan(out=zre[:, :n], data0=rb, data1=wre[:, :n], initial=ire[:, 0:1], op0=ALU.mult, op1=ALU.add), reads=[rmag, wre, ire], writes=[zre])
                k.op("dve", lambda e: e.tensor_tensor_scan(out=zim[:, :n], data0=rb, data1=wim[:, :n], initial=iim[:, 0:1], op0=ALU.mult, op1=ALU.add), reads=[rmag, wim, iim], writes=[zim])
                k.op("pool", lambda e: e.tensor_tensor(out=t1[:, :n], in0=Rre[:, :n], in1=zre[:, :n], op=ALU.mult), reads=[Rre, zre], writes=[t1])
                k.op("pool", lambda e: e.tensor_tensor(out=t2[:, :n], in0=Rim[:, :n], in1=zim[:, :n], op=ALU.mult), reads=[Rim, zim], writes=[t2])
                k.op("pool", lambda e: e.tensor_tensor(out=t3[:, :n], in0=Rre[:, :n], in1=zim[:, :n], op=ALU.mult), reads=[Rre, zim], writes=[t3])
                k.op("pool", lambda e: e.tensor_tensor(out=t4[:, :n], in0=Rim[:, :n], in1=zre[:, :n], op=ALU.mult), reads=[Rim, zre], writes=[t4])
                k.op("pool", lambda e: e.tensor_tensor(out=t1[:, :n], in0=t1[:, :n], in1=t2[:, :n], op=ALU.add), reads=[t1, t2], writes=[t1])
                k.op("pool", lambda e: e.tensor_tensor(out=t3[:, :n], in0=t3[:, :n], in1=t4[:, :n], op=ALU.subtract), reads=[t3, t4], writes=[t3])
                k.op("dve", lambda e: e.tensor_copy(out=ire[:, :], in_=t1[:, n - 1:n]), reads=[t1], writes=[ire])
                k.op("dve", lambda e: e.tensor_copy(out=iim[:, :], in_=t3[:, n - 1:n]), reads=[t3], writes=[iim])
                if d == 0:
                    k.op("act", lambda e: e.copy(out=Sre[:, t0:t0 + n], in_=t1[:, :n]), reads=[t1], writes=[Sre])
                    k.op("act", lambda e: e.copy(out=Sim[:, t0:t0 + n], in_=t3[:, :n]), reads=[t3], writes=[Sim])
                else:
                    k.op("pool", lambda e: e.tensor_tensor(out=Sre[:, t0:t0 + n], in0=Sre[:, t0:t0 + n], in1=t1[:, n - 1::-1], op=ALU.add), reads=[Sre, t1], writes=[Sre])
                    k.op("pool", lambda e: e.tensor_tensor(out=Sim[:, t0:t0 + n], in0=Sim[:, t0:t0 + n], in1=t3[:, n - 1::-1], op=ALU.add), reads=[Sim, t3], writes=[Sim])
        k.op("act", lambda e: e.copy(out=Sreb[jl][:, :], in_=Sre[:, :]), reads=[Sre], writes=[Sreb[jl]])
        k.op("act", lambda e: e.copy(out=Simb[jl][:, :], in_=Sim[:, :]), reads=[Sim], writes=[Simb[jl]])
        if jl == 0:
            continue
        for a in range(0, T, 512):
            n = min(512, T - a)
            p = py[(a // 512) % 2]
            for q in range(2):
                stq = st - 1 + q
                k.op("pe", lambda e: e.matmul(out=p[pr, :n], lhsT=Cre[:, stq, :], rhs=Sreb[q][:, a:a + n], start=(q == 0), stop=False), reads=[Cre, Sreb[q]], writes=[p], sig=False)
                k.op("pe", lambda e: e.matmul(out=p[pr, :n], lhsT=Cimn[:, stq, :], rhs=Simb[q][:, a:a + n], start=False, stop=(q == 1)), reads=[Cimn, Simb[q]], writes=[p], sig=(q == 1))
            uu, usem_ = ust5.get()
            k.dma("act", usem_, uu[pr, :n], uT[cc * 128 + 64 * jh:cc * 128 + 64 * jh + 64, a:a + n], dst=uu, src=uT)
            k.op("dve", lambda e: e.scalar_tensor_tensor(out=ya[pr, :n], in0=uu[pr, :n], scalar=dsk[pr, cc:cc + 1], in1=p[pr, :n], op0=ALU.mult, op1=ALU.add),
                 reads=[uu, dsk, p], writes=[ya])
            o, osem = yst.get()
            gelu_tanh(k, ya, yb, yc, o, pr, n)
            r0 = out_row0 + 128 * cc + 64 * jh
            if isinstance(outT, list):
                for a2 in range(a, a + n, 128):
                    op_ = outT[a2 // 128]
                    k.dma("sp", osem, op_[r0:r0 + 64, :], o[pr, a2 - a:a2 - a + 128], src=o, dst=op_)
            else:
                k.dma("sp", osem, outT[r0:r0 + 64, a:a + n], o[pr, :n], src=o, dst=outT)


def gelu_tanh(k, y, b1, b2, o, pr, n, eng2="pool"):
    k.op("dve", lambda e: e.tensor_tensor(out=b1[pr, :n], in0=y[pr, :n], in1=y[pr, :n], op=ALU.mult), reads=[y], writes=[b1])
    k.op("dve", lambda e: e.tensor_scalar(out=b1[pr, :n], in0=b1[pr, :n], scalar1=0.044715, scalar2=1.0, op0=ALU.mult, op1=ALU.add), reads=[b1], writes=[b1])
    k.op(eng2, lambda e: e.tensor_tensor(out=b1[pr, :n], in0=b1[pr, :n], in1=y[pr, :n], op=ALU.mult), reads=[b1, y], writes=[b1])
    k.op("act", lambda e: e.activation(out=b2[pr, :n], in_=b1[pr, :n], func=AF.Sigmoid, scale=1.5957691216057308), reads=[b1], writes=[b2])
    k.op(eng2, lambda e: e.tensor_tensor(out=o[pr, :n], in0=b2[pr, :n], in1=y[pr, :n], op=ALU.mult), reads=[b2, y], writes=[o])


def s5_host_layout(a_re, a_im, log_dt, b_re, b_im, c_re, c_im, d_skip):
    NG = a_re.shape[1]; NST = NG // 2; NCC = NG * 16 // 128
    par = np.zeros((128, 3, 2 * NST), np.float32)
    Bt = np.zeros((2, 128, NST, 128), np.float32)
    Ct = np.zeros((2, 128, NST, 64), np.float32)
    for st in range(NST):
        jh, jl = (st // 2) % 2, st % 2
        for gl in range(2):
            g = 2 * st + gl
            for d in range(2):
                par[64 * gl:64 * gl + 64, 0, d * NST + st] = a_re[d, g]
                par[64 * gl:64 * gl + 64, 1, d * NST + st] = a_im[d, g]
                par[64 * gl:64 * gl + 64, 2, d * NST + st] = log_dt[d, g]
            for ri, (bb, cmat) in enumerate(((b_re, c_re), (b_im, c_im))):
                r0 = 64 * jh + 32 * jl + 16 * gl
                Bt[ri, r0:r0 + 16, st, 64 * gl:64 * gl + 64] = bb[g].T
                Ct[ri, 64 * gl:64 * gl + 64, st, 32 * jl + 16 * gl:32 * jl + 16 * gl + 16] = cmat[g].T
    dsk = np.ascontiguousarray(d_skip.reshape(NCC, 128).T).astype(np.float32)
    return par, Bt[0], Bt[1], Ct[0], Ct[1], dsk


def na_phase(k, C, QT, KT, V, Tg, cmask_d, flags_d, attnT, NHEADS, scale):
    NKT = 2816
    s0 = k.dsem()
    cm = k.sb("nacm", [64, 64], F32); fl = k.sb("nafl", [128, 4], F32)
    k.dma("sp", s0, cm[:, :], cmask_d[:, :], dst=cm)
    k.dma("sp", s0, fl[:, :], flags_d[:, :], dst=fl)
    cm.w = list(fl.w)
    nb = 2
    qh = [k.sb("naq%d" % i, [128, 2048], BF16) for i in range(nb)]
    kh = [k.sb("nak%d" % i, [128, NKT], BF16) for i in range(nb)]
    ve = [k.sb("nave%d" % i, [128, 22, 128], BF16) for i in range(nb)]
    vo = [k.sb("navo%d" % i, [128, 21, 128], BF16) for i in range(nb)]
    th = [k.sb("nath%d" % i, [64, 15, 64], F32) for i in range(nb)]
    ao = [k.sb("naao%d" % i, [128, 2048], BF16) for i in range(nb)]
    lsem = [k.dsem() for _ in range(nb)]
    osem = [k.dsem() for _ in range(nb)]
    ps_s = [k.ps("naps%d" % i, [128, 1024]) for i in range(2)]
    ps_t = [k.ps("napt%d" % i, [128, 1024], BF16) for i in range(2)]
    ps_o = [k.ps("napo%d" % i, [128, 512]) for i in range(2)]
    ssb = [k.sb("nas%d" % i, [64, 768], F32) for i in range(2)]
    pnb = [k.sb("napn%d" % i, [64, 768], BF16) for i in range(2)]
    pT = [k.sb("napT%d" % i, [128, 6, 64], BF16) for i in range(2)]
    mx = [k.sb("namx%d" % i, [64, 1], F32) for i in range(2)]
    sm = [k.sb("nasm%d" % i, [64, 1], F32) for i in range(2)]
    tmpo = k.sb("natmp", [128, 64], F32)
    items = []
    for r in range(32):
        if r < 4:
            items.append((r, 0, 7 - r, ("first", 0)))
            items.append((r, r - 4, 3, ("second", 1)))
        elif r >= 29:
            items.append((r, 24, 24 - r + 7, ("first", 2)))
            items.append((r, r - 4, 3, ("second", 3)))
        else:
            items.append((r, r - 4, 3, None))
    it = 0
    for h in range(NHEADS):
        b = h % nb
        ls = lsem[b]
        k.dma("sp", ls, qh[b][:, :], QT[h * 128:(h + 1) * 128, :], dst=qh[b], src=QT)
        k.dma("sp", ls, kh[b][:, :], KT[h * 128:(h + 1) * 128, :], dst=kh[b], src=KT)
        k.dma("act", ls, ve[b][:, :, :], V[0:2816, h * 128:(h + 1) * 128].rearrange("(t p) d -> p t d", p=128), dst=ve[b], src=V)
        k.dma("act", ls, vo[b][:, :, :], V[64:64 + 2688, h * 128:(h + 1) * 128].rearrange("(t p) d -> p t d", p=128), dst=vo[b], src=V)
        k.dma("sp", ls, th[b][:, :, :], Tg[h], dst=th[b])
        for t in (qh[b], kh[b], ve[b], vo[b]):
            t.w = list(th[b].w)
        k.op("pool", lambda e: e.tensor_tensor(out=th[b][:, :, :], in0=th[b][:, :, :], in1=cm[:, :].unsqueeze(1).broadcast_to([64, 15, 64]), op=ALU.add),
             reads=[th[b], cm], writes=[th[b]])
        GI = 2
        for g0 in range(0, len(items), GI):
            grp = items[g0:g0 + GI]
            js = []
            for _ in grp:
                js.append(it % GI); it += 1
            for (r, b0, ri0, blend), j in zip(grp, js):
                kc0 = (b0 + 4) * 64
                qv = qh[b][:, r * 64:(r + 1) * 64]
                k.op("pe", lambda e: e.matmul(out=ps_s[j][:64, :512], lhsT=qv, rhs=kh[b][:, kc0:kc0 + 512], start=True, stop=True), reads=[qh[b], kh[b]], writes=[ps_s[j]])
                k.op("pe", lambda e: e.matmul(out=ps_s[j][:64, 512:768], lhsT=qv, rhs=kh[b][:, 2560:2816], start=True, stop=True), reads=[qh[b], kh[b]], writes=[ps_s[j]])
            for (r, b0, ri0, blend), j in zip(grp, js):
                s_ = ssb[j]
                k.op("dve", lambda e: e.scalar_tensor_tensor(out=s_[:, 0:512], in0=ps_s[j][:64, :512], scalar=scale, in1=th[b][:, ri0:ri0 + 8, :].rearrange("p a b -> p (a b)"),
                                                            op0=ALU.mult, op1=ALU.add), reads=[ps_s[j], th[b]], writes=[s_])
                k.op("act", lambda e: e.activation(out=s_[:, 512:768], in_=ps_s[j][:64, 512:768], func=AF.Copy, scale=scale), reads=[ps_s[j], s_], writes=[s_])
            for (r, b0, ri0, blend), j in zip(grp, js):
                s_ = ssb[j]
                k.op("dve", lambda e: e.tensor_reduce(out=mx[j][:, :], in_=s_[:, :], axis=AX.X, op=ALU.max, negate=True), reads=[s_], writes=[mx[j]])
            for (r, b0, ri0, blend), j in zip(grp, js):
                s_ = ssb[j]
                k.op("act", lambda e: e.activation(out=s_[:, :], in_=s_[:, :], func=AF.Exp, bias=mx[j][:, 0:1], scale=1.0, accum_out=sm[j][:, 0:1]), reads=[s_, mx[j]], writes=[s_, sm[j]])
            for (r, b0, ri0, blend), j in zip(grp, js):
                s_ = ssb[j]
                k.op("dve", lambda e: e.reciprocal(out=sm[j][:, :], in_=sm[j][:, :]), reads=[sm[j]], writes=[sm[j]])
                k.op("dve", lambda e: e.tensor_scalar(out=pnb[j][:, :], in0=s_[:, :], scalar1=sm[j][:, 0:1], scalar2=None, op0=ALU.mult), reads=[s_, sm[j]], writes=[pnb[j]])
            for (r, b0, ri0, blend), j in zip(grp, js):
                for kk in range(6):
                    k.op("pe", lambda e: e.transpose(out=ps_t[j][:, kk * 64:(kk + 1) * 64], in_=pnb[j][:, kk * 128:(kk + 1) * 128], identity=C["ident_bf"][:64, :64]),
                         reads=[pnb[j], C["ident_bf"]], writes=[ps_t[j]], sig=(kk == 5))
            for (r, b0, ri0, blend), j in zip(grp, js):
                k.op("act", lambda e: e.copy(out=pT[j][:, :, :], in_=ps_t[j][:, 0:384].rearrange("p (a b) -> p a b", a=6)), reads=[ps_t[j]], writes=[pT[j]])
            for (r, b0, ri0, blend), j in zip(grp, js):
                par = (b0 + 4) % 2
                jo = j
                for kk in range(6):
                    if kk < 4:
                        vt = (ve[b][:, (b0 + 4) // 2 + kk, :], ve[b]) if par == 0 else (vo[b][:, (b0 + 3) // 2 + kk, :], vo[b])
                    else:
                        vt = (ve[b][:, 20 + (kk - 4), :], ve[b])
                    k.op("pe", lambda e: e.matmul(out=ps_o[jo][:, :64], lhsT=vt[0], rhs=pT[j][:, kk, :], start=(kk == 0), stop=(kk == 5)), reads=[vt[1], pT[j]], writes=[ps_o[jo]], sig=(kk == 5))
                dst = ao[b][:, r * 64:(r + 1) * 64]
                if blend is None:
                    k.op("act", lambda e: e.copy(out=dst, in_=ps_o[jo][:, :64]), reads=[ps_o[jo]], writes=[ao[b]])
                elif blend[0] == "first":
                    k.op("act", lambda e: e.activation(out=tmpo[:, :], in_=ps_o[jo][:, :64], func=AF.Copy, scale=fl[:, blend[1]:blend[1] + 1]), reads=[ps_o[jo], fl], writes=[tmpo])
                else:
                    k.op("dve", lambda e: e.scalar_tensor_tensor(out=dst, in0=ps_o[jo][:, :64], scalar=fl[:, blend[1]:blend[1] + 1], in1=tmpo[:, :], op0=ALU.mult, op1=ALU.add),
                         reads=[ps_o[jo], fl, tmpo], writes=[ao[b]])
        k.dma("sp", osem[b], attnT[h * 128:(h + 1) * 128, :], ao[b][:, :], src=ao[b], dst=attnT)


def na_host_tables(rpb):
    col = np.arange(64)
    ci = np.clip(col[None, :] - col[:, None], -15, 15) + 15
    Tg = rpb[:, :, ci]
    Tg = np.ascontiguousarray(Tg.transpose(0, 2, 1, 3)).astype(np.float32)
    cs = np.clip(col - 8, 0, 64 - 16)
    valid = (col[None, :] >= cs[:, None]) & (col[None, :] < cs[:, None] + 16)
    cmask = np.where(valid, 0.0, -30000.0).astype(np.float32)
    return Tg, cmask


D = 4096
KC = 32
TCX = 128
TLAT = 2048
TOWN = TCX + TLAT
TP = 2 * TOWN
DFF = 11008
NJC = DFF // 128
PAIRS = [[0, 1], [2, 3], [4, 5], [6, 7]]
PARITY = [[0, 2, 4, 6], [1, 3, 5, 7]]
ALL8 = [list(range(8))]
WIN_N = 5632
WIN_U0 = 5120


def mod_phase(k, C, IN, MV):
    modsh = k.dram("modsh", [10, 3072], F32)
    modall = k.dram("modall", [80, 3072], F32)
    modpair = k.dram("modpair", [20, 3072], F32)
    with k.phase():
        s0 = k.dsem()
        cond = k.sb("cond", [128, 32, 5], F32); sc = k.sb("scond", [128, 32, 5], F32)
        mb = k.sb("mb", [5, 6144], F32)
        k.dma("sp", s0, cond[:, :, :], IN["condT"][:, :, :], dst=cond)
        k.dma("sp", s0, mb[:, :], IN["modb"][0:1, :].partition_broadcast(5), dst=mb)
        cond.w = list(mb.w)
        k.op("act", lambda e: e.activation(out=sc[:, :, :], in_=cond[:, :, :], func=AF.Silu), reads=[cond], writes=[sc])
        wb = [k.sb("modw%d" % i, [128, 32, 512], F32) for i in range(2)]
        ws = [k.dsem() for _ in range(2)]
        stg = [k.sb("modst%d" % i, [5, 3072], F32) for i in range(2)]
        ss = [k.dsem() for _ in range(2)]
        pp = PsPool(k, 2, "modps")
        it = 0
        for l in range(2):
            for blk in range(6):
                w = wb[it % 2]
                k.dma("sp" if it % 2 == 0 else "act", ws[it % 2], w[:, :, :], IN["modw"][l, :, blk * 512:(blk + 1) * 512].rearrange("(kc p) n -> p kc n", p=128), dst=w)
                it += 1
                ps = pp.get()
                for kc in range(32):
                    k.op("pe", lambda e: e.matmul(out=ps[:5, :512], lhsT=sc[:, kc, :], rhs=w[:, kc, :], start=(kc == 0), stop=(kc == 31)), reads=[sc, w], writes=[ps], sig=(kc == 31))
                k.op("dve", lambda e: e.tensor_tensor(out=stg[l][:5, blk * 512:(blk + 1) * 512], in0=ps[:5, :512], in1=mb[:5, l * 3072 + blk * 512:l * 3072 + (blk + 1) * 512], op=ALU.add),
                     reads=[ps, mb], writes=[stg[l]])
            k.dma("sp", ss[l], modsh[l * 5:(l + 1) * 5, :], stg[l][:5, :], src=stg[l], dst=modsh)
    k.allgather(k.dsem(), modpair, modsh, PAIRS)
    k.allgather(k.dsem(), modall, modpair, PARITY)
    with k.phase():
        s0 = k.dsem()
        oh = k.sb("oh", [128, 5], F32)
        k.dma("sp", s0, oh[:, :], IN["onehot"][:, :], dst=oh)
        gm = k.sb("gmix", [128, 2, 32], F32); gf = k.sb("gffn", [128, 2, 32], F32)
        k.dma("sp", s0, gm[:, :, :], IN["gmix"][:, :, :], dst=gm)
        k.dma("sp", s0, gf[:, :, :], IN["gffn"][:, :, :], dst=gf)
        oh.w = list(gf.w); gm.w = list(gf.w)
        G = [k.sb("modG%d" % i, [32, 5, 128], F32) for i in range(2)]
        gs = [k.dsem() for _ in range(2)]
        sel = [k.sb("modsel%d" % i, [32, 128], F32) for i in range(2)]
        pp = PsPool(k, 2, "modps2")
        it = 0
        for l in range(2):
            for which in range(6):
                g = G[it % 2]; it += 1
                for r in range(8):
                    k.dma("sp", gs[(it - 1) % 2], g[4 * r:4 * r + 4, :, :],
                          modall[r * 10 + l * 5:r * 10 + l * 5 + 5, which * 512:(which + 1) * 512].rearrange("r (q p) -> q r p", p=128), dst=g, src=modall)
                sl = sel[(it - 1) % 2]
                k.op("dve", lambda e: e.tensor_scalar(out=sl[:, :], in0=g[:, 0, :], scalar1=oh[:32, 0:1], scalar2=None, op0=ALU.mult), reads=[g, oh], writes=[sl])
                for r in range(1, 4):
                    k.op("dve", lambda e: e.scalar_tensor_tensor(out=sl[:, :], in0=g[:, r, :], scalar=oh[:32, r:r + 1], in1=sl[:, :], op0=ALU.mult, op1=ALU.add), reads=[g, oh, sl], writes=[sl])
                for si, src in enumerate((sl[:, :], g[:, 4, :])):
                    ps = pp.get()
                    k.op("pe", lambda e: e.transpose(out=ps[:, 0:32], in_=src, identity=C["ident"][:32, :32]), reads=[sl, g, C["ident"]], writes=[ps])
                    dst = MV[l, which, si]
                    k.op("act", lambda e: e.copy(out=dst[:, :], in_=ps[:, 0:32]), reads=[ps], writes=[dst])
        for l in range(2):
            for si in range(2):
                for (nm, which, gt) in (("gscm", 1, gm), ("gscf", 4, gf)):
                    d = MV[l, nm, si]
                    k.op("dve", lambda e: e.tensor_scalar(out=d[:, :], in0=MV[l, which, si][:, :], scalar1=1.0, scalar2=None, op0=ALU.add), reads=[MV[l, which, si]], writes=[d])
                    k.op("dve", lambda e: e.tensor_tensor(out=d[:, :], in0=d[:, :], in1=gt[:, l, :], op=ALU.mult), reads=[d, gt], writes=[d])


def blend_cols(k, dst, jobs, fl, ca, cb, R):
    nb = 3
    CW = 128
    ta = [k.sb("bla%d" % i, [128, CW], BF16) for i in range(nb)]; tb = [k.sb("blb%d" % i, [128, CW], BF16) for i in range(nb)]
    tf = [k.sb("blf%d" % i, [128, CW], F32) for i in range(nb)]; to = [k.sb("blo%d" % i, [128, CW], BF16) for i in range(nb)]
    si = [k.dsem() for _ in range(nb)]; so = [k.dsem() for _ in range(nb)]
    i = 0
    for (dcol0, A, B) in jobs:
        for r0 in range(0, R, 128):
            b = i % nb; i += 1
            k.dma("sp", si[b], ta[b][:, :], A[r0:r0 + 128, :], dst=ta[b], src=A)
            k.dma("act", si[b], tb[b][:, :], B[r0:r0 + 128, :], dst=tb[b], src=B)
            ta[b].w = list(tb[b].w)
            k.op("pool", lambda e: e.tensor_scalar(out=tf[b][:, :], in0=ta[b][:, :], scalar1=fl[:, ca:ca + 1], scalar2=None, op0=ALU.mult), reads=[ta[b], fl], writes=[tf[b]])
            k.op("dve", lambda e: e.scalar_tensor_tensor(out=to[b][:, :], in0=tb[b][:, :], scalar=fl[:, cb:cb + 1], in1=tf[b][:, :], op0=ALU.mult, op1=ALU.add),
                 reads=[tb[b], fl, tf[b]], writes=[to[b]])
            k.dma("sp", so[b], dst[r0:r0 + 128, dcol0:dcol0 + CW], to[b][:, :], src=to[b], dst=dst)


def scale_cols(k, dst, dcol0, src, scol0, fl, ca, R, ncols):
    nb = 2
    ta = [k.sb("sca%d" % i, [128, ncols], BF16) for i in range(nb)]; to = [k.sb("sco%d" % i, [128, ncols], BF16) for i in range(nb)]
    si = [k.dsem() for _ in range(nb)]; so = [k.dsem() for _ in range(nb)]
    i = 0
    for r0 in range(0, R, 128):
        b = i % nb; i += 1
        k.dma("sp", si[b], ta[b][:, :], src[r0:r0 + 128, scol0:scol0 + ncols], dst=ta[b], src=src)
        k.op("dve", lambda e: e.tensor_scalar(out=to[b][:, :], in0=ta[b][:, :], scalar1=fl[:, ca:ca + 1], scalar2=None, op0=ALU.mult), reads=[ta[b], fl], writes=[to[b]])
        k.dma("act", so[b], dst[r0:r0 + 128, dcol0:dcol0 + ncols], to[b][:, :], src=to[b], dst=dst)


def resid_epilogue(k, xsrc, xdst, gate, nstage=3, tcol0=0):
    xi = Stage(k, nstage, "rxi", [128, 512], F32)
    xo = Stage(k, nstage, "rxo", [128, 512], F32)

    def epi(ps, m, n, f0, t0, **kw):
        fc = f0 // 128
        t0 = t0 + tcol0
        si = 1 if t0 < TCX else 0
        a, asem = xi.get()
        k.dma("act", asem, a[:m, :n], xsrc[f0:f0 + m, t0:t0 + n], dst=a, src=xsrc)
        o, osem = xo.get()
        k.op("dve", lambda e: e.scalar_tensor_tensor(out=o[:m, :n], in0=ps[:m, :n], scalar=gate[si][:m, fc:fc + 1], in1=a[:m, :n], op0=ALU.mult, op1=ALU.add),
             reads=[ps, gate[si], a], writes=[o])
        k.dma("act", osem, xdst[f0:f0 + m, t0:t0 + n], o[:m, :n], src=o, dst=xdst)
    return epi


def rope_phase(k, C, q_tm, k_tm, ropeT, qT, kT, ktm_r, NH):
    W = NH * 256
    nb = 2
    xq = [k.sb("rq%d" % i, [128, NH, 2, 2, 64], F32) for i in range(nb)]
    xk = [k.sb("rk%d" % i, [128, NH, 2, 2, 64], F32) for i in range(nb)]
    tb = [k.sb("rt%d" % i, [128, 2, 2, 64], F32) for i in range(nb)]
    ls = [k.dsem() for _ in range(nb)]
    rq = [k.sb("rrq%d" % i, [128, NH, 2, 2, 64], F32) for i in range(nb)]
    rk = [k.sb("rrk%d" % i, [128, NH, 2, 2, 64], F32) for i in range(nb)]
    t1 = k.sb("rt1", [128, NH, 2, 64], F32); t2 = k.sb("rt2", [128, NH, 2, 64], F32)
    oq = [k.sb("roq%d" % i, [128, NH * 2, 128], F32) for i in range(nb)]
    ok = [k.sb("rok%d" % i, [128, NH * 2, 128], F32) for i in range(nb)]
    sq = [k.dsem() for _ in range(nb)]; sk = [k.dsem() for _ in range(nb)]; sr = [k.dsem() for _ in range(nb)]
    pp = PsPool(k, 4, "rps")
    flat = "p a b c d -> p (a b c d)"
    for ci in range(TP // 128):
        b = ci % nb
        t0 = ci * 128
        k.dma("sp", ls[b], xq[b][:].rearrange(flat), q_tm[t0:t0 + 128, :], dst=xq[b], src=q_tm)
        k.dma("act", ls[b], xk[b][:].rearrange(flat), k_tm[t0:t0 + 128, :], dst=xk[b], src=k_tm)
        k.dma("sp", ls[b], tb[b][:].rearrange("p a b c -> p (a b c)"), ropeT[t0:t0 + 128, :], dst=tb[b], src=ropeT)
        xq[b].w = list(tb[b].w); xk[b].w = list(tb[b].w)
        cosv = tb[b][:, 0, :, :].unsqueeze(1).broadcast_to([128, NH, 2, 64])
        sinv = tb[b][:, 1, :, :].unsqueeze(1).broadcast_to([128, NH, 2, 64])
        for (x, r) in ((xq[b], rq[b]), (xk[b], rk[b])):
            x1, x2 = x[:, :, :, 0, :], x[:, :, :, 1, :]
            k.op("dve", lambda e: e.tensor_tensor(out=t1[:], in0=x1, in1=cosv, op=ALU.mult), reads=[x, tb[b]], writes=[t1])
            k.op("pool", lambda e: e.tensor_tensor(out=t2[:], in0=x2, in1=sinv, op=ALU.mult), reads=[x, tb[b]], writes=[t2])
            k.op("dve", lambda e: e.tensor_tensor(out=r[:, :, :, 0, :], in0=t1[:], in1=t2[:], op=ALU.subtract), reads=[t1, t2], writes=[r])
            k.op("dve", lambda e: e.tensor_tensor(out=t1[:], in0=x1, in1=sinv, op=ALU.mult), reads=[x, tb[b], r], writes=[t1])
            k.op("pool", lambda e: e.tensor_tensor(out=t2[:], in0=x2, in1=cosv, op=ALU.mult), reads=[x, tb[b], r], writes=[t2])
            k.op("dve", lambda e: e.tensor_tensor(out=r[:, :, :, 1, :], in0=t1[:], in1=t2[:], op=ALU.add), reads=[t1, t2], writes=[r])
        k.dma("act", sr[b], ktm_r[t0:t0 + 128, :], rk[b][:].rearrange(flat), src=rk[b], dst=ktm_r)
        for (r, o, osem, dstT, scl) in ((rq[b], oq[b], sq[b], qT, 1.0 / 16.0), (rk[b], ok[b], sk[b], kT, 1.0)):
            rf = r[:].rearrange(flat)
            for half in range(NH * 2 // 3):
                ps = pp.get()
                for j in range(3):
                    c = half * 3 + j
                    k.op("pe", lambda e: e.transpose(out=ps[:, j * 128:(j + 1) * 128], in_=rf[:, c * 128:(c + 1) * 128], identity=C["ident"][:, :]),
                         reads=[r, C["ident"]], writes=[ps], sig=(j == 2))
                k.op("act", lambda e: e.activation(out=o[:, half * 3:half * 3 + 3, :], in_=ps[:, 0:384].rearrange("p (a b) -> p a b", a=3), func=AF.Copy, scale=scl),
                     reads=[ps], writes=[o])
            k.dma("sp", osem, dstT[:, t0:t0 + 128].rearrange("(c p) t -> p c t", p=128), o[:, :, :], src=o, dst=dstT)


def mlstm_readout(k, C, hdir, o_tm, hg_d, mixT, NH):
    W = NH * 512
    s0 = k.dsem()
    hg = k.sb("rohg", [128, W], F32)
    k.dma("sp", s0, hg[:, :], hg_d[0:1, :].partition_broadcast(128), dst=hg)
    nb = 2
    h0 = [k.sb("roh0%d" % i, [128, W], F32) for i in range(nb)]
    h1 = [k.sb("roh1%d" % i, [128, W], F32) for i in range(nb)]
    ot = [k.sb("roo%d" % i, [128, W], F32) for i in range(nb)]
    ls = [k.dsem() for _ in range(nb)]
    sqj = k.sb("rosq", [128, 512], F32)
    ss = k.sb("ross", [128, NH], F32)
    mo = [k.sb("romo%d" % i, [128, W // 128, 128], BF16) for i in range(nb)]
    ms = [k.dsem() for _ in range(nb)]
    pp = PsPool(k, 4, "rops")
    for ci in range(TP // 128):
        b = ci % nb
        t0 = ci * 128
        k.dma("sp", ls[b], h0[b][:, :], hdir[0][t0:t0 + 128, :], dst=h0[b], src=hdir[0])
        k.dma("act", ls[b], h1[b][:, :], hdir[1][t0:t0 + 128, :], dst=h1[b], src=hdir[1])
        k.dma("sp", ls[b], ot[b][:, :], o_tm[t0:t0 + 128, :], dst=ot[b], src=o_tm)
        h0[b].w = list(ot[b].w); h1[b].w = list(ot[b].w)
        k.op("dve", lambda e: e.tensor_tensor(out=h0[b][:, :], in0=h0[b][:, :], in1=h1[b][:, :], op=ALU.add), reads=[h0[b], h1[b]], writes=[h0[b]])
        k.op("act", lambda e: e.activation(out=ot[b][:, :], in_=ot[b][:, :], func=AF.Sigmoid), reads=[ot[b]], writes=[ot[b]])
        for h in range(NH):
            k.op("act", lambda e: e.activation(out=sqj[:, :], in_=h0[b][:, h * 512:(h + 1) * 512], func=AF.Square, accum_out=ss[:, h:h + 1]), reads=[h0[b]], writes=[sqj, ss])
        k.op("act", lambda e: e.activation(out=ss[:, :], in_=ss[:, :], func=AF.Sqrt, scale=1.0 / 512.0, bias=k.eps_t[:, 0:1]), reads=[ss, k.eps_t], writes=[ss])
        k.op("dve", lambda e: e.reciprocal(out=ss[:, :], in_=ss[:, :]), reads=[ss], writes=[ss])
        for h in range(NH):
            sl = slice(h * 512, (h + 1) * 512)
            k.op("dve", lambda e: e.scalar_tensor_tensor(out=h1[b][:, sl], in0=h0[b][:, sl], scalar=ss[:, h:h + 1], in1=hg[:, sl], op0=ALU.mult, op1=ALU.mult),
                 reads=[h0[b], ss, hg], writes=[h1[b]])
        k.op("pool", lambda e: e.tensor_tensor(out=h1[b][:, :], in0=h1[b][:, :], in1=ot[b][:, :], op=ALU.mult), reads=[h1[b], ot[b]], writes=[h1[b]])
        for grp in range(W // 512):
            ps = pp.get()
            for j in range(4):
                c = grp * 4 + j
                k.op("pe", lambda e: e.transpose(out=ps[:, j * 128:(j + 1) * 128], in_=h1[b][:, c * 128:(c + 1) * 128], identity=C["ident"][:, :]),
                     reads=[h1[b], C["ident"]], writes=[ps], sig=(j == 3))
            k.op("act", lambda e: e.copy(out=mo[b][:, grp * 4:grp * 4 + 4, :], in_=ps[:, :].rearrange("p (a b) -> p a b", a=4)), reads=[ps], writes=[mo[b]])
        mp = mixT[ci]
        k.dma("sp", ms[b], mp[0:W, :].rearrange("(c p) t -> p c t", p=128), mo[b][:, :, :], src=mo[b], dst=mp)


def ffn_up_phase(k, C, h2T, halo, Wup, cw, cb, uffn, segs):
    NBW = 256
    blocks = []
    for (t0, n, left, right) in segs:
        nblk = ceil_div(n, 510)
        sz = ceil_div(n, nblk)
        x = t0
        while x < t0 + n:
            c = min(sz, t0 + n - x)
            blocks.append((x, c, left if x == t0 else ("int",), right if x + c == t0 + n else ("int",)))
            x += c
    half_n = ceil_div(len(blocks), 2)
    supers = [blocks[:half_n], blocks[half_n:]] if len(blocks) > 3 else [blocks]
    maxb = max(len(s) for s in supers)
    A = [k.sb("fuA%d" % i, [128, KC, 512], BF16) for i in range(maxb)]
    As = [k.dsem() for _ in range(maxb)]
    wa = [k.sb("fuwa%d" % i, [128, KC, NBW], BF16) for i in range(2)]
    wg = [k.sb("fuwg%d" % i, [128, KC, NBW], BF16) for i in range(2)]
    wsm = [k.dsem() for _ in range(2)]
    pg = [k.ps("fupg%d" % i, [128, 512]) for i in range(3)]
    pa = [k.ps("fupa%d" % i, [128, 512]) for i in range(3)]
    c1 = [k.sb("fuc%d" % i, [128, 512], F32) for i in range(2)]
    y1 = [k.sb("fuy%d" % i, [128, 512], F32) for i in range(2)]
    y2 = [k.sb("fuz%d" % i, [128, 512], F32) for i in range(2)]
    ust = Stage(k, 3, "fuu", [128, 512], BF16)
    allr = slice(0, 128)
    it = 0
    nwb = DFF // NBW
    for sblocks in supers:
        for bi, (x, c, left, right) in enumerate(sblocks):
            a = A[bi]
            lo = x - 1 if left[0] == "int" else x
            hi = x + c + 1 if right[0] == "int" else x + c
            k.dma("act", As[bi], a[:, :, 1 - (x - lo):1 + c + (hi - x - c)], h2T[:, lo:hi].rearrange("(kc p) t -> p kc t", p=128), dst=a, src=h2T)
            if left[0] == "halo":
                k.op("dve", lambda e: e.tensor_copy(out=a[:, :, 0:1], in_=halo[:, :, left[1]:left[1] + 1]), reads=[halo], writes=[a])
            if right[0] == "halo":
                k.op("dve", lambda e: e.tensor_copy(out=a[:, :, c + 1:c + 2], in_=halo[:, :, right[1]:right[1] + 1]), reads=[halo], writes=[a])

        def issue_w(wi):
            j = wi % 2
            tka, apa = Wup(wi * NBW, NBW)
            tkg, apg = Wup(DFF + wi * NBW, NBW)
            k.dma("sp", wsm[j], wa[j][:, :, :], apa.rearrange("(kc p) n -> p kc n", p=128), dst=wa[j], src=tka)
            k.dma("sp", wsm[j], wg[j][:, :, :], apg.rearrange("(kc p) n -> p kc n", p=128), dst=wg[j], src=tkg)
            wa[j].w = list(wg[j].w)
        issue_w(0)
        for wi in range(nwb):
            if wi + 1 < nwb:
                issue_w(wi + 1)
            j = wi % 2
            for cj in range(NBW // 128):
                jc = wi * (NBW // 128) + cj
                for bi, (x, c, left, right) in enumerate(sblocks):
                    a = A[bi]
                    q = it % 3; q2 = it % 2; it += 1
                    for kc in range(KC):
                        k.op("pe", lambda e: e.matmul(out=pg[q][:, :c + 2], lhsT=wg[j][:, kc, cj * 128:(cj + 1) * 128], rhs=a[:, kc, 0:c + 2], start=(kc == 0), stop=(kc == KC - 1)),
                             reads=[wg[j], a], writes=[pg[q]], sig=(kc == KC - 1))
                    for kc in range(KC):
                        k.op("pe", lambda e: e.matmul(out=pa[q][:, :c], lhsT=wa[j][:, kc, cj * 128:(cj + 1) * 128], rhs=a[:, kc, 1:c + 1], start=(kc == 0), stop=(kc == KC - 1)),
                             reads=[wa[j], a], writes=[pa[q]], sig=(kc == KC - 1))
                    cc = c1[q2]
                    k.op("dve", lambda e: e.tensor_scalar(out=cc[:, :c], in0=pg[q][:, 0:c], scalar1=cw[:, 0, jc:jc + 1], scalar2=None, op0=ALU.mult), reads=[pg[q], cw], writes=[cc])
                    k.op("dve", lambda e: e.scalar_tensor_tensor(out=cc[:, :c], in0=pg[q][:, 1:c + 1], scalar=cw[:, 1, jc:jc + 1], in1=cc[:, :c], op0=ALU.mult, op1=ALU.add), reads=[pg[q], cw, cc], writes=[cc])
                    k.op("dve", lambda e: e.scalar_tensor_tensor(out=cc[:, :c], in0=pg[q][:, 2:c + 2], scalar=cw[:, 2, jc:jc + 1], in1=cc[:, :c], op0=ALU.mult, op1=ALU.add), reads=[pg[q], cw, cc], writes=[cc])
                    k.op("act", lambda e: e.activation(out=cc[:, :c], in_=cc[:, :c], func=AF.Identity, bias=cb[:, jc:jc + 1], scale=1.0), reads=[cc, cb], writes=[cc])
                    gl = y2[q2]
                    gelu_tanh(k, cc, y1[q2], gl, gl, allr, c, eng2="pool")
                    u, usem = ust.get()
                    k.op("dve", lambda e: e.tensor_tensor(out=u[:, :c], in0=pa[q][:, :c], in1=gl[:, :c], op=ALU.mult), reads=[pa[q], gl], writes=[u])
                    k.dma("act", usem, uffn[jc * 128:(jc + 1) * 128, x:x + c], u[:, :c], src=u, dst=uffn)


def copy_epilogue(k, dst, dt, tm=False, nstage=3, eng=("act", "dve"), dcol0=0, drow0=0):
    st = Stage(k, nstage, "cpe", [128, 512], dt)
    cnt = [0]

    def epi(ps, m, n, a0, b0, **kw):
        s, sem = st.get()
        e = eng[cnt[0] % len(eng)]; cnt[0] += 1
        if e == "act":
            k.op("act", lambda en: en.copy(out=s[:m, :n], in_=ps[:m, :n]), reads=[ps], writes=[s])
        else:
            k.op("dve", lambda en: en.tensor_copy(out=s[:m, :n], in_=ps[:m, :n]), reads=[ps], writes=[s])
        k.dma("act", sem, dst[drow0 + a0:drow0 + a0 + m, dcol0 + b0:dcol0 + b0 + n], s[:m, :n], src=s, dst=dst)
    return epi


def dram_copy(k, q, dst, dst_ap, src, src_ap, **kw):
    k.dma(q, k.dsem(), dst_ap, src_ap, dst=dst, src=src, **kw)


class _Stop(Exception):
    pass


def build_program(stop_after=None, debug=(), internal_inputs=False):
    nc = bass.Bass("TRN2", target_bir_lowering=False, num_devices=8)
    k = K(nc)
    try:
        _build_body(nc, k, stop_after, debug, internal_inputs)
    except _Stop:
        pass
    return nc


def _build_body(nc, k, stop_after, debug, internal_inputs):
    IN = {}

    def inp(name, shape, dt=F32):
        ext = (not internal_inputs) or (isinstance(internal_inputs, (list, tuple)) and name in internal_inputs)
        IN[name] = k.dram(name, shape, dt, kind="ExternalInput" if ext else "Internal")

    def ck(tag):
        if stop_after == tag:
            dd = k.dram("dummy_out", [128, 4], F32, kind="ExternalOutput")
            k.dma("sp", k.dsem(), dd[:, :], fl[:, :], src=fl, dst=dd)
            k.barrier()
            k.close()
            raise _Stop()
    inp("xin", [D, TOWN]); inp("condT", [128, 32, 5]); inp("onehot", [128, 5]); inp("flags", [128, 4])
    inp("modw", [2, D, 3072]); inp("modb", [1, 6144]); inp("gmix", [128, 2, 32]); inp("gffn", [128, 2, 32]); inp("gfin", [128, 32])
    inp("consts", [128, 640])
    wspec = [("w_in", 1024, WIN_N, PARITY), ("w_glu", 128, 1024, ALL8), ("w_out0", 512, D, ALL8), ("w_up0", 512, 2 * DFF, ALL8),
             ("w_dn0", DFF // 8, D, ALL8), ("w_qkv", 512, 3 * D, ALL8), ("w_out1", 512, D, ALL8), ("w_up1", 512, 2 * DFF, ALL8), ("w_dn1", DFF // 8, D, ALL8)]
    for (nm, R, N, grp) in wspec:
        inp(nm, [R, N])
    inp("gate_b", [1, 12]); inp("head_g", [1, 1536]); inp("ropeT", [TP, 256])
    inp("s5par", [128, 3, 32]); inp("s5Bre", [128, 16, 128]); inp("s5Bim", [128, 16, 128]); inp("s5Cre", [128, 16, 64]); inp("s5Cim", [128, 16, 64])
    inp("s5dsk", [128, 4]); inp("glu_b", [128, 8]); inp("conv_w", [128, 2, 3, NJC]); inp("conv_b", [128, 2, NJC])
    inp("rpbT", [32, 64, 15, 64]); inp("cmask", [64, 64])
    outT = k.dram("outT", [D, TLAT], F32, kind="ExternalOutput")
    DBG = {}
    for (nm, shape, dt) in debug:
        DBG[nm] = k.dram("dbg_" + nm, shape, dt, kind="ExternalOutput")

    C = load_consts(k, IN["consts"])
    s0 = k.dsem()
    fl = k.sb("flags", [128, 4], F32); gfin = k.sb("gfin", [128, 32], F32); glub = k.sb("glub", [128, 8], F32)
    cw = k.sb("convw", [128, 2, 3, NJC], F32); cb = k.sb("convb", [128, 2, NJC], F32)
    k.dma("sp", s0, fl[:, :], IN["flags"][:, :], dst=fl); k.dma("sp", s0, gfin[:, :], IN["gfin"][:, :], dst=gfin)
    k.dma("sp", s0, glub[:, :], IN["glu_b"][:, :], dst=glub)
    k.dma("sp", s0, cw[:].rearrange("p a b c -> p (a b c)"), IN["conv_w"][:].rearrange("p a b c -> p (a b c)"), dst=cw)
    k.dma("sp", s0, cb[:].rearrange("p a b -> p (a b)"), IN["conv_b"][:].rearrange("p a b -> p (a b)"), dst=cb)
    for t in (fl, gfin, glub, cw):
        t.w = list(cb.w)
    MV = {}
    for l in range(2):
        for si in range(2):
            for key in (0, 1, 2, 3, 4, 5, "gscm", "gscf"):
                MV[l, key, si] = k.sb("mv", [128, 32], F32)

    WT = {}
    PW = {"w_in": 512, "w_glu": 1024, "w_out0": 512, "w_up0": 512, "w_dn0": 128, "w_qkv": 512, "w_out1": 512, "w_up1": 512, "w_dn1": 128}
    semA = k.dsem()
    wsem = {nm: k.dsem() for (nm, R, N, grp) in wspec}
    st1_in = {}; st1_out = {}; full = {}
    for (nm, R, N, grp) in wspec:
        w = PW[nm]
        npc = N // w
        st1_in[nm] = [k.dram("%s_a%d" % (nm, j), [R, w], BF16) for j in range(npc)]
        with k.phase():
            cast_rows_pieces(k, IN[nm], st1_in[nm], R, N, w)
    ck('cast')
    if "wdump" in DBG:
        with k.phase():
            for i_, j_ in enumerate((0, 8, 9, 10)):
                dram_copy(k, "sp", DBG["wdump"], DBG["wdump"][i_ * D:(i_ + 1) * D, :], full["w_in"][j_], full["w_in"][j_][:, :])
    ck('weights')
    mod_phase(k, C, IN, MV)
    ck('mod')

    def done(tag):
        return stop_after == tag

    xA = k.dram("xA", [D, TOWN], F32); xB = k.dram("xB", [D, TOWN], F32)
    hT_own = k.dram("hT_own", [D, TOWN], BF16)
    NPC = TOWN // 128
    hT_own_p = [k.dram("hTo%d" % j, [D, 128], BF16) for j in range(NPC)]
    hT_pair_p = [k.dram("hTp%d" % j, [2 * D, 128], BF16) for j in range(NPC)]
    h2T = k.dram("h2T", [D, TOWN], BF16); uffn = k.dram("uffn", [DFF, TOWN], BF16)
    hb_own = k.dram("hb_own", [D, 64], BF16); hb_pair = k.dram("hb_pair", [2 * D, 64], BF16)
    SEG01 = [(0, TCX, 1), (TCX, TLAT, 0)]

    def ffn(l, xsrc, xdst, with_ctx):
        segn = SEG01 if with_ctx else [(TCX, TLAT, 0)]
        with k.phase():
            norm_mod(k, xsrc, h2T, segn, {0: MV[l, "gscf", 0], 1: MV[l, "gscf", 1]}, {0: MV[l, 3, 0], 1: MV[l, 3, 1]}, C["ones_bf"], D)
        with k.phase():
            for j, col in enumerate((0, TCX - 1, TCX, TOWN - 1)):
                if with_ctx or j >= 2:
                    dram_copy(k, "sp", hb_own, hb_own[:, j:j + 1], h2T, h2T[:, col:col + 1], allow_slow_non_contiguous=True)
        ck('ffnhalo%d' % l)
        k.allgather(k.dsem(), hb_pair, hb_own, PAIRS)
        with k.phase():
            hraw = k.sb("hraw", [128, KC, 2, 4], BF16); halo = k.sb("halo", [128, KC, 4], BF16)
            hs = k.dsem()
            for r in range(2):
                k.dma("sp", hs, hraw[:, :, r, :], hb_pair[r * D:(r + 1) * D, 0:4].rearrange("(kc p) j -> p kc j", p=128), dst=hraw, src=hb_pair)
            for (o, r, cidx, fcol) in ((0, 0, 1, 1), (1, 1, 0, 0), (2, 0, 3, 1), (3, 1, 2, 0)):
                k.op("dve", lambda e: e.tensor_scalar(out=halo[:, :, o:o + 1], in0=hraw[:, :, r, cidx:cidx + 1], scalar1=fl[:, fcol:fcol + 1], scalar2=None, op0=ALU.mult),
                     reads=[hraw, fl], writes=[halo])
            segs = []
            if with_ctx:
                segs.append((0, TCX, ("halo", 0), ("halo", 1)))
            segs.append((TCX, TLAT, ("halo", 2), ("halo", 3)))
            ffn_up_phase(k, C, h2T, halo, WT["w_up%d" % l], _Sub3(cw, l), _Sub2(cb, l), uffn, segs)
        ck('ffnup%d' % l)
        with k.phase():
            pp = PsPool(k, 4, "fdps")
            t00 = 0 if with_ctx else TCX
            nt = TOWN - t00
            blocks = mk_blocks(nt, 512, 512, bounds=(TCX - t00,))

            def load_a(dst, sem, t0, n):
                load_rows(k, "act", sem, dst, dst[:, :, :n], uffn, uffn[:, t00 + t0:t00 + t0 + n])
            epi = resid_epilogue(k, xsrc, xdst, [MV[l, 5, 0], MV[l, 5, 1]], tcol0=t00)
            gemm(k, "fm", NJC, blocks, load_a, WT["w_dn%d" % l], 0, D, 128, epi, pp, tag="fd")

    with k.phase():
        def wr_h(h, a, m, sem):
            for a2 in range(a, a + m, 128):
                pc = hT_own_p[a2 // 128]
                k.dma("act", sem, pc[:, :].rearrange("(kc p) t -> p kc t", p=128), h[:, :, a2 - a:a2 - a + 128], src=h, dst=pc)
        norm_mod(k, IN["xin"], None, SEG01, {0: MV[0, "gscm", 0], 1: MV[0, "gscm", 1]}, {0: MV[0, 0, 0], 1: MV[0, 0, 1]}, C["ones_bf"], D, writer=wr_h)
    for j in range(NPC):
        k.allgather(k.dsem(), hT_pair_p[j], hT_own_p[j], PAIRS)
    ck('norm0')
    k.lazy = set([semA] + list(wsem.values()))
    for (nm, R, N, grp) in wspec:
        w = PW[nm]
        npc = N // w
        if nm == "w_in":
            full[nm] = [k.dram("%s_f%d" % (nm, j), [4 * R, w], BF16) for j in range(npc)]
            for j in range(npc):
                k.allgather(wsem[nm], full[nm][j], st1_in[nm][j], PARITY)
        else:
            st1_out[nm] = [k.dram("%s_b%d" % (nm, j), [2 * R, w], BF16) for j in range(npc)]
            for j in range(npc):
                k.allgather(semA, st1_out[nm][j], st1_in[nm][j], PAIRS)
    for nm in st1_out:
        for t in st1_out[nm]:
            t.w = [(semA, k.cnt[semA])]
    for (nm, R, N, grp) in wspec:
        if nm == "w_in":
            continue
        w = PW[nm]
        npc = N // w
        full[nm] = [k.dram("%s_f%d" % (nm, j), [8 * R, w], BF16) for j in range(npc)]
        for j in range(npc):
            k.allgather(wsem[nm], full[nm][j], st1_out[nm][j], PARITY)
    for nm in full:
        for t in full[nm]:
            t.w = [(wsem[nm], k.cnt[wsem[nm]])]

    def wget(nm):
        w = PW[nm]

        def f(c0, n):
            tk = full[nm][c0 // w]
            off = c0 % w
            assert off + n <= w, (nm, c0, n)
            return tk, tk[:, off:off + n]
        return f
    for nm in full:
        WT[nm] = wget(nm)
    w_in_ap = WT["w_in"]
    q_tm = k.dram("q_tm", [TP, 768], F32); k_tm = k.dram("k_tm", [TP, 768], F32); v_tm = k.dram("v_tm", [TP, 1536], F32)
    o_tm = k.dram("o_tm", [TP, 1536], F32); gates = k.dram("gates", [TP, 12], F32); uT = k.dram("uT", [512, TP], F32)
    pblocks_tm = []
    pblocks_fm = []
    for r in range(2):
        for (t0, n, subs) in mk_blocks(TOWN, 1152, 128):
            pblocks_tm.append((r * TOWN + t0, n, subs))
        for (t0, n, subs) in mk_blocks(TOWN, 1152, 512):
            pblocks_fm.append((r * TOWN + t0, n, subs))

    def load_pair(dst, sem, t0, n):
        r, pos = t0 // TOWN, t0 % TOWN
        for a2 in range(pos, pos + n, 128):
            pc = hT_pair_p[a2 // 128]
            k.dma("act", sem, dst[:, :, a2 - pos:a2 - pos + 128], pc[r * D:(r + 1) * D, :].rearrange("(kc p) t -> p kc t", p=128), dst=dst, src=pc)
    with k.phase():
        pp = PsPool(k, 4, "ipps")
        st = Stage(k, 3, "ipst", [128, 512], F32)
        dests = [(0, 768, q_tm), (768, 1536, k_tm), (1536, 3072, v_tm), (3072, 4608, o_tm), (4608, 4620, gates)]

        def epi_ip(ps, m, n, t0, c0, **kw):
            s, sem = st.get()
            k.op("act", lambda e: e.copy(out=s[:m, :n], in_=ps[:m, :n]), reads=[ps], writes=[s])
            for (a, b_, dtk) in dests:
                lo, hi = max(a, c0), min(b_, c0 + n)
                if lo < hi:
                    k.dma("act", sem, dtk[t0:t0 + m, lo - a:hi - a], s[:m, lo - c0:hi - c0], src=s, dst=dtk)
        gemm(k, "tm", KC, pblocks_tm, load_pair, w_in_ap, 0, 4620, 512, epi_ip, pp, tag="ip")
    with k.phase():
        pp = PsPool(k, 4, "iups")
        gemm(k, "fm", KC, pblocks_fm, load_pair, w_in_ap, WIN_U0, 512, 512, copy_epilogue(k, uT, F32), pp, tag="iu")
    ck('inproj')
    qT = k.dram("qT", [768, TP], F32); kT = k.dram("kT", [768, TP], F32); ktm_r = k.dram("ktm_r", [TP, 768], F32)
    with k.phase():
        rope_phase(k, C, q_tm, k_tm, IN["ropeT"], qT, kT, ktm_r, 3)
    ck('rope')
    hdir = [k.dram("hdir0", [TP, 1536], F32), k.dram("hdir1", [TP, 1536], F32)]
    with k.phase():
        gb_bc = k.sb("gb_bc", [128, 12], F32)
        k.dma("sp", k.dsem(), gb_bc[:, :], IN["gate_b"][0:1, :].partition_broadcast(128), dst=gb_bc)
        fwd = [0, 17] + list(range(1, 17)) + list(range(18, 34))
        bwd = [17, 0] + list(range(33, 17, -1)) + list(range(16, 0, -1))
        mlstm_scan(k, C, qT, kT, ktm_r, v_tm, gates, gb_bc, hdir, [fwd, bwd], 3)
    ck('mlstm')
    mix_own = [k.dram("mixo%d" % j, [2048, 128], BF16) for j in range(TP // 128)]
    mix_pair = [k.dram("mixp%d" % j, [4096, 128], BF16) for j in range(TP // 128)]
    with k.phase():
        mlstm_readout(k, C, hdir, o_tm, IN["head_g"], mix_own, 3)
    ck('readout')
    with k.phase():
        lat0 = [(TCX + 512 * i, 512) for i in range(4)]
        lat1 = [(TOWN + TCX + 512 * i, 512) for i in range(4)]
        chf = [(0, TCX), (TOWN, TCX)] + lat0 + lat1
        chb = [(TOWN, TCX), (0, TCX)] + lat1[::-1] + lat0[::-1]
        s5_phase(k, C, uT, IN["s5par"], IN["s5Bre"], IN["s5Bim"], IN["s5Cre"], IN["s5Cim"], IN["s5dsk"], mix_own, 1536, TP, 32, [chf, chb])
    ck('s5')
    for j in range(TP // 128):
        k.allgather(k.dsem(), mix_pair[j], mix_own[j], PAIRS)
    mixsel = k.dram("mixsel", [4096, TOWN], BF16); ysT = k.dram("ysT", [1024, TOWN], BF16)
    with k.phase():
        blend_cols(k, mixsel, [(j * 128, mix_pair[j], mix_pair[NPC + j]) for j in range(NPC)], fl, 0, 1, 4096)
    ck('blend')
    own_blocks = mk_blocks(TOWN, 1088, 512, bounds=(TCX,))
    with k.phase():
        pp = PsPool(k, 4, "glps")
        st = Stage(k, 3, "glst", [128, 512], BF16)
        sg = [k.sb("glsg%d" % i, [128, 512], F32) for i in range(2)]
        cnt = [0]

        def load_g(dst, sem, t0, n):
            for r in range(2):
                k.dma("act", sem, dst[:, 4 * r:4 * r + 4, :n], mixsel[r * 2048 + 1536:r * 2048 + 2048, t0:t0 + n].rearrange("(kc p) t -> p kc t", p=128), dst=dst, src=mixsel)

        def epi_glu(ps, m, n, f0, t0, at=None, ts=None):
            fc = f0 // 128
            g_ = sg[cnt[0] % 2]; cnt[0] += 1
            k.op("act", lambda e: e.activation(out=g_[:m, :n], in_=ps[:m, :n], func=AF.Sigmoid, bias=glub[:m, fc:fc + 1], scale=1.0), reads=[ps, glub], writes=[g_])
            s, sem = st.get()
            k.op("dve", lambda e: e.tensor_tensor(out=s[:m, :n], in0=g_[:m, :n], in1=at[:m, fc, ts:ts + n], op=ALU.mult), reads=[g_, at], writes=[s])
            k.dma("act", sem, ysT[f0:f0 + m, t0:t0 + n], s[:m, :n], src=s, dst=ysT)
        gemm(k, "fm", 8, own_blocks, load_g, WT["w_glu"], 0, 1024, 512, epi_glu, pp, tag="gl")
    with k.phase():
        pp = PsPool(k, 4, "o0ps")

        def load_mix(dst, sem, t0, n):
            for r in range(2):
                k.dma("act", sem, dst[:, 12 * r:12 * r + 12, :n], mixsel[r * 2048:r * 2048 + 1536, t0:t0 + n].rearrange("(kc p) t -> p kc t", p=128), dst=dst, src=mixsel)
            k.dma("act", sem, dst[:, 24:32, :n], ysT[:, t0:t0 + n].rearrange("(kc p) t -> p kc t", p=128), dst=dst, src=ysT)
        gemm(k, "fm", KC, own_blocks, load_mix, WT["w_out0"], 0, D, 512, resid_epilogue(k, IN["xin"], xA, [MV[0, 2, 0], MV[0, 2, 1]]), pp, tag="o0")
    ck('outproj0')
    if DBG:
        with k.phase():
            mvd = DBG["mv"]
            for i_, key in enumerate(((0, "gscm", 0), (0, 0, 0), (0, 2, 0), (0, 2, 1), (0, 1, 0), (0, "gscm", 1), (1, 2, 0), (0, 5, 0))):
                k.dma("sp", k.dsem(), mvd[:, i_ * 32:(i_ + 1) * 32], MV[key][:, :], src=MV[key], dst=mvd)
            dram_copy(k, "sp", DBG["hT1"], DBG["hT1"][:, :], hT_own_p[1], hT_own_p[1][:, :])
            dram_copy(k, "sp", DBG["qtm"], DBG["qtm"][:, :], q_tm, q_tm[0:512, :])
            dram_copy(k, "sp", DBG["ktm"], DBG["ktm"][:, :], k_tm, k_tm[0:512, :])
            dram_copy(k, "sp", DBG["vtm"], DBG["vtm"][:, :], v_tm, v_tm[0:512, :])
            dram_copy(k, "sp", DBG["gates"], DBG["gates"][:, :], gates, gates[0:512, :])
            dram_copy(k, "sp", DBG["uTd"], DBG["uTd"][:, :], uT, uT[:, 0:512])
            dram_copy(k, "sp", DBG["qTd"], DBG["qTd"][:, :], qT, qT[:, 0:512])
            dram_copy(k, "sp", DBG["hd0"], DBG["hd0"][:, :], hdir[0], hdir[0][0:512, :])
            dram_copy(k, "sp", DBG["hd1"], DBG["hd1"][:, :], hdir[1], hdir[1][0:512, :])
            dram_copy(k, "sp", DBG["mixsel"], DBG["mixsel"][:, :], mixsel, mixsel[:, 0:256])
            dram_copy(k, "sp", DBG["ysT"], DBG["ysT"][:, :], ysT, ysT[:, 0:256])
            dram_copy(k, "sp", DBG["xA0"], DBG["xA0"][:, :], xA, xA[:, :])
    ffn(0, xA, xB, True)
    ck('ffn0')
    if DBG:
        with k.phase():
            dram_copy(k, "sp", DBG["xB0"], DBG["xB0"][:, :], xB, xB[:, :])

    h1T_own = k.dram("h1T_own", [D, TOWN], BF16)
    h1b_own = [k.dram("h1bo%d" % j, [D, 128], BF16) for j in range(5)]
    h1b_pair = [k.dram("h1bp%d" % j, [2 * D, 128], BF16) for j in range(5)]
    hk = k.dram("hk", [D, 2816], BF16)
    with k.phase():
        norm_mod(k, xB, h1T_own, SEG01, {0: MV[1, "gscm", 0], 1: MV[1, "gscm", 1]}, {0: MV[1, 0, 0], 1: MV[1, 0, 1]}, C["ones_bf"], D)
    with k.phase():
        for j, c0 in enumerate((TCX, TCX + 128, TOWN - 256, TOWN - 128, 0)):
            dram_copy(k, "sp", h1b_own[j], h1b_own[j][:, :], h1T_own, h1T_own[:, c0:c0 + 128])
        dram_copy(k, "act", hk, hk[:, 256:2304], h1T_own, h1T_own[:, TCX:TOWN])
    for j in range(5):
        k.allgather(k.dsem(), h1b_pair[j], h1b_own[j], PAIRS)
    with k.phase():
        scale_cols(k, hk, 0, _RowWin(h1b_pair[2], 0, D), 0, fl, 1, D, 128)
        scale_cols(k, hk, 128, _RowWin(h1b_pair[3], 0, D), 0, fl, 1, D, 128)
        scale_cols(k, hk, 2304, _RowWin(h1b_pair[0], D, 2 * D), 0, fl, 0, D, 128)
        scale_cols(k, hk, 2432, _RowWin(h1b_pair[1], D, 2 * D), 0, fl, 0, D, 128)
        dram_copy(k, "pool", hk, hk[:, 2560:2688], h1b_pair[4], h1b_pair[4][0:D, :])
        dram_copy(k, "pool", hk, hk[:, 2688:2816], h1b_pair[4], h1b_pair[4][D:2 * D, :])
    ck('hk')
    QT = k.dram("QT", [D, TLAT], BF16); KT = k.dram("KT", [D, 2816], BF16); Vn = k.dram("Vn", [2816, D], BF16); attnT = k.dram("attnT", [D, TLAT], BF16)

    def load_hk(off):
        def f(dst, sem, t0, n):
            load_rows(k, "act", sem, dst, dst[:, :, :n], hk, hk[:, off + t0:off + t0 + n])
        return f
    with k.phase():
        gemm(k, "fm", KC, mk_blocks(TLAT, 1024, 512), load_hk(256), WT["w_qkv"], 0, D, 512, copy_epilogue(k, QT, BF16), PsPool(k, 4, "qps"), tag="q")
    with k.phase():
        gemm(k, "fm", KC, mk_blocks(2816, 1024, 512), load_hk(0), WT["w_qkv"], D, D, 512, copy_epilogue(k, KT, BF16), PsPool(k, 4, "kps"), tag="kk")
    with k.phase():
        gemm(k, "tm", KC, mk_blocks(2816, 1024, 128), load_hk(0), WT["w_qkv"], 2 * D, D, 512, copy_epilogue(k, Vn, BF16), PsPool(k, 4, "vps"), tag="vv")
    ck('qkv')
    with k.phase():
        na_phase(k, C, QT, KT, Vn, IN["rpbT"], IN["cmask"], IN["flags"], attnT, 32, 128 ** -0.5)
    with k.phase():
        def load_at(dst, sem, t0, n):
            load_rows(k, "act", sem, dst, dst[:, :, :n], attnT, attnT[:, t0:t0 + n])
        gemm(k, "fm", KC, mk_blocks(TLAT, 1024, 512), load_at, WT["w_out1"], 0, D, 512,
             resid_epilogue(k, xB, xA, [MV[1, 2, 0], MV[1, 2, 1]], tcol0=TCX), PsPool(k, 4, "o1ps"), tag="o1")
    ck('outproj1')
    if DBG:
        with k.phase():
            dram_copy(k, "sp", DBG["xA1"], DBG["xA1"][:, :], xA, xA[:, :])
    ffn(1, xA, xB, False)
    ck('ffn1')
    with k.phase():
        norm_mod(k, xB, None, [(TCX, TLAT, 0)], {0: gfin}, None, C["ones_bf"], D, hT_col0=-TCX, out_f32=outT)
    k.close()


class _RowWin:
    def __init__(self, tk, r0, r1):
        self.tk = tk; self.r0 = r0; self.r1 = r1

    @property
    def w(self):
        return self.tk.w

    @w.setter
    def w(self, v):
        self.tk.w = v

    @property
    def r(self):
        return self.tk.r

    @r.setter
    def r(self, v):
        self.tk.r = v

    def __getitem__(self, idx):
        return self.tk.t[self.r0:self.r1, :][idx]


class _Sub3:
    def __init__(self, tk, l):
        self.tk = tk; self.l = l; self.w = tk.w; self.r = tk.r

    def __getitem__(self, idx):
        return self.tk.t[:, self.l, :, :][idx]


class _Sub2:
    def __init__(self, tk, l):
        self.tk = tk; self.l = l; self.w = tk.w; self.r = tk.r

    def __getitem__(self, idx):
        return self.tk.t[:, self.l, :][idx]


_NC_CACHE = {}


def _rope_table():
    inv = (10000.0 ** (-np.arange(0, 128, 2, dtype=np.float32) / np.float32(128))).astype(np.float32)
    tab = np.zeros((TP, 2, 2, 64), np.float32)
    tab[:, 0] = 1.0
    for r in range(2):
        t = r * TLAT + np.arange(TLAT)
        row = (t // 64).astype(np.float32)[:, None] * inv[None, :]
        col = (t % 64).astype(np.float32)[:, None] * inv[None, :]
        base = r * TOWN + TCX
        tab[base:base + TLAT, 0, 0] = np.cos(row); tab[base:base + TLAT, 0, 1] = np.cos(col)
        tab[base:base + TLAT, 1, 0] = np.sin(row); tab[base:base + TLAT, 1, 1] = np.sin(col)
    return tab.reshape(TP, 256)


def make_in_maps(x, c, ctx, c_ctx, mod_w, mod_b, norm_mix_g, norm_ffn_g, ab_w_in, mlstm_gate_b,
                 mlstm_head_g, s5_a_re, s5_a_im, s5_log_dt, s5_b_re, s5_b_im, s5_c_re, s5_c_im,
                 s5_d, s5_glu_w, s5_glu_b, ab_w_out, na_w_qkv, na_rpb, na_w_out,
                 ffn_w_up, ffn_conv_w, ffn_conv_b, ffn_w_down, final_norm_g):
    f = lambda a: np.asarray(a, dtype=np.float32)
    x, c, ctx, c_ctx = f(x), f(c), f(ctx), f(c_ctx)
    cond = np.concatenate([c, c_ctx[None, :]], 0)
    condT = np.ascontiguousarray(cond.reshape(5, 32, 128).transpose(2, 1, 0))
    mod_w, mod_b = f(mod_w), f(mod_b)
    pl = lambda g: np.ascontiguousarray(f(g).reshape(2, 32, 128).transpose(2, 0, 1))
    gmix, gffn = pl(norm_mix_g), pl(norm_ffn_g)
    gfin = np.ascontiguousarray(f(final_norm_g).reshape(32, 128).T)
    consts = host_consts()
    ropeT = _rope_table()
    Tg, cmask = na_host_tables(f(na_rpb)[0])
    conv_w = np.ascontiguousarray(f(ffn_conv_w).reshape(2, 3, NJC, 128).transpose(3, 0, 1, 2))
    conv_b = np.ascontiguousarray(f(ffn_conv_b).reshape(2, NJC, 128).transpose(2, 0, 1))
    glu_b = np.ascontiguousarray(f(s5_glu_b)[0].reshape(8, 128).T)
    w_in = f(ab_w_in)[0]; gate_b = f(mlstm_gate_b)[0]; head_g = f(mlstm_head_g)[0]
    in_maps = []
    for core in range(8):
        b, g = core // 2, core % 2
        m = {}
        m["xin"] = np.ascontiguousarray(np.concatenate([ctx[b, g * TCX:(g + 1) * TCX], x[b, g * TLAT:(g + 1) * TLAT]], 0).T)
        m["condT"] = condT
        oh = np.zeros((128, 5), np.float32); oh[:, b] = 1.0
        m["onehot"] = oh
        fl = np.zeros((128, 4), np.float32); fl[:, 0] = 1 - g; fl[:, 1] = g; fl[:, 2] = g; fl[:, 3] = 1 - g
        m["flags"] = fl
        m["modw"] = np.stack([np.concatenate([mod_w[l][:, w * D + core * 512:w * D + (core + 1) * 512] for w in range(6)], 1) for l in range(2)], 0)
        m["modb"] = np.concatenate([np.concatenate([mod_b[l][w * D + core * 512:w * D + (core + 1) * 512] for w in range(6)]) for l in range(2)])[None, :]
        m["gmix"] = gmix; m["gffn"] = gffn; m["gfin"] = gfin; m["consts"] = consts
        gcols = [9216 + d_ * 12 + i_ * 6 + 3 * g + h_ for d_ in range(2) for i_ in range(2) for h_ in range(3)]
        cols = np.concatenate([np.arange(g * 768, (g + 1) * 768), 1536 + np.arange(g * 768, (g + 1) * 768),
                               3072 + np.arange(g * 1536, (g + 1) * 1536), 6144 + np.arange(g * 1536, (g + 1) * 1536),
                               np.array(gcols), 9240 + np.arange(g * 512, (g + 1) * 512)])
        wi = np.zeros((1024, WIN_N), np.float32)
        wsel = w_in[b * 1024:(b + 1) * 1024][:, cols]
        wi[:, :4620] = wsel[:, :4620]
        wi[:, WIN_U0:WIN_U0 + 512] = wsel[:, 4620:5132]
        m["w_in"] = wi
        m["w_glu"] = f(s5_glu_w)[0][core * 128:(core + 1) * 128]
        m["w_out0"] = f(ab_w_out)[0][core * 512:(core + 1) * 512]
        m["w_up0"] = f(ffn_w_up)[0][core * 512:(core + 1) * 512]; m["w_up1"] = f(ffn_w_up)[1][core * 512:(core + 1) * 512]
        r8 = DFF // 8
        m["w_dn0"] = f(ffn_w_down)[0][core * r8:(core + 1) * r8]; m["w_dn1"] = f(ffn_w_down)[1][core * r8:(core + 1) * r8]
        m["w_qkv"] = f(na_w_qkv)[0][core * 512:(core + 1) * 512]
        m["w_out1"] = f(na_w_out)[0][core * 512:(core + 1) * 512]
        m["gate_b"] = np.ascontiguousarray(gate_b[:, :, 3 * g:3 * g + 3]).reshape(1, 12)
        m["head_g"] = head_g[g * 1536:(g + 1) * 1536][None, :]
        m["ropeT"] = ropeT
        gs = slice(32 * g, 32 * g + 32)
        par, Bre, Bim, Cre, Cim, dsk = s5_host_layout(f(s5_a_re)[0][:, gs], f(s5_a_im)[0][:, gs], f(s5_log_dt)[0][:, gs], f(s5_b_re)[0][gs], f(s5_b_im)[0][gs],
                                                      f(s5_c_re)[0][gs], f(s5_c_im)[0][gs], f(s5_d)[0][g * 512:(g + 1) * 512])
        m["s5par"] = par; m["s5Bre"] = Bre; m["s5Bim"] = Bim; m["s5Cre"] = Cre; m["s5Cim"] = Cim; m["s5dsk"] = dsk
        m["glu_b"] = glu_b; m["conv_w"] = conv_w; m["conv_b"] = conv_b; m["rpbT"] = Tg; m["cmask"] = cmask
        in_maps.append(m)
    return in_maps


def kernel(**inputs):
    if "nc" not in _NC_CACHE:
        _NC_CACHE["nc"] = build_program()
    nc = _NC_CACHE["nc"]
    in_maps = make_in_maps(**inputs)
    res = run_bass_kernel_spmd(nc, in_maps, core_ids=list(range(8)))
    out = np.empty((4, 4096, 4096), np.float32)
    for core in range(8):
        b, g = core // 2, core % 2
        out[b, g * TLAT:(g + 1) * TLAT, :] = res.results[core]["outT"].T
    return out
```
